# Optimizing a Trainium2 kernel written in Bass

```python
import math
import jax, jax.numpy as jnp
from jax import lax
import numpy as np

D_MODEL = 1024
BATCH = 2
SEQ = 8192
DEPTH = 2

CHUNK = 64
Q_BLOCK = 128
N_MIXERS = 2
N_A = (DEPTH + 1) // 2
N_B = DEPTH // 2

MLA_HEADS = 8
MLA_NOPE = 128
MLA_ROPE = 64
MLA_V = 128
MLA_Q_RANK = 384
MLA_KV_RANK = 256
MLA_IN = MLA_Q_RANK + MLA_KV_RANK + MLA_ROPE
ROPE_THETA = 10000.0

DIFF_HEADS = 8
DIFF_HEAD_DIM = 64
DIFF_V = 2 * DIFF_HEAD_DIM
DIFF_IN = 3 * DIFF_HEADS * 2 * DIFF_HEAD_DIM

D_FF = -(-8 * D_MODEL // (3 * 256)) * 256

ALPHA = (2 * DEPTH) ** 0.25
BETA = (8 * DEPTH) ** -0.25
LN_EPS = 1e-5
RMS_EPS = 1e-6
NEG_BIG = -1e30

kernel_name = "hybrid_mla_diffattn_deepnorm_encoder"


def _layernorm(x, g, b):
    x32 = x.astype(jnp.float32)
    mu = jnp.mean(x32, axis=-1, keepdims=True)
    var = jnp.mean(jnp.square(x32 - mu), axis=-1, keepdims=True)
    y = (x32 - mu) * lax.rsqrt(var + LN_EPS) * g.astype(jnp.float32) + b.astype(jnp.float32)
    return y.astype(x.dtype)


def _rmsnorm(x, g):
    x32 = x.astype(jnp.float32)
    y = x32 * lax.rsqrt(jnp.mean(jnp.square(x32), axis=-1, keepdims=True) + RMS_EPS)
    return (y * g.astype(jnp.float32)).astype(x.dtype)


def _chunk_mask(q_start, seq):
    q_idx = q_start + jnp.arange(Q_BLOCK)
    k_idx = jnp.arange(seq)
    return (k_idx[None, :] // CHUNK) <= (q_idx[:, None] // CHUNK)


def _rope_tables(seq, dtype):
    inv_freq = ROPE_THETA ** (-jnp.arange(0, MLA_ROPE, 2, dtype=jnp.float32) / MLA_ROPE)
    ang = jnp.arange(seq, dtype=jnp.float32)[:, None] * inv_freq[None, :]
    return jnp.cos(ang).astype(dtype), jnp.sin(ang).astype(dtype)


def _apply_rope(x, cos, sin):
    x1, x2 = jnp.split(x, 2, axis=-1)
    return jnp.concatenate([x1 * cos - x2 * sin, x1 * sin + x2 * cos], axis=-1)


def _to_blocks(t, n_blocks):
    b = t.shape[0]
    t = t.reshape((b, n_blocks, Q_BLOCK) + t.shape[2:])
    return jnp.moveaxis(t, 1, 0)


def _from_blocks(t):
    t = jnp.moveaxis(t, 0, 1)
    return t.reshape((t.shape[0], t.shape[1] * t.shape[2]) + t.shape[3:])


def _mla_mixer(x, w_in, g_q, g_kv, w_uq, w_ukv, w_o):
    b, s, _ = x.shape
    h = x @ w_in
    c_q, c_kv, k_rope = jnp.split(h, [MLA_Q_RANK, MLA_Q_RANK + MLA_KV_RANK], axis=-1)
    c_q = _rmsnorm(c_q, g_q)
    c_kv = _rmsnorm(c_kv, g_kv)
    q = (c_q @ w_uq).reshape(b, s, MLA_HEADS, MLA_NOPE + MLA_ROPE)
    q_nope, q_rope = q[..., :MLA_NOPE], q[..., MLA_NOPE:]
    kv = (c_kv @ w_ukv).reshape(b, s, MLA_HEADS, MLA_NOPE + MLA_V)
    k_nope, v = kv[..., :MLA_NOPE], kv[..., MLA_NOPE:]
    cos, sin = _rope_tables(s, x.dtype)
    q_rope = _apply_rope(q_rope, cos[:, None, :], sin[:, None, :])
    k_rope = _apply_rope(k_rope, cos, sin)
    scale = (MLA_NOPE + MLA_ROPE) ** -0.5
    n_blocks = s // Q_BLOCK

    def block(args):
        i, qn_b, qr_b = args
        logits = (jnp.einsum('bqhd,bkhd->bhqk', qn_b, k_nope, preferred_element_type=jnp.float32)
                  + jnp.einsum('bqhd,bkd->bhqk', qr_b, k_rope, preferred_element_type=jnp.float32))
        logits = jnp.where(_chunk_mask(i * Q_BLOCK, s), logits * scale, NEG_BIG)
        p = jax.nn.softmax(logits, axis=-1)
        o = jnp.einsum('bhqk,bkhd->bqhd', p.astype(v.dtype), v, preferred_element_type=jnp.float32)
        return o.astype(x.dtype)

    o = lax.map(block, (jnp.arange(n_blocks), _to_blocks(q_nope, n_blocks), _to_blocks(q_rope, n_blocks)))
    o = _from_blocks(o).reshape(b, s, MLA_HEADS * MLA_V)
    return o @ w_o


def _diff_mixer(x, w_in, lam_q1, lam_k1, lam_q2, lam_k2, g_sub, w_o, lambda_init):
    b, s, _ = x.shape
    qk_w = DIFF_HEADS * 2 * DIFF_HEAD_DIM
    h = x @ w_in
    q, k, v = jnp.split(h, [qk_w, 2 * qk_w], axis=-1)
    q = q.reshape(b, s, DIFF_HEADS, 2, DIFF_HEAD_DIM)
    k = k.reshape(b, s, DIFF_HEADS, 2, DIFF_HEAD_DIM)
    v = v.reshape(b, s, DIFF_HEADS, DIFF_V)
    f32 = jnp.float32
    lam = (jnp.exp(jnp.sum(lam_q1.astype(f32) * lam_k1.astype(f32)))
           - jnp.exp(jnp.sum(lam_q2.astype(f32) * lam_k2.astype(f32))) + lambda_init)
    slopes = jnp.exp2(-8.0 * jnp.arange(1, DIFF_HEADS + 1, dtype=f32) / DIFF_HEADS)
    scale = DIFF_HEAD_DIM ** -0.5
    n_blocks = s // Q_BLOCK
    k_pos = jnp.arange(s)

    def block(args):
        i, q_b = args
        q_pos = i * Q_BLOCK + jnp.arange(Q_BLOCK)
        dist = jnp.abs(q_pos[:, None] - k_pos[None, :]).astype(f32)
        logits = jnp.einsum('bqhmd,bkhmd->bmhqk', q_b, k, preferred_element_type=f32) * scale
        logits = logits - slopes[:, None, None] * dist
        logits = jnp.where(_chunk_mask(i * Q_BLOCK, s), logits, NEG_BIG)
        p = jax.nn.softmax(logits, axis=-1)
        a = p[:, 0] - lam * p[:, 1]
        o = jnp.einsum('bhqk,bkhe->bqhe', a.astype(v.dtype), v, preferred_element_type=f32)
        return o.astype(x.dtype)

    o = lax.map(block, (jnp.arange(n_blocks), _to_blocks(q, n_blocks)))
    o = _from_blocks(o)
    o = _rmsnorm(o, g_sub) * (1.0 - lambda_init)
    return o.reshape(b, s, DIFF_HEADS * DIFF_V) @ w_o


def _swiglu(x, w_gu, w_down):
    gate, up = jnp.split(x @ w_gu, 2, axis=-1)
    return (jax.nn.silu(gate) * up) @ w_down


def setup_inputs(seed: int = 0) -> dict:
    key = jax.random.key(seed)
    ks = jax.random.split(key, 24)
    n = jax.random.normal
    f = jnp.float32
    d = D_MODEL
    return {
        "x": n(ks[0], (BATCH, SEQ, d), f),
        "mla_w_in": n(ks[1], (N_A, d, MLA_IN), f) * d ** -0.5,
        "mla_g_q": 1.0 + 0.02 * n(ks[2], (N_A, MLA_Q_RANK), f),
        "mla_g_kv": 1.0 + 0.02 * n(ks[3], (N_A, MLA_KV_RANK), f),
        "mla_w_uq": n(ks[4], (N_A, MLA_Q_RANK, MLA_HEADS * (MLA_NOPE + MLA_ROPE)), f) * MLA_Q_RANK ** -0.5,
        "mla_w_ukv": n(ks[5], (N_A, MLA_KV_RANK, MLA_HEADS * (MLA_NOPE + MLA_V)), f) * MLA_KV_RANK ** -0.5,
        "mla_w_o": n(ks[6], (N_A, MLA_HEADS * MLA_V, d), f) * (MLA_HEADS * MLA_V) ** -0.5 * BETA,
        "diff_w_in": n(ks[7], (N_B, d, DIFF_IN), f) * d ** -0.5,
        "diff_lam_q1": 0.1 * n(ks[8], (N_B, DIFF_HEAD_DIM), f),
        "diff_lam_k1": 0.1 * n(ks[9], (N_B, DIFF_HEAD_DIM), f),
        "diff_lam_q2": 0.1 * n(ks[10], (N_B, DIFF_HEAD_DIM), f),
        "diff_lam_k2": 0.1 * n(ks[11], (N_B, DIFF_HEAD_DIM), f),
        "diff_g_sub": 1.0 + 0.02 * n(ks[12], (N_B, DIFF_V), f),
        "diff_w_o": n(ks[13], (N_B, DIFF_HEADS * DIFF_V, d), f) * (DIFF_HEADS * DIFF_V) ** -0.5 * BETA,
        "ln_mix_g": 1.0 + 0.02 * n(ks[14], (DEPTH, d), f),
        "ln_mix_b": 0.02 * n(ks[15], (DEPTH, d), f),
        "ffn_w_gu": n(ks[16], (DEPTH, d, 2 * D_FF), f) * d ** -0.5,
        "ffn_w_down": n(ks[17], (DEPTH, D_FF, d), f) * D_FF ** -0.5 * BETA,
        "ln_ffn_g": 1.0 + 0.02 * n(ks[18], (DEPTH, d), f),
        "ln_ffn_b": 0.02 * n(ks[19], (DEPTH, d), f),
    }


def reference(x, mla_w_in, mla_g_q, mla_g_kv, mla_w_uq, mla_w_ukv, mla_w_o,
              diff_w_in, diff_lam_q1, diff_lam_k1, diff_lam_q2, diff_lam_k2, diff_g_sub, diff_w_o,
              ln_mix_g, ln_mix_b, ffn_w_gu, ffn_w_down, ln_ffn_g, ln_ffn_b):
    h = x
    for i in range(DEPTH):
        j = i // N_MIXERS
        if i % N_MIXERS == 0:
            mix = _mla_mixer(h, mla_w_in[j], mla_g_q[j], mla_g_kv[j], mla_w_uq[j], mla_w_ukv[j], mla_w_o[j])
        else:
            lambda_init = 0.8 - 0.6 * math.exp(-0.3 * i)
            mix = _diff_mixer(h, diff_w_in[j], diff_lam_q1[j], diff_lam_k1[j], diff_lam_q2[j], diff_lam_k2[j],
                              diff_g_sub[j], diff_w_o[j], lambda_init)
        h = _layernorm(ALPHA * h + mix, ln_mix_g[i], ln_mix_b[i])
        h = _layernorm(ALPHA * h + _swiglu(h, ffn_w_gu[i], ffn_w_down[i]), ln_ffn_g[i], ln_ffn_b[i])
    return h
```

```python
import numpy as np
import ml_dtypes
import concourse.bass as bass
import concourse.mybir as mybir
from concourse.bass_utils import run_bass_kernel_spmd

F32 = mybir.dt.float32
BF16 = mybir.dt.bfloat16
ALU = mybir.AluOpType
AF = mybir.ActivationFunctionType

NCORES = 8


class Prog:
    ENGS = ("pe", "act", "dve", "pool", "sp")

    def __init__(self, nc):
        self.nc = nc
        self.ops = []
        self.last_write = {}
        self.readers = {}
        self._n = 0

    def add(self, eng, fn, reads=(), writes=(), kind="c", key=None):
        idx = len(self.ops)
        raw, war = set(), set()
        for t in reads:
            w = self.last_write.get(t)
            if w is not None:
                raw.add(w)
        for t in writes:
            w = self.last_write.get(t)
            if w is not None:
                war.add(w)
            r = self.readers.get(t)
            if r:
                war.update(r[0].values())
                war.update(r[1])
        for t in writes:
            self.last_write[t] = idx
            self.readers[t] = ({}, [])
        for t in reads:
            r = self.readers.setdefault(t, ({}, []))
            if kind == "c":
                r[0][eng] = idx
            else:
                r[1].append(idx)
        raw.discard(idx)
        war.discard(idx)
        if kind != "c" and key is None:
            key = "dma_%s" % eng
        self.ops.append(dict(eng=eng, fn=fn, raw=raw, war=war - raw, kind=kind, key=key, sig=False))
        return idx

    def pe(self, fn, r=(), w=()):
        return self.add("pe", fn, r, w)

    def act(self, fn, r=(), w=()):
        return self.add("act", fn, r, w)

    def dve(self, fn, r=(), w=()):
        return self.add("dve", fn, r, w)

    def pool(self, fn, r=(), w=()):
        return self.add("pool", fn, r, w)

    def dma(self, eng, fn, r=(), w=(), key=None):
        return self.add(eng, fn, r, w, kind="d", key=key)

    def _needed(self, o, d, is_raw):
        if d["kind"] != "c":
            return True
        if o["kind"] != "c":
            return True
        if d["eng"] != o["eng"]:
            return True
        if o["eng"] == "pe":
            return False
        return is_raw

    def emit(self):
        nc = self.nc
        ops = self.ops
        for o in ops:
            dl = []
            for di in o["raw"]:
                if self._needed(o, ops[di], True):
                    dl.append(di)
            for di in o["war"]:
                if self._needed(o, ops[di], False):
                    dl.append(di)
            o["dl"] = dl
            for di in dl:
                ops[di]["sig"] = True
        cnt = {}
        for o in ops:
            if o["kind"] == "bar":
                continue
            if o["kind"] == "c":
                if o["sig"]:
                    k = "eng_" + o["eng"]
                    cnt[k] = cnt.get(k, 0) + 1
                    o["sv"] = (k, cnt[k])
            else:
                k = o["key"]
                inc = 16 if o["kind"] == "d" else 1
                cnt[k] = cnt.get(k, 0) + inc
                o["sv"] = (k, cnt[k])
                o["inc"] = inc
        sems = {k: nc.alloc_semaphore("s_" + k) for k in cnt}
        self.sem_final = cnt
        per_eng = {e: [] for e in self.ENGS}
        for o in ops:
            per_eng[o["eng"]].append(o)
        handles = dict(pe="tensor", act="scalar", dve="vector", pool="gpsimd", sp="sync")

        def run_engine(ename):
            def body(eng):
                known = {}
                if ename == "sp":
                    self.pid = eng.partition_id()
                for o in per_eng[ename]:
                    need = {}
                    for di in o["dl"]:
                        k, v = ops[di]["sv"]
                        if need.get(k, 0) < v:
                            need[k] = v
                    for k, v in need.items():
                        if known.get(k, 0) < v:
                            eng.wait_ge(sems[k], v)
                            known[k] = v
                    if o["kind"] == "bar":
                        continue
                    ins = o["fn"](eng)
                    if o["kind"] == "c":
                        if o["sig"]:
                            ins.then_inc(sems[o["sv"][0]], 1)
                    else:
                        ins.then_inc(sems[o["sv"][0]], o["inc"])
                if ename in self.final_wait_engs:
                    for k, v in cnt.items():
                        if known.get(k, 0) < v:
                            eng.wait_ge(sems[k], v)
            return body

        self.final_wait_engs = ("sp",)
        with nc.Block() as block:
            for ename in self.ENGS:
                if per_eng[ename] or ename in self.final_wait_engs:
                    getattr(block, handles[ename])(run_engine(ename))
        return len(ops)

    def barrier(self):
        last = {}
        dmas = []
        for i, o in enumerate(self.ops):
            if o["kind"] == "c":
                last[o["eng"]] = i
            elif o["kind"] in ("d", "cc") and i >= getattr(self, "_bar_from", 0):
                dmas.append(i)
        deps = set(last.values()) | set(dmas)
        self._bar_from = len(self.ops)
        for e in self.ENGS:
            self.ops.append(dict(eng=e, fn=None, raw=set(deps), war=set(), kind="bar", key=None, sig=False))
        self.last_write = {}
        self.readers = {}


SEQ, BATCH, DM = 8192, 2, 1024
NT = 2048
DFF = 2816
ALPHA = 4.0 ** 0.25
LN_EPS = 1e-5
RMS_EPS = 1e-6
SC0 = 192.0 ** -0.5
SC1 = 0.125
LAMBDA_INIT = 0.8 - 0.6 * float(np.exp(-0.3))
SB_BASE, SB_TOP = 16512, 229344 - 1024


class SBA:
    def __init__(self, nc):
        self.nc, self.off, self.n = nc, SB_BASE, 0

    def alloc(self, shape, dtype, name="t"):
        sz = int(np.prod(shape[1:])) * (2 if dtype == BF16 else 4)
        sz = (sz + 63) // 64 * 64
        assert self.off + sz <= SB_TOP, ("SBUF overflow", name, self.off, sz)
        t = self.nc.alloc_sbuf_tensor_at("%s_%d" % (name, self.n), list(shape), dtype, offset=self.off)
        self.off += sz
        self.n += 1
        return t


class Ctx:
    pass


def _transposes(P, ps_bf, src, n, ident, rtok, wtok, width=128):
    for j in range(n):
        P.pe(lambda e, j=j: e.transpose(ps_bf[0:width, j * 128:(j + 1) * 128], src[:, j * width:(j + 1) * width] if width == 128 else src, ident[:, :]),
             r=[rtok, "ident"], w=[wtok])


def load_weight(P, C, dst, src, K, F, wtok, srctok, cast_eng="pool", q="sp"):
    kmax = max(1, C.stg_elems // F)
    k0 = 0
    while k0 < K:
        ks = min(kmax, K - k0)
        s = C.stg_i % len(C.stg)
        C.stg_i += 1
        st = C.stg[s]
        stv = st[:, 0:ks * F].rearrange("p (k f) -> p k f", f=F)
        P.dma(q, lambda e, stv=stv, k0=k0, ks=ks: e.dma_start(out=stv, in_=src[:, k0:k0 + ks, :]),
              r=[srctok], w=[("stg", s)], key="k_stg%d" % s)
        P.add(cast_eng, lambda e, stv=stv, k0=k0, ks=ks: e.tensor_copy(out=dst[:, k0:k0 + ks, :], in_=stv),
              reads=[("stg", s)], writes=[wtok])
        k0 += ks


def phase_a0(nc, P, C, T):
    _ = (T.x, T.g_q, T.g_kv, T.cs_tok, T.w_in0)
    sb = SBA(nc)
    ps = C.ps
    ident = C.ident
    w_in_b = sb.alloc([128, 8, 704], BF16, "w_in_b")
    gq = sb.alloc([128, 384], F32, "gq")
    gkv = sb.alloc([128, 256], F32, "gkv")
    cs = sb.alloc([128, 16, 64], F32, "cs")
    latT = sb.alloc([128, 6, NT], BF16, "latT")
    C.stg = [sb.alloc([128, 4096], F32, "stg") for _ in range(2)]
    C.stg_elems, C.stg_i = 4096, 0
    xf = [sb.alloc([128, 1024], F32, "xf") for _ in range(2)]
    xb = [sb.alloc([128, 1024], BF16, "xb") for _ in range(2)]
    xT = [sb.alloc([128, 8, 128], BF16, "xT") for _ in range(2)]
    lat = [sb.alloc([128, 704], BF16, "lat") for _ in range(2)]
    kr = [sb.alloc([128, 64], F32, "kr") for _ in range(2)]
    tmp = [sb.alloc([128, 4, 32], F32, "tmp") for _ in range(2)]
    junk = sb.alloc([128, 384], F32, "junk")
    st = [sb.alloc([128, 8], F32, "st") for _ in range(2)]

    P.dma("sp", lambda e: e.dma_start(out=gq[:, :], in_=T.g_q[:, :]), r=[], w=["constA"], key="k_c0")
    P.dma("sp", lambda e: e.dma_start(out=gkv[:, :], in_=T.g_kv[:, :]), r=[], w=["constA"], key="k_c0")
    P.dma("sp", lambda e: e.dma_start(out=cs[:, :, :], in_=T.cs_tok[:, :, :]), r=[], w=["constA"], key="k_c0")
    load_weight(P, C, w_in_b, T.w_in0.ap().rearrange("(k p) f -> p k f", p=128), 8, 704, "w_in_b", "w_in0")

    def pre(i):
        b = i % 2
        P.dma("sp", lambda e, i=i, b=b: e.dma_start(out=xf[b][:, :], in_=T.x[i * 128:(i + 1) * 128, :]),
              r=[], w=[("xf", b)], key="k_xf%d" % b)
        P.act(lambda e, b=b: e.copy(out=xb[b][:, :], in_=xf[b][:, :]), r=[("xf", b)], w=[("xb", b)])
        pT = ps[4 + b][:, :].bitcast(BF16)
        for j in range(8):
            P.pe(lambda e, j=j, b=b, pT=pT: e.transpose(pT[:, j * 128:(j + 1) * 128], xb[b][:, j * 128:(j + 1) * 128], ident[:, :]),
                 r=[("xb", b), "ident"], w=[("ps", 4 + b)])
        P.dve(lambda e, b=b, pT=pT: e.tensor_copy(out=xT[b][:, :, :], in_=pT.rearrange("p (k t) -> p k t", t=128)),
              r=[("ps", 4 + b)], w=[("xT", b)])
        ph1, ph2 = ps[b], ps[2 + b]
        for k in range(8):
            P.pe(lambda e, k=k, b=b, ph1=ph1: e.matmul(ph1[:, 0:384], lhsT=xT[b][:, k, :], rhs=w_in_b[:, k, 0:384], start=(k == 0), stop=(k == 7)),
                 r=[("xT", b), "w_in_b"], w=[("ps", b)])
        for k in range(8):
            P.pe(lambda e, k=k, b=b, ph2=ph2: e.matmul(ph2[:, 0:320], lhsT=xT[b][:, k, :], rhs=w_in_b[:, k, 384:704], start=(k == 0), stop=(k == 7)),
                 r=[("xT", b), "w_in_b"], w=[("ps", 2 + b)])

    def post(i):
        b = i % 2
        ph1, ph2 = ps[b], ps[2 + b]
        P.act(lambda e, b=b, ph1=ph1: e.activation(out=junk[:, 0:384], in_=ph1[:, 0:384], func=AF.Square, accum_out=st[b][:, 0:1]),
              r=[("ps", b)], w=["junk", ("st", b)])
        P.act(lambda e, b=b, ph2=ph2: e.activation(out=junk[:, 0:256], in_=ph2[:, 0:256], func=AF.Square, accum_out=st[b][:, 1:2]),
              r=[("ps", 2 + b)], w=["junk", ("st", b)])
        P.act(lambda e, b=b, ph2=ph2: e.copy(out=kr[b][:, :], in_=ph2[:, 256:320]), r=[("ps", 2 + b)], w=[("kr", b)])
        P.dve(lambda e, b=b: e.tensor_scalar(out=st[b][:, 2:3], in0=st[b][:, 0:1], scalar1=1.0 / 384, scalar2=RMS_EPS, op0=ALU.mult, op1=ALU.add),
              r=[("st", b)], w=[("st2", b)])
        P.dve(lambda e, b=b: e.tensor_scalar(out=st[b][:, 3:4], in0=st[b][:, 1:2], scalar1=1.0 / 256, scalar2=RMS_EPS, op0=ALU.mult, op1=ALU.add),
              r=[("st", b), ("st2", b)], w=[("st2", b)])
        P.act(lambda e, b=b: e.activation(out=st[b][:, 4:6], in_=st[b][:, 2:4], func=AF.Sqrt), r=[("st2", b)], w=[("st3", b)])
        P.dve(lambda e, b=b: e.reciprocal(out=st[b][:, 6:8], in_=st[b][:, 4:6]), r=[("st3", b)], w=[("st4", b)])
        P.dve(lambda e, b=b, ph1=ph1: e.scalar_tensor_tensor(out=lat[b][:, 0:384], in0=ph1[:, 0:384], scalar=st[b][:, 6:7], in1=gq[:, :], op0=ALU.mult, op1=ALU.mult),
              r=[("ps", b), ("st4", b), "constA"], w=[("lat", b)])
        P.dve(lambda e, b=b, ph2=ph2: e.scalar_tensor_tensor(out=lat[b][:, 384:640], in0=ph2[:, 0:256], scalar=st[b][:, 7:8], in1=gkv[:, :], op0=ALU.mult, op1=ALU.mult),
              r=[("ps", 2 + b), ("st4", b), "constA"], w=[("lat", b)])
        cosv, sinv = cs[:, i, 0:32], cs[:, i, 32:64]
        P.pool(lambda e, b=b, cosv=cosv: e.tensor_tensor(out=tmp[b][:, 0, :], in0=kr[b][:, 0:32], in1=cosv, op=ALU.mult), r=[("kr", b), "constA"], w=[("tmp", b)])
        P.pool(lambda e, b=b, sinv=sinv: e.tensor_tensor(out=tmp[b][:, 1, :], in0=kr[b][:, 32:64], in1=sinv, op=ALU.mult), r=[("kr", b), "constA", ("tmp", b)], w=[("tmp", b)])
        P.pool(lambda e, b=b, sinv=sinv: e.tensor_tensor(out=tmp[b][:, 2, :], in0=kr[b][:, 0:32], in1=sinv, op=ALU.mult), r=[("kr", b), "constA", ("tmp", b)], w=[("tmp", b)])
        P.pool(lambda e, b=b, cosv=cosv: e.tensor_tensor(out=tmp[b][:, 3, :], in0=kr[b][:, 32:64], in1=cosv, op=ALU.mult), r=[("kr", b), "constA", ("tmp", b)], w=[("tmp", b)])
        P.pool(lambda e, b=b: e.tensor_tensor(out=lat[b][:, 640:672], in0=tmp[b][:, 0, :], in1=tmp[b][:, 1, :], op=ALU.subtract), r=[("tmp", b)], w=[("lat", b)])
        P.pool(lambda e, b=b: e.tensor_tensor(out=lat[b][:, 672:704], in0=tmp[b][:, 2, :], in1=tmp[b][:, 3, :], op=ALU.add), r=[("tmp", b), ("lat", b)], w=[("lat", b)])
        pT2 = ps[6 + b][:, :].bitcast(BF16)
        for j in range(5):
            P.pe(lambda e, j=j, b=b, pT2=pT2: e.transpose(pT2[:, j * 128:(j + 1) * 128], lat[b][:, j * 128:(j + 1) * 128], ident[:, :]),
                 r=[("lat", b), "ident"], w=[("ps", 6 + b)])
        P.pe(lambda e, b=b, pT2=pT2: e.transpose(pT2[0:64, 640:768], lat[b][:, 640:704], ident[:, :]),
             r=[("lat", b), "ident"], w=[("ps", 6 + b)])
        P.act(lambda e, i=i, b=b, pT2=pT2: e.copy(out=latT[:, 0:5, i * 128:(i + 1) * 128], in_=pT2[:, 0:640].rearrange("p (k t) -> p k t", t=128)),
              r=[("ps", 6 + b)], w=["latT"])
        P.act(lambda e, i=i, b=b, pT2=pT2: e.copy(out=latT[0:64, 5, i * 128:(i + 1) * 128], in_=pT2[0:64, 640:768]),
              r=[("ps", 6 + b)], w=["latT"])
    def ship(c):
        for j in range(5):
            P.dma("sp", lambda e, j=j, c=c: e.dma_start(out=T.latT_in[c][j * 128:(j + 1) * 128, :], in_=latT[:, j, c * 512:(c + 1) * 512]), r=["latT"], w=[("latT_in", c)], key="k_lst%d" % c)
        P.dma("sp", lambda e, c=c: e.dma_start(out=T.latT_in[c][640:704, :], in_=latT[0:64, 5, c * 512:(c + 1) * 512]), r=["latT"], w=[("latT_in", c)], key="k_lst%d" % c)
        P.add("pool", lambda e, c=c: e.collective_compute("AllGather", ALU.bypass, replica_groups=[[0, 1, 2, 3], [4, 5, 6, 7]],
                                                           ins=[T.latT_in[c].ap()], outs=[T.latT_all[c].ap()]),
              reads=[("latT_in", c)], writes=["latT_all"], kind="cc", key="k_cc")

    for i in range(16):
        pre(i)
        if i >= 1:
            post(i - 1)
            if (i - 1) % 4 == 3:
                ship((i - 1) // 4)
    post(15)
    ship(3)
    P.barrier()


def phase_b0(nc, P, C, T):
    _ = (T.cc_f, T.ss_f, T.w_uq_c, T.w_ukv_c)
    sb = SBA(nc)
    ps = C.ps
    ident = C.ident
    wq = sb.alloc([128, 3, 512], BF16, "wq")
    wkv = sb.alloc([128, 2, 512], BF16, "wkv")
    KnT = [sb.alloc([128, SEQ], BF16, "KnT") for _ in range(2)]
    KrT = sb.alloc([128, SEQ], BF16, "KrT")
    Vaug = sb.alloc([128, 2, 64, 129], BF16, "Vaug")
    C.stg = [sb.alloc([128, 2048], F32, "stg") for _ in range(2)]
    C.stg_elems, C.stg_i = 2048, 0
    cqT = [sb.alloc([128, 3, 512], BF16, "cqT") for _ in range(2)]
    ckvT = [sb.alloc([128, 2, 512], BF16, "ckvT") for _ in range(2)]
    ccf = [sb.alloc([64, 512], F32, "ccf") for _ in range(2)]
    ssf = [sb.alloc([64, 512], F32, "ssf") for _ in range(2)]
    QnT = [[sb.alloc([128, 512], BF16, "QnT") for _ in range(2)] for _ in range(2)]
    QrT = [[sb.alloc([128, 512], BF16, "QrT") for _ in range(2)] for _ in range(2)]
    r1 = [sb.alloc([64, 512], F32, "r1") for _ in range(2)]
    r2 = [sb.alloc([64, 512], F32, "r2") for _ in range(2)]
    PT = [sb.alloc([128, 512], BF16, "PT") for _ in range(6)]
    rl = [sb.alloc([128, 4], F32, "rl") for _ in range(2)]
    ob = [sb.alloc([128, 128], BF16, "ob") for _ in range(2)]
    OTs = [sb.alloc([128, 512], BF16, "OTs") for _ in range(2)]
    Pacc = [sb.alloc([128, 512], F32, "Pacc") for _ in range(2)]
    OTf = [sb.alloc([128, 512], F32, "OTf") for _ in range(2)]

    load_weight(P, C, wq, T.w_uq_c.ap().rearrange("(k p) f -> p k f", p=128), 3, 512, "wq", "w_uq_c")
    load_weight(P, C, wkv, T.w_ukv_c.ap().rearrange("(k p) f -> p k f", p=128), 2, 512, "wkv", "w_ukv_c")
    P.pool(lambda e: e.memset(Vaug[:, :, :, 128:129], 1.0), r=[], w=["Vones"])
    P.pool(lambda e: e.memset(KrT[64:128, :], 0.0), r=[], w=["Kzero"])
    for bt_ in range(2):
        for hl_ in range(2):
            P.pool(lambda e, bt_=bt_, hl_=hl_: e.memset(QrT[bt_][hl_][64:128, :], 0.0), r=[], w=[("Qzero", bt_, hl_)])
    def lat_rows(r, f0, n):
        k = f0 // 256
        rows_k = (256, 256, 192)[k]
        return T.latT_all[k], r * rows_k + f0 % 256

    for tt_ in range(16):
        P.dma("sp", lambda e, tt_=tt_: e.dma_start(out=KrT[0:64, tt_ * 512:(tt_ + 1) * 512], in_=T.latT_all[tt_ % 4][(tt_ // 4) * 704 + 640:(tt_ // 4) * 704 + 704, :]),
              r=["latT_all"], w=["KrT"], key="k_krt")

    pj = [0]

    def pbank():
        b = 5 + (pj[0] % 3)
        pj[0] += 1
        return b

    sidx = [0]
    for t in range(16):
        bt = t % 2
        r, c0 = t // 4, (t % 4) * 512
        lt = T.latT_all[t % 4]
        P.dma("sp", lambda e, lt=lt, r=r, bt=bt: e.dma_start(out=cqT[bt][:, :, :], in_=lt[r * 704:r * 704 + 384, :].rearrange("(j p) c -> p j c", p=128)),
              r=["latT_all"], w=[("cqT", bt)], key="k_cq%d" % bt)
        P.dma("sp", lambda e, lt=lt, r=r, bt=bt: e.dma_start(out=ckvT[bt][:, :, :], in_=lt[r * 704 + 384:r * 704 + 640, :].rearrange("(j p) c -> p j c", p=128)),
              r=["latT_all"], w=[("ckvT", bt)], key="k_ckv%d" % bt)
        P.dma("sp", lambda e, t=t, bt=bt: e.dma_start(out=ccf[bt][:, :], in_=T.cc_f[:, t * 512:(t + 1) * 512]), r=[], w=[("ccf", bt)], key="k_ccf%d" % bt)
        P.dma("sp", lambda e, t=t, bt=bt: e.dma_start(out=ssf[bt][:, :], in_=T.ss_f[:, t * 512:(t + 1) * 512]), r=[], w=[("ssf", bt)], key="k_ssf%d" % bt)
        for s in range(4):
            pb = pbank()
            for j in range(2):
                P.pe(lambda e, s=s, j=j, pb=pb, bt=bt: e.matmul(ps[pb][:, 0:256], lhsT=ckvT[bt][:, j, s * 128:(s + 1) * 128], rhs=wkv[:, j, 256:512], start=(j == 0), stop=(j == 1)),
                     r=[("ckvT", bt), "wkv"], w=[("ps", pb)])
            P.act(lambda e, s=s, pb=pb, t=t: e.copy(out=Vaug[:, :, 4 * t + s, 0:128], in_=ps[pb][:, 0:256].rearrange("p (h d) -> p h d", d=128)),
                  r=[("ps", pb)], w=[("V", 4 * t + s)])
        for hl in range(2):
            pb = pbank()
            for j in range(2):
                P.pe(lambda e, j=j, pb=pb, bt=bt, hl=hl: e.matmul(ps[pb][:, :], lhsT=wkv[:, j, hl * 128:(hl + 1) * 128], rhs=ckvT[bt][:, j, :], start=(j == 0), stop=(j == 1)),
                     r=[("ckvT", bt), "wkv"], w=[("ps", pb)])
            P.dve(lambda e, pb=pb, hl=hl, t=t: e.tensor_copy(out=KnT[hl][:, t * 512:(t + 1) * 512], in_=ps[pb][:, :]),
                  r=[("ps", pb)], w=[("KnT", hl, t)])
            pb = pbank()
            for j in range(3):
                P.pe(lambda e, j=j, pb=pb, bt=bt, hl=hl: e.matmul(ps[pb][:, :], lhsT=wq[:, j, hl * 256:hl * 256 + 128], rhs=cqT[bt][:, j, :], start=(j == 0), stop=(j == 2)),
                     r=[("cqT", bt), "wq"], w=[("ps", pb)])
            P.act(lambda e, pb=pb, hl=hl, bt=bt: e.copy(out=QnT[bt][hl][:, :], in_=ps[pb][:, :]), r=[("ps", pb)], w=[("QnT", bt, hl)])
            pa = pbank()
            for j in range(3):
                P.pe(lambda e, j=j, pa=pa, bt=bt, hl=hl: e.matmul(ps[pa][0:64, :], lhsT=wq[:, j, hl * 256 + 128:hl * 256 + 192], rhs=cqT[bt][:, j, :], start=(j == 0), stop=(j == 2)),
                     r=[("cqT", bt), "wq"], w=[("ps", pa)])
            P.dve(lambda e, pa=pa, bt=bt, hl=hl: e.tensor_tensor(out=r1[hl][:, :], in0=ps[pa][0:64, :], in1=ccf[bt][:, :], op=ALU.mult),
                  r=[("ps", pa), ("ccf", bt)], w=[("r1", hl)])
            pb2 = pbank()
            for j in range(3):
                P.pe(lambda e, j=j, pb2=pb2, bt=bt, hl=hl: e.matmul(ps[pb2][0:64, :], lhsT=wq[:, j, hl * 256 + 192:hl * 256 + 256], rhs=cqT[bt][:, j, :], start=(j == 0), stop=(j == 2)),
                     r=[("cqT", bt), "wq"], w=[("ps", pb2)])
            P.dve(lambda e, pb2=pb2, bt=bt, hl=hl: e.tensor_tensor(out=r2[hl][:, :], in0=ps[pb2][0:64, :], in1=ssf[bt][:, :], op=ALU.mult),
                  r=[("ps", pb2), ("ssf", bt)], w=[("r2", hl)])
            P.pool(lambda e, bt=bt, hl=hl: e.tensor_tensor(out=QrT[bt][hl][0:64, :], in0=r1[hl][:, :], in1=r2[hl][:, :], op=ALU.add),
                   r=[("r1", hl), ("r2", hl)], w=[("QrT", bt, hl)])
        for hl in range(2):
            nJ = 4 * (t + 1)
            tiles = list(range(nJ))
            meta = {}

            def qk(J, hl=hl, t=t, bt=bt):
                n = sidx[0]
                sidx[0] += 1
                sbk, pbf = n % 3, n % 6
                m = max(0, J - 4 * t)
                q0 = 128 * m
                meta[J] = (sbk, pbf, m, q0)
                P.pe(lambda e: e.matmul(ps[sbk][:, q0:512], lhsT=KnT[hl][:, J * 128:(J + 1) * 128], rhs=QnT[bt][hl][:, q0:512], start=True, stop=False),
                     r=[("KnT", hl, J // 4), ("QnT", bt, hl)], w=[("ps", sbk)])
                P.pe(lambda e: e.matmul(ps[sbk][:, q0:512], lhsT=KrT[:, J * 128:(J + 1) * 128], rhs=QrT[bt][hl][:, q0:512], start=False, stop=True),
                     r=["KrT", "Kzero", ("QrT", bt, hl), ("Qzero", bt, hl)], w=[("ps", sbk)])

            def ex(J, t=t):
                sbk, pbf, m, q0 = meta[J]
                P.act(lambda e: e.activation(out=PT[pbf][:, q0:512], in_=ps[sbk][:, q0:512], func=AF.Exp, scale=SC0),
                      r=[("ps", sbk)], w=[("PT", pbf)])
                if J >= 4 * t:
                    P.pool(lambda e: e.memset(PT[pbf][64:128, q0:q0 + 64], 0.0), r=[("PT", pbf)], w=[("PT", pbf)])

            ehl = (2 * t + hl) % 2
            acc = 3 + ehl
            pa = Pacc[ehl]

            def av(J, hl=hl, nJ=nJ, acc=acc, pa=pa, ehl=ehl):
                sbk, pbf, m, q0 = meta[J]
                P.pe(lambda e: e.matmul(ps[acc][:, q0:512], lhsT=Vaug[:, hl, J, 0:128], rhs=PT[pbf][:, q0:512], start=(J == 0), stop=(J == nJ - 1)),
                     r=[("PT", pbf), ("V", J)], w=[("ps", acc)])
                SPL = 320
                if J == 0:
                    P.dve(lambda e: e.tensor_copy(out=pa[:, 0:SPL], in_=PT[pbf][:, 0:SPL]), r=[("PT", pbf)], w=[("PaccD", ehl)])
                    P.pool(lambda e: e.tensor_copy(out=pa[:, SPL:512], in_=PT[pbf][:, SPL:512]), r=[("PT", pbf)], w=[("PaccP", ehl)])
                else:
                    if q0 < SPL:
                        P.dve(lambda e: e.tensor_tensor(out=pa[:, q0:SPL], in0=pa[:, q0:SPL], in1=PT[pbf][:, q0:SPL], op=ALU.add),
                              r=[("PT", pbf), ("PaccD", ehl)], w=[("PaccD", ehl)])
                    q1 = max(q0, SPL)
                    P.pool(lambda e: e.tensor_tensor(out=pa[:, q1:512], in0=pa[:, q1:512], in1=PT[pbf][:, q1:512], op=ALU.add),
                           r=[("PT", pbf), ("PaccP", ehl)], w=[("PaccP", ehl)])

            qk(tiles[0])
            if nJ > 1:
                qk(tiles[1])
            for i, J in enumerate(tiles):
                ex(J)
                if i + 2 < nJ:
                    qk(tiles[i + 2])
                av(J)
            eb = ehl
            P.act(lambda e, eb=eb, acc=acc: e.copy(out=OTf[eb][:, :], in_=ps[acc][:, :]), r=[("ps", acc)], w=[("OTf", eb)])
            pl = pbank()
            for qs in range(4):
                P.pe(lambda e, qs=qs, pl=pl, pa=pa: e.matmul(ps[pl][:, qs:qs + 1], lhsT=pa[:, qs * 128:(qs + 1) * 128], rhs=C.ones_f[:, 0:1], start=True, stop=True, skip_group_check=True),
                     r=[("PaccD", ehl), ("PaccP", ehl), "ident"], w=[("ps", pl)])
            ptr = pbank()
            for qs in range(4):
                P.pe(lambda e, qs=qs, ptr=ptr, eb=eb: e.transpose(ps[ptr][:, qs * 128:(qs + 1) * 128], OTf[eb][:, qs * 128:(qs + 1) * 128], C.ident_f[:, :]),
                     r=[("OTf", eb), "ident"], w=[("ps", ptr)])
            P.dve(lambda e, eb=eb, pl=pl: e.reciprocal(out=rl[eb][:, 0:4], in_=ps[pl][:, 0:4]), r=[("ps", pl)], w=[("rl", eb)])
            pbT = pbank()
            pT = ps[pbT][:, :].bitcast(BF16)
            for qs in range(4):
                ob_i = qs % 2
                P.dve(lambda e, qs=qs, ptr=ptr, eb=eb, ob_i=ob_i: e.tensor_scalar(out=ob[ob_i][:, :], in0=ps[ptr][:, qs * 128:(qs + 1) * 128], scalar1=rl[eb][:, qs:qs + 1], scalar2=None, op0=ALU.mult),
                      r=[("ps", ptr), ("rl", eb)], w=[("ob", ob_i)])
                P.pe(lambda e, qs=qs, ob_i=ob_i, pT=pT: e.transpose(pT[:, qs * 128:(qs + 1) * 128], ob[ob_i][:, :], ident[:, :]),
                     r=[("ob", ob_i), "ident"], w=[("ps", pbT)])
            P.act(lambda e, eb=eb, pT=pT: e.copy(out=OTs[eb][:, :], in_=pT[:, 0:512]), r=[("ps", pbT)], w=[("OTs", eb)])
            P.dma("sp", lambda e, eb=eb, hl=hl, t=t: e.dma_start(out=T.OT_in[t // 4, hl * 128:(hl + 1) * 128, (t % 4) * 512:(t % 4 + 1) * 512], in_=OTs[eb][:, :]),
                  r=[("OTs", eb)], w=[("OT_in", t // 4, eb)], key="k_ot%d_%d" % (t // 4, eb))
            if t % 4 == 3 and hl == 1:
                P.add("pool", lambda e, c=t // 4: e.collective_compute("AllGather", ALU.bypass, replica_groups=[[0, 1, 2, 3], [4, 5, 6, 7]],
                                                               ins=[T.OT_in[c]], outs=[T.OT_all[c]]),
                      reads=[("OT_in", t // 4, 0), ("OT_in", t // 4, 1)], writes=["OT_all"], kind="cc", key="k_cc")
    P.barrier()


def phase_b1(nc, P, C, T):
    _ = (T.w1_c, T.lamv, T.g_sub, T.bias1, T.F1, T.qaug)
    sb = SBA(nc)
    ps = C.ps
    ident = C.ident
    w1 = sb.alloc([128, 8, 768], BF16, "w1")
    KT = [[sb.alloc([128, SEQ], BF16, "KT") for _ in range(2)] for _ in range(2)]
    Vaug = sb.alloc([128, 2, 64, 129], BF16, "Vaug1")
    C.stg = [sb.alloc([128, 2048], F32, "stg") for _ in range(2)]
    C.stg_elems, C.stg_i = 2048, 0
    xT = [sb.alloc([128, 8, 512], BF16, "xT1") for _ in range(2)]
    QT = [[[sb.alloc([128, 512], BF16, "QT") for _ in range(2)] for _ in range(2)] for _ in range(2)]
    PT = [sb.alloc([128, 512], BF16, "PT1") for _ in range(4)]
    bias1 = sb.alloc([128, 2, 67], F32, "bias1")
    F1 = sb.alloc([128, 2, 128], F32, "F1")
    lamv = sb.alloc([128, 4, 64], F32, "lamv")
    gsub = sb.alloc([128, 128], F32, "gsub")
    lm = sb.alloc([128, 8], F32, "lm")
    junk = sb.alloc([128, 128], F32, "junk1")
    rl = [sb.alloc([128, 8], F32, "rl1") for _ in range(2)]
    o1 = [sb.alloc([128, 128], F32, "o1") for _ in range(2)]
    oo = [sb.alloc([128, 128], F32, "oo") for _ in range(2)]
    ob = [sb.alloc([128, 128], BF16, "ob1") for _ in range(2)]
    OTs = [sb.alloc([128, 512], BF16, "OTs1") for _ in range(2)]

    P.dma("sp", lambda e: e.dma_start(out=bias1[:, :, :], in_=T.bias1[:, :, :]), r=[], w=["constB"], key="k_c0")
    P.dma("sp", lambda e: e.dma_start(out=F1[:, :, :], in_=T.F1[:, :, :]), r=[], w=["constB"], key="k_c0")
    P.dma("sp", lambda e: e.dma_start(out=lamv[:, :, :], in_=T.lamv[:, :, :]), r=[], w=["constB"], key="k_c0")
    P.dma("sp", lambda e: e.dma_start(out=gsub[:, :], in_=T.g_sub[:, :]), r=[], w=["constB"], key="k_c0")
    for mp in range(2):
        for bt in range(2):
            for hl in range(2):
                P.dma("sp", lambda e, mp=mp, bt=bt, hl=hl: e.dma_start(out=QT[mp][bt][hl][64:66, :], in_=T.qaug[:, hl, :]), r=[], w=["constB"], key="k_c0")
        for hl in range(2):
            P.pool(lambda e, mp=mp, hl=hl: e.memset(KT[mp][hl][64:66, :], 1.0), r=[], w=[("Kones", mp, hl)])
    P.pool(lambda e: e.memset(Vaug[:, :, :, 128:129], 1.0), r=[], w=["Vones"])
    load_weight(P, C, w1, T.w1_c.ap().rearrange("(k p) f -> p k f", p=128), 8, 768, "w1", "w1_c")
    P.dve(lambda e: e.scalar_tensor_tensor(out=junk[:, 0:64], in0=lamv[:, 0, :], scalar=1.0, in1=lamv[:, 1, :], op0=ALU.mult, op1=ALU.mult, accum_out=lm[:, 0:1]),
          r=["constB"], w=["junk", "lm0"])
    P.dve(lambda e: e.scalar_tensor_tensor(out=junk[:, 0:64], in0=lamv[:, 2, :], scalar=1.0, in1=lamv[:, 3, :], op0=ALU.mult, op1=ALU.mult, accum_out=lm[:, 1:2]),
          r=["constB", "junk", "lm0"], w=["junk", "lm0"])
    P.act(lambda e: e.activation(out=lm[:, 2:4], in_=lm[:, 0:2], func=AF.Exp), r=["lm0"], w=["lm1"])
    P.dve(lambda e: e.tensor_tensor(out=lm[:, 4:5], in0=lm[:, 2:3], in1=lm[:, 3:4], op=ALU.subtract), r=["lm1"], w=["lm2"])
    P.dve(lambda e: e.tensor_scalar(out=lm[:, 5:6], in0=lm[:, 4:5], scalar1=-1.0, scalar2=-LAMBDA_INIT, op0=ALU.mult, op1=ALU.add), r=["lm2"], w=["neglam"])
    P.dve(lambda e: e.tensor_scalar(out=gsub[:, :], in0=gsub[:, :], scalar1=1.0 - LAMBDA_INIT, scalar2=None, op0=ALU.mult), r=["constB"], w=["gsub2"])

    pj = [0]

    def pbank():
        b = 6 + (pj[0] % 2)
        pj[0] += 1
        return b

    sidx = [0]
    for t in range(16):
        bt = t % 2
        r, c0 = t // 4, (t % 4) * 512
        P.dma("sp", lambda e, r=r, t=t, bt=bt: e.dma_start(out=xT[bt][:, :, :], in_=T.x1T_all[t % 4][r * 1024:(r + 1) * 1024, :].rearrange("(k p) c -> p k c", p=128)),
              r=["x1T_all"], w=[("xT", bt)], key="k_lat%d" % bt)
        for s in range(4):
            pb = pbank()
            for k in range(8):
                P.pe(lambda e, s=s, k=k, pb=pb, bt=bt: e.matmul(ps[pb][:, 0:256], lhsT=xT[bt][:, k, s * 128:(s + 1) * 128], rhs=w1[:, k, 512:768], start=(k == 0), stop=(k == 7)),
                     r=[("xT", bt), "w1"], w=[("ps", pb)])
            P.act(lambda e, s=s, pb=pb, t=t: e.copy(out=Vaug[:, :, 4 * t + s, 0:128], in_=ps[pb][:, 0:256].rearrange("p (h d) -> p h d", d=128)),
                  r=[("ps", pb)], w=[("V", 4 * t + s)])
        for hl in range(2):
            pb = pbank()
            for k in range(8):
                P.pe(lambda e, k=k, pb=pb, bt=bt, hl=hl: e.matmul(ps[pb][:, :], lhsT=w1[:, k, 256 + hl * 128:256 + (hl + 1) * 128], rhs=xT[bt][:, k, :], start=(k == 0), stop=(k == 7)),
                     r=[("xT", bt), "w1"], w=[("ps", pb)])
            P.act(lambda e, pb=pb, hl=hl, t=t: e.copy(out=KT[0][hl][0:64, t * 512:(t + 1) * 512], in_=ps[pb][0:64, :]), r=[("ps", pb)], w=[("K", 0, hl, t)])
            P.dve(lambda e, pb=pb, hl=hl, t=t: e.tensor_copy(out=KT[1][hl][0:64, t * 512:(t + 1) * 512], in_=ps[pb][64:128, :]), r=[("ps", pb)], w=[("K", 1, hl, t)])
            pb = pbank()
            for k in range(8):
                P.pe(lambda e, k=k, pb=pb, bt=bt, hl=hl: e.matmul(ps[pb][:, :], lhsT=w1[:, k, hl * 128:(hl + 1) * 128], rhs=xT[bt][:, k, :], start=(k == 0), stop=(k == 7)),
                     r=[("xT", bt), "w1"], w=[("ps", pb)])
            P.act(lambda e, pb=pb, hl=hl, bt=bt: e.copy(out=QT[0][bt][hl][0:64, :], in_=ps[pb][0:64, :]), r=[("ps", pb)], w=[("Q", 0, bt, hl)])
            P.dve(lambda e, pb=pb, hl=hl, bt=bt: e.tensor_copy(out=QT[1][bt][hl][0:64, :], in_=ps[pb][64:128, :]), r=[("ps", pb)], w=[("Q", 1, bt, hl)])
        for hl in range(2):
            nJ = 4 * (t + 1)
            tiles = [(J, mp) for J in range(nJ) for mp in range(2)]
            meta = {}

            def qk(tl, hl=hl, t=t, bt=bt):
                J, mp = tl
                n = sidx[0]
                sidx[0] += 1
                sbk, pbf = n % 3, n % 4
                m = max(0, J - 4 * t)
                q0 = 128 * m
                meta[tl] = (sbk, pbf, m, q0)
                P.pe(lambda e: e.matmul(ps[sbk][:, q0:512], lhsT=KT[mp][hl][0:66, J * 128:(J + 1) * 128], rhs=QT[mp][bt][hl][0:66, q0:512], start=True, stop=True),
                     r=[("K", mp, hl, J // 4), ("Kones", mp, hl), ("Q", mp, bt, hl), "constB"], w=[("ps", sbk)])

            def ex(tl, hl=hl, t=t):
                J, mp = tl
                sbk, pbf, m, q0 = meta[tl]
                idx = (4 * t - J) + 3
                P.act(lambda e: e.activation(out=PT[pbf][:, q0:512], in_=ps[sbk][:, q0:512], func=AF.Exp, bias=bias1[:, hl, idx:idx + 1], scale=SC1),
                      r=[("ps", sbk), "constB"], w=[("PT", pbf)])
                if J >= 4 * t:
                    P.pool(lambda e: e.tensor_tensor(out=PT[pbf][:, q0:q0 + 128], in0=PT[pbf][:, q0:q0 + 128], in1=F1[:, hl, :], op=ALU.mult),
                           r=[("PT", pbf), "constB"], w=[("PT", pbf)])

            def av(tl, hl=hl, nJ=nJ):
                J, mp = tl
                sbk, pbf, m, q0 = meta[tl]
                for qs in range(m, 4):
                    a = mp * 4 + qs
                    bank, off = 3 + a // 3, (a % 3) * 129
                    P.pe(lambda e, qs=qs, bank=bank, off=off, a=a: e.matmul(ps[bank][:, off:off + 129], lhsT=PT[pbf][:, qs * 128:(qs + 1) * 128], rhs=Vaug[:, hl, J, :],
                                                                      start=(J == 0 and a % 3 == 0), stop=(J == nJ - 1), skip_group_check=True),
                         r=[("PT", pbf), ("V", J), "Vones"], w=[("ps", bank)])

            qk(tiles[0])
            qk(tiles[1])
            for i, tl in enumerate(tiles):
                ex(tl)
                if i + 2 < len(tiles):
                    qk(tiles[i + 2])
                av(tl)
            eb = (2 * t + hl) % 2
            pbT = pbank()
            pT = ps[pbT][:, :].bitcast(BF16)
            for qs in range(4):
                a1, a2 = qs, 4 + qs
                b1_, f1_ = 3 + a1 // 3, (a1 % 3) * 129
                b2_, f2_ = 3 + a2 // 3, (a2 % 3) * 129
                q2 = qs % 2
                P.dve(lambda e, b1_=b1_, f1_=f1_, q2=q2: e.reciprocal(out=rl[q2][:, 0:1], in_=ps[b1_][:, f1_ + 128:f1_ + 129]), r=[("ps", b1_)], w=[("rl", q2)])
                P.dve(lambda e, b2_=b2_, f2_=f2_, q2=q2: e.reciprocal(out=rl[q2][:, 1:2], in_=ps[b2_][:, f2_ + 128:f2_ + 129]), r=[("ps", b2_), ("rl", q2)], w=[("rl", q2)])
                P.dve(lambda e, q2=q2: e.tensor_tensor(out=rl[q2][:, 2:3], in0=rl[q2][:, 1:2], in1=lm[:, 5:6], op=ALU.mult), r=[("rl", q2), "neglam"], w=[("rl2", q2)])
                P.dve(lambda e, b1_=b1_, f1_=f1_, q2=q2: e.tensor_scalar(out=o1[q2][:, :], in0=ps[b1_][:, f1_:f1_ + 128], scalar1=rl[q2][:, 0:1], scalar2=None, op0=ALU.mult),
                      r=[("ps", b1_), ("rl", q2)], w=[("o1", q2)])
                P.dve(lambda e, b2_=b2_, f2_=f2_, q2=q2: e.scalar_tensor_tensor(out=oo[q2][:, :], in0=ps[b2_][:, f2_:f2_ + 128], scalar=rl[q2][:, 2:3], in1=o1[q2][:, :], op0=ALU.mult, op1=ALU.add),
                      r=[("ps", b2_), ("rl2", q2), ("o1", q2)], w=[("oo", q2)])
                P.act(lambda e, q2=q2: e.activation(out=junk[:, :], in_=oo[q2][:, :], func=AF.Square, accum_out=rl[q2][:, 3:4]), r=[("oo", q2)], w=["junk", ("rl3", q2)])
                P.dve(lambda e, q2=q2: e.tensor_scalar(out=rl[q2][:, 4:5], in0=rl[q2][:, 3:4], scalar1=1.0 / 128, scalar2=RMS_EPS, op0=ALU.mult, op1=ALU.add), r=[("rl3", q2)], w=[("rl4", q2)])
                P.act(lambda e, q2=q2: e.activation(out=rl[q2][:, 5:6], in_=rl[q2][:, 4:5], func=AF.Sqrt), r=[("rl4", q2)], w=[("rl5", q2)])
                P.dve(lambda e, q2=q2: e.reciprocal(out=rl[q2][:, 6:7], in_=rl[q2][:, 5:6]), r=[("rl5", q2)], w=[("rl6", q2)])
                P.dve(lambda e, q2=q2: e.scalar_tensor_tensor(out=ob[q2][:, :], in0=oo[q2][:, :], scalar=rl[q2][:, 6:7], in1=gsub[:, :], op0=ALU.mult, op1=ALU.mult),
                      r=[("oo", q2), ("rl6", q2), "gsub2"], w=[("ob", q2)])
                P.pe(lambda e, qs=qs, q2=q2, pT=pT: e.transpose(pT[:, qs * 128:(qs + 1) * 128], ob[q2][:, :], ident[:, :]), r=[("ob", q2), "ident"], w=[("ps", pbT)])
            P.act(lambda e, eb=eb, pT=pT: e.copy(out=OTs[eb][:, :], in_=pT[:, 0:512]), r=[("ps", pbT)], w=[("OTs", eb)])
            P.dma("sp", lambda e, eb=eb, hl=hl, t=t: e.dma_start(out=T.OT_in[t // 4, hl * 128:(hl + 1) * 128, (t % 4) * 512:(t % 4 + 1) * 512], in_=OTs[eb][:, :]),
                  r=[("OTs", eb)], w=[("OT_in", t // 4, eb)], key="k_ot%d_%d" % (t // 4, eb))
            if t % 4 == 3 and hl == 1:
                P.add("pool", lambda e, c=t // 4: e.collective_compute("AllGather", ALU.bypass, replica_groups=[[0, 1, 2, 3], [4, 5, 6, 7]],
                                                               ins=[T.OT_in[c]], outs=[T.OT_all[c]]),
                      reads=[("OT_in", t // 4, 0), ("OT_in", t // 4, 1)], writes=["OT_all"], kind="cc", key="k_cc")
    P.barrier()


def _layernorm(P, C, src, dst, g, b, st, mv, tag, srctok, dsttok):
    for hf in range(2):
        P.dve(lambda e, hf=hf: e.bn_stats(out=st[:, hf * 6:(hf + 1) * 6], in_=src[:, hf * 512:(hf + 1) * 512]),
              r=[srctok] + ([(tag, "st")] if hf else []), w=[(tag, "st")])
    P.dve(lambda e: e.bn_aggr(out=mv[:, 0:2], in_=st[:, 0:12]), r=[(tag, "st")], w=[(tag, "mv")])
    P.dve(lambda e: e.tensor_scalar(out=mv[:, 2:3], in0=mv[:, 1:2], scalar1=LN_EPS, scalar2=None, op0=ALU.add), r=[(tag, "mv")], w=[(tag, "mv2")])
    P.act(lambda e: e.activation(out=mv[:, 3:4], in_=mv[:, 2:3], func=AF.Sqrt), r=[(tag, "mv2")], w=[(tag, "mv3")])
    P.dve(lambda e: e.reciprocal(out=mv[:, 4:5], in_=mv[:, 3:4]), r=[(tag, "mv3")], w=[(tag, "mv4")])
    P.dve(lambda e: e.tensor_scalar(out=dst[:, :], in0=src[:, :], scalar1=mv[:, 0:1], scalar2=mv[:, 4:5], op0=ALU.subtract, op1=ALU.mult),
          r=[srctok, (tag, "mv"), (tag, "mv4")], w=[dsttok])
    P.pool(lambda e: e.tensor_tensor(out=dst[:, :], in0=dst[:, :], in1=g[:, :], op=ALU.mult), r=[dsttok, "lnp"], w=[dsttok])
    P.pool(lambda e: e.tensor_tensor(out=dst[:, :], in0=dst[:, :], in1=b[:, :], op=ALU.add), r=[dsttok, "lnp"], w=[dsttok])


def phase_c(nc, P, C, T, L):
    ps = C.ps
    ident = C.ident
    KB = 1024
    base = SB_BASE

    def at(off_kb, shape, dtype, name):
        C.cn += 1
        return nc.alloc_sbuf_tensor_at("%s_c%d" % (name, C.cn), list(shape), dtype, offset=base + int(off_kb * KB))

    res_src = T.x if L == 0 else T.h1
    out_dst = T.h1 if L == 0 else T.out
    w_o = T.w_o0 if L == 0 else T.w_o1
    lng = [T.ln_mix_g[L], T.ln_mix_b[L], T.ln_ffn_g[L], T.ln_ffn_b[L]]
    w_gu, w_dn = T.w_gu[L], T.w_dn[L]

    XmT = at(0, [128, 8, NT], BF16, "XmT")
    HT = at(32, [128, 22, NT], BF16, "HT")
    wd_b = at(120, [128, 22, 1024], BF16, "wd_b")
    ln2g = at(164, [128, 1024], F32, "ln2g")
    ln2b = at(168, [128, 1024], F32, "ln2b")
    small = at(172, [128, 64], F32, "small")
    OTm = at(32, [128, 8, NT], BF16, "OTm")
    wo_b = at(64, [128, 8, 1024], BF16, "wo_b")
    ln1g = at(80, [128, 1024], F32, "ln1g")
    ln1b = at(84, [128, 1024], F32, "ln1b")
    xres = [at(88 + 4 * i, [128, 1024], F32, "xres") for i in range(2)]
    y = [at(96 + 4 * i, [128, 1024], F32, "y") for i in range(2)]
    hm = [at(104 + 4 * i, [128, 1024], F32, "hm") for i in range(2)]
    hb = [at(112 + 2 * i, [128, 1024], BF16, "hb") for i in range(2)]
    C.stg = [at(120 + 16 * i, [128, 4096], F32, "stg") for i in range(2)]
    C.stg_elems, C.stg_i = 4096, 0
    st = [small[:, 0:12], small[:, 16:28]]
    mv = [small[:, 32:40], small[:, 40:48]]

    def dyn_load(e, h):
        r = P.pid % 4
        return e.dma_start(out=OTm[:, h, :], in_=T.OT_all[bass.ds(r, 1), h * 128:(h + 1) * 128, :].rearrange("o p c -> (o p) c"))

    for h in range(8):
        P.dma("sp", lambda e, h=h: dyn_load(e, h), r=["OT_all"], w=["OTm"], key="k_otm")
    P.dma("sp", lambda e: e.dma_start(out=ln1g[:, :], in_=lng[0][:, :]), r=[], w=["lnp"], key="k_c0")
    P.dma("sp", lambda e: e.dma_start(out=ln1b[:, :], in_=lng[1][:, :]), r=[], w=["lnp"], key="k_c0")
    load_weight(P, C, wo_b, w_o.ap().rearrange("(k p) f -> p k f", p=128), 8, 1024, "wo_b", "w_o", q="act")

    def pre1(i):
        b = i % 2
        P.dma("sp", lambda e, i=i, b=b: e.dma_start(out=xres[b][:, :], in_=res_src[i * 128:(i + 1) * 128, :]), r=["res_src"], w=[("xres", b)], key="k_xr%d" % b)
        for hf in range(2):
            pb = 2 * b + hf
            for h in range(8):
                P.pe(lambda e, h=h, hf=hf, pb=pb, i=i: e.matmul(ps[pb][:, :], lhsT=OTm[:, h, i * 128:(i + 1) * 128], rhs=wo_b[:, h, hf * 512:(hf + 1) * 512], start=(h == 0), stop=(h == 7)),
                     r=["OTm", "wo_b"], w=[("ps", pb)])

    def y1(i):
        b = i % 2
        for hf in range(2):
            pb = 2 * b + hf
            P.dve(lambda e, hf=hf, pb=pb, b=b: e.scalar_tensor_tensor(out=y[b][:, hf * 512:(hf + 1) * 512], in0=xres[b][:, hf * 512:(hf + 1) * 512], scalar=ALPHA, in1=ps[pb][:, :], op0=ALU.mult, op1=ALU.add),
                  r=[("xres", b), ("ps", pb)], w=[("y", b)])

    def post1(i):
        b = i % 2
        _layernorm(P, C, y[b], hm[b], ln1g, ln1b, st[b], mv[b], ("ln", b), ("y", b), ("hm", b))
        P.act(lambda e, b=b: e.copy(out=hb[b][:, :], in_=hm[b][:, :]), r=[("hm", b)], w=[("hb", b)])
        pT = ps[4 + b][:, :].bitcast(BF16)
        for j in range(8):
            P.pe(lambda e, j=j, b=b, pT=pT: e.transpose(pT[:, j * 128:(j + 1) * 128], hb[b][:, j * 128:(j + 1) * 128], ident[:, :]),
                 r=[("hb", b), "ident"], w=[("ps", 4 + b)])
        P.act(lambda e, i=i, b=b, pT=pT: e.copy(out=XmT[:, :, i * 128:(i + 1) * 128], in_=pT.rearrange("p (k t) -> p k t", t=128)),
              r=[("ps", 4 + b)], w=["XmT"])
        P.dma("sp", lambda e, i=i, b=b: e.dma_start(out=T.hmid[i * 128:(i + 1) * 128, :], in_=hm[b][:, :]), r=[("hm", b)], w=["hmid"], key="k_hm%d" % b)

    for i in range(16):
        pre1(i)
        if i >= 1:
            post1(i - 1)
        y1(i)
    post1(15)
    P.barrier()

    wgu = [at(172.5 + 4 * i, [128, 8, 256], BF16, "wgu") for i in range(2)]
    stgA = [at(180.5 + 8 * i, [128, 2048], F32, "stgA") for i in range(2)]
    stgB = [at(196.5 + 4 * i, [128, 1024], F32, "stgB") for i in range(2)]
    C.sg = [at(164 + 2 * i, [128, 512], F32, "sg") for i in range(2)]
    gsrc = w_gu.ap().rearrange("(k p) f -> p k f", p=128)
    wd_src = w_dn.ap().rearrange("(j p) f -> p j f", p=128)
    for j in range(22):
        bj = j % 2
        s = j % 2
        stv = stgA[s][:, :].rearrange("p (k f) -> p k f", f=256)
        P.dma("sp", lambda e, j=j, stv=stv: e.dma_start(out=stv[:, :, 0:128], in_=gsrc[:, :, j * 128:(j + 1) * 128]), r=["w_gu"], w=[("stgA", s)], key="k_stg%d" % s)
        P.dma("sp", lambda e, j=j, stv=stv: e.dma_start(out=stv[:, :, 128:256], in_=gsrc[:, :, DFF + j * 128:DFF + (j + 1) * 128]), r=["w_gu"], w=[("stgA", s)], key="k_stg%d" % s)
        P.pool(lambda e, bj=bj, stv=stv: e.tensor_copy(out=wgu[bj][:, :, :], in_=stv), r=[("stgA", s)], w=[("wgu", bj)])
        stw = stgB[s][:, :]
        P.dma("sp", lambda e, j=j, stw=stw: e.dma_start(out=stw, in_=wd_src[:, j, :]), r=["w_dn"], w=[("stgB", s)], key="k_stgb%d" % s)
        P.pool(lambda e, j=j, stw=stw: e.tensor_copy(out=wd_b[:, j, :], in_=stw), r=[("stgB", s)], w=["wd_b"])
        for tt in range(4):
            n = (j * 4 + tt) % 2
            pg, pu = 2 * n, 2 * n + 1
            for k in range(8):
                P.pe(lambda e, k=k, bj=bj, tt=tt, pg=pg: e.matmul(ps[pg][:, :], lhsT=wgu[bj][:, k, 0:128], rhs=XmT[:, k, tt * 512:(tt + 1) * 512], start=(k == 0), stop=(k == 7)),
                     r=[("wgu", bj), "XmT"], w=[("ps", pg)])
            for k in range(8):
                P.pe(lambda e, k=k, bj=bj, tt=tt, pu=pu: e.matmul(ps[pu][:, :], lhsT=wgu[bj][:, k, 128:256], rhs=XmT[:, k, tt * 512:(tt + 1) * 512], start=(k == 0), stop=(k == 7)),
                     r=[("wgu", bj), "XmT"], w=[("ps", pu)])
            P.act(lambda e, n=n, pg=pg: e.activation(out=C.sg[n][:, :], in_=ps[pg][:, :], func=AF.Silu), r=[("ps", pg)], w=[("sg", n)])
            P.dve(lambda e, n=n, pu=pu, j=j, tt=tt: e.tensor_tensor(out=HT[:, j, tt * 512:(tt + 1) * 512], in0=C.sg[n][:, :], in1=ps[pu][:, :], op=ALU.mult),
                  r=[("sg", n), ("ps", pu)], w=["HT"])
    P.barrier()

    xres2 = [at(0 + 4 * i, [128, 1024], F32, "xres2") for i in range(2)]
    y2 = [at(8 + 4 * i, [128, 1024], F32, "y2") for i in range(2)]
    o2 = [at(16 + 4 * i, [128, 1024], F32, "o2") for i in range(2)]
    hb2 = [at(24 + 2 * i, [128, 1024], BF16, "hb2") for i in range(2)]
    if L == 0:
        X1T = at(172.5, [128, 8, NT], BF16, "X1T")
    P.dma("sp", lambda e: e.dma_start(out=ln2g[:, :], in_=lng[2][:, :]), r=[], w=["lnp"], key="k_c0")
    P.dma("sp", lambda e: e.dma_start(out=ln2b[:, :], in_=lng[3][:, :]), r=[], w=["lnp"], key="k_c0")
    def pre3(i):
        b = i % 2
        P.dma("sp", lambda e, i=i, b=b: e.dma_start(out=xres2[b][:, :], in_=T.hmid[i * 128:(i + 1) * 128, :]), r=["hmid"], w=[("xres2", b)], key="k_xr%d" % b)
        for hf in range(2):
            pb = 2 * b + hf
            for j in range(22):
                P.pe(lambda e, j=j, hf=hf, pb=pb, i=i: e.matmul(ps[pb][:, :], lhsT=HT[:, j, i * 128:(i + 1) * 128], rhs=wd_b[:, j, hf * 512:(hf + 1) * 512], start=(j == 0), stop=(j == 21)),
                     r=["HT", "wd_b"], w=[("ps", pb)])

    def y3(i):
        b = i % 2
        for hf in range(2):
            pb = 2 * b + hf
            P.dve(lambda e, hf=hf, pb=pb, b=b: e.scalar_tensor_tensor(out=y2[b][:, hf * 512:(hf + 1) * 512], in0=xres2[b][:, hf * 512:(hf + 1) * 512], scalar=ALPHA, in1=ps[pb][:, :], op0=ALU.mult, op1=ALU.add),
                  r=[("xres2", b), ("ps", pb)], w=[("y2", b)])

    def post3(i):
        b = i % 2
        _layernorm(P, C, y2[b], o2[b], ln2g, ln2b, st[b], mv[b], ("ln", b), ("y2", b), ("o2", b))
        P.dma("sp", lambda e, i=i, b=b: e.dma_start(out=out_dst[i * 128:(i + 1) * 128, :], in_=o2[b][:, :]), r=[("o2", b)], w=["out_dst"], key="k_o2%d" % b)
        if L == 0:
            P.act(lambda e, b=b: e.copy(out=hb2[b][:, :], in_=o2[b][:, :]), r=[("o2", b)], w=[("hb2", b)])
            pT = ps[4 + b][:, :].bitcast(BF16)
            for j in range(8):
                P.pe(lambda e, j=j, b=b, pT=pT: e.transpose(pT[:, j * 128:(j + 1) * 128], hb2[b][:, j * 128:(j + 1) * 128], ident[:, :]),
                     r=[("hb2", b), "ident"], w=[("ps", 4 + b)])
            P.act(lambda e, i=i, b=b, pT=pT: e.copy(out=X1T[:, :, i * 128:(i + 1) * 128], in_=pT.rearrange("p (k t) -> p k t", t=128)),
                  r=[("ps", 4 + b)], w=["X1T"])
    def ship3(c):
        for k in range(8):
            P.dma("sp", lambda e, k=k, c=c: e.dma_start(out=T.x1T_in[c][k * 128:(k + 1) * 128, :], in_=X1T[:, k, c * 512:(c + 1) * 512]), r=["X1T"], w=[("x1T_in", c)], key="k_x1st%d" % c)
        P.add("pool", lambda e, c=c: e.collective_compute("AllGather", ALU.bypass, replica_groups=[[0, 1, 2, 3], [4, 5, 6, 7]],
                                                           ins=[T.x1T_in[c].ap()], outs=[T.x1T_all[c].ap()]),
              reads=[("x1T_in", c)], writes=["x1T_all"], kind="cc", key="k_cc")

    for i in range(16):
        pre3(i)
        if i >= 1:
            post3(i - 1)
            if L == 0 and (i - 1) % 4 == 3:
                ship3((i - 1) // 4)
        y3(i)
    post3(15)
    if L == 0:
        ship3(3)
    P.barrier()


NLAYERS = 2


def build(nlayers=NLAYERS, phases=None):
    nc = bass.Bass("TRN2", target_bir_lowering=False)
    specs = dict(x=([NT, DM], F32), w_in0=([DM, 704], F32), g_q=([128, 384], F32), g_kv=([128, 256], F32),
                 cs_tok=([128, 16, 64], F32), cc_f=([64, SEQ], F32), ss_f=([64, SEQ], F32), w_uq_c=([384, 512], F32),
                 w_ukv_c=([256, 512], F32), w_o0=([DM, DM], F32), w_o1=([DM, DM], F32), ident_in=([128, 128], BF16), identf_in=([128, 128], F32),
                 w1_c=([DM, 768], F32), lamv=([128, 4, 64], F32), g_sub=([128, 128], F32), bias1=([128, 2, 67], F32),
                 F1=([128, 2, 128], F32), qaug=([2, 2, 512], BF16))
    for l in range(2):
        for nm in ("ln_mix_g", "ln_mix_b", "ln_ffn_g", "ln_ffn_b"):
            specs["%s%d" % (nm, l)] = ([128, DM], F32)
        specs["w_gu%d" % l] = ([DM, 2 * DFF], F32)
        specs["w_dn%d" % l] = ([DFF, DM], F32)

    class Lazy:
        def __init__(self):
            self.__dict__["names"] = []

        def __getattr__(self, name):
            if name in specs:
                t = nc.dram_tensor(name, list(specs[name][0]), specs[name][1], kind="ExternalInput")
                self.__dict__[name] = t
                self.names.append(name)
                return t
            if name in ("ln_mix_g", "ln_mix_b", "ln_ffn_g", "ln_ffn_b", "w_gu", "w_dn"):
                outer = self

                class Idx:
                    def __getitem__(self, l):
                        return getattr(outer, "%s%d" % (name, l))
                return Idx()
            raise AttributeError(name)

    T = Lazy()
    T.out = nc.dram_tensor("out", [NT, DM], F32, kind="ExternalOutput")
    T.latT_in = [nc.dram_tensor("latT_in%d" % k, [704, 512], BF16) for k in range(4)]
    T.latT_all = [nc.dram_tensor("latT_all%d" % k, [4 * 704, 512], BF16) for k in range(4)]
    T.OT_in = nc.dram_tensor("OT_in", [4, 256, NT], BF16)
    T.OT_all = nc.dram_tensor("OT_all", [4, 1024, NT], BF16)
    T.hmid = nc.dram_tensor("hmid", [NT, DM], F32)
    if nlayers == 1:
        T.h1 = T.out
    else:
        T.h1 = nc.dram_tensor("h1", [NT, DM], F32)
    T.x1T_in = [nc.dram_tensor("x1T_in%d" % k, [DM, 512], BF16) for k in range(4)]
    T.x1T_all = [nc.dram_tensor("x1T_all%d" % k, [4 * DM, 512], BF16) for k in range(4)]

    C = Ctx()
    C.cn = 0
    C.ps = [nc.alloc_psum_tensor("ps%d" % i, [128, 512], F32) for i in range(8)]
    C.ident = nc.alloc_sbuf_tensor_at("ident", [128, 128], BF16, offset=SB_TOP + 768)
    C.ident_f = nc.alloc_sbuf_tensor_at("ident_f", [128, 128], F32, offset=SB_TOP + 256)
    C.ones_f = nc.alloc_sbuf_tensor_at("ones_f", [128, 16], F32, offset=SB_TOP + 128)
    P = Prog(nc)
    _ = (T.ident_in, T.x, T.identf_in)
    P.dma("sp", lambda e: e.dma_start(out=C.ident[:, :], in_=T.ident_in[:, :]), r=[], w=["ident"], key="k_id")
    P.dma("sp", lambda e: e.dma_start(out=C.ident_f[:, :], in_=T.identf_in[:, :]), r=[], w=["ident"], key="k_id")
    P.pool(lambda e: e.memset(C.ones_f[:, :], 1.0), r=[], w=["ones_f"])
    if phases is None:
        phases = ["a0", "b0", "c0"] + (["b1", "c1"] if nlayers == 2 else [])
    if "a0" in phases:
        phase_a0(nc, P, C, T)
    if "b0" in phases:
        phase_b0(nc, P, C, T)
    if "c0" in phases:
        phase_c(nc, P, C, T, 0)
    if "b1" in phases:
        phase_b1(nc, P, C, T)
    if "c1" in phases:
        phase_c(nc, P, C, T, 1)
    if ("c1" if nlayers == 2 else "c0") not in phases:
        P.dma("sp", lambda e: e.dma_start(out=T.out[0:128, :], in_=T.x[0:128, :]), r=[], w=["out_dst"], key="k_c0")
    n = P.emit()
    nc.input_names = T.names
    return nc, n


def _host_inputs(inputs):
    f32 = np.float32
    x = np.ascontiguousarray(inputs["x"], dtype=f32).reshape(BATCH * SEQ, DM)
    rep = lambda v, n=128: np.ascontiguousarray(np.broadcast_to(np.asarray(v, dtype=f32).reshape(1, -1), (n, np.asarray(v).size)))
    inv_freq = (np.float32(10000.0) ** (-np.arange(0, 64, 2, dtype=f32) / np.float32(64))).astype(f32)
    ang = (np.arange(SEQ, dtype=f32)[:, None] * inv_freq[None, :]).astype(f32)
    cos, sin = np.cos(ang).astype(f32), np.sin(ang).astype(f32)
    cc_f = np.ascontiguousarray(np.concatenate([cos.T, cos.T], 0))
    ss_f = np.ascontiguousarray(np.concatenate([-sin.T, sin.T], 0))
    w_uq = np.asarray(inputs["mla_w_uq"][0], dtype=f32)
    w_ukv = np.asarray(inputs["mla_w_ukv"][0], dtype=f32)
    w_in1 = np.asarray(inputs["diff_w_in"][0], dtype=f32)
    ident = np.eye(128, dtype=f32).astype(ml_dtypes.bfloat16)
    lamv = np.stack([rep(inputs["diff_lam_q1"][0]), rep(inputs["diff_lam_k1"][0]), rep(inputs["diff_lam_q2"][0]), rep(inputs["diff_lam_k2"][0])], 1)
    common = dict(
        w_in0=np.ascontiguousarray(inputs["mla_w_in"][0], dtype=f32),
        g_q=rep(inputs["mla_g_q"][0]), g_kv=rep(inputs["mla_g_kv"][0]),
        cc_f=cc_f, ss_f=ss_f,
        w_o0=np.ascontiguousarray(inputs["mla_w_o"][0], dtype=f32),
        w_o1=np.ascontiguousarray(inputs["diff_w_o"][0], dtype=f32),
        ident_in=ident, identf_in=np.eye(128, dtype=f32), lamv=np.ascontiguousarray(lamv), g_sub=rep(inputs["diff_g_sub"][0]),
    )
    for l in range(2):
        common["ln_mix_g%d" % l] = rep(inputs["ln_mix_g"][l])
        common["ln_mix_b%d" % l] = rep(inputs["ln_mix_b"][l])
        common["ln_ffn_g%d" % l] = rep(inputs["ln_ffn_g"][l])
        common["ln_ffn_b%d" % l] = rep(inputs["ln_ffn_b"][l])
        common["w_gu%d" % l] = np.ascontiguousarray(inputs["ffn_w_gu"][l], dtype=f32)
        common["w_dn%d" % l] = np.ascontiguousarray(inputs["ffn_w_down"][l], dtype=f32)
    kk = np.arange(128, dtype=np.float64)
    maps = []
    for c in range(NCORES):
        g = c % 4
        m = dict(common)
        m["x"] = np.ascontiguousarray(x[c * NT:(c + 1) * NT])
        pos = g * NT + np.arange(NT)
        cs = np.concatenate([cos[pos], sin[pos]], 1)
        m["cs_tok"] = np.ascontiguousarray(cs.reshape(16, 128, 64).transpose(1, 0, 2))
        cols_q, cols_kv_k, cols_kv_v, cq1, ck1, cv1 = [], [], [], [], [], []
        for hl in range(2):
            h = 2 * g + hl
            b0 = h * 192
            cols_q += list(range(b0, b0 + 128)) + list(range(b0 + 128, b0 + 192)) + list(range(b0 + 160, b0 + 192)) + list(range(b0 + 128, b0 + 160))
            cols_kv_k += list(range(h * 256, h * 256 + 128))
            cols_kv_v += list(range(h * 256 + 128, h * 256 + 256))
            cq1 += list(range(h * 128, (h + 1) * 128))
            ck1 += list(range(1024 + h * 128, 1024 + (h + 1) * 128))
            cv1 += list(range(2048 + h * 128, 2048 + (h + 1) * 128))
        m["w_uq_c"] = np.ascontiguousarray(w_uq[:, cols_q])
        m["w_ukv_c"] = np.ascontiguousarray(w_ukv[:, cols_kv_k + cols_kv_v])
        m["w1_c"] = np.ascontiguousarray(w_in1[:, cq1 + ck1 + cv1])
        bias1 = np.zeros((128, 2, 67), f32)
        F1 = np.zeros((128, 2, 128), f32)
        qaug = np.zeros((2, 2, 512), f32)
        qq = np.arange(512)
        for hl in range(2):
            slope = 2.0 ** (-(2 * g + hl + 1))
            for idx in range(67):
                bias1[:, hl, idx] = slope * (kk - 128.0 * (idx - 3))
            K_, Q_ = np.meshgrid(np.arange(128), np.arange(128), indexing="ij")
            same = (K_ // 64) == (Q_ // 64)
            Fm = np.where(K_ // 64 > Q_ // 64, 0.0, np.where(same & (K_ > Q_), np.exp(-2.0 * slope * (K_ - Q_)), 1.0))
            F1[:, hl, :] = Fm
            qaug[0, hl] = -8.0 * slope * (qq % 256)
            qaug[1, hl] = -8.0 * slope * 256.0 * (qq // 256)
        m["bias1"], m["F1"], m["qaug"] = bias1, F1, qaug.astype(ml_dtypes.bfloat16)
        maps.append(m)
    return maps


_CACHE = {}


def kernel(**inputs):
    maps = _host_inputs(inputs)
    if "nc" not in _CACHE:
        _CACHE["nc"] = build(NLAYERS)[0]
    nc = _CACHE["nc"]
    res = run_bass_kernel_spmd(nc, maps, core_ids=list(range(NCORES)))
    out = np.concatenate([np.asarray(res.results[c]["out"], dtype=np.float32) for c in range(NCORES)], 0)
    return out.reshape(BATCH, SEQ, DM)
```

```python
import numpy as np
import ml_dtypes
import concourse.bass as bass
import concourse.mybir as mybir
from concourse.bass_utils import run_bass_kernel_spmd

F32 = mybir.dt.float32
BF16 = mybir.dt.bfloat16
ALU = mybir.AluOpType
AF = mybir.ActivationFunctionType

NCORES = 8


class Prog:
    ENGS = ("pe", "act", "dve", "pool", "sp")

    def __init__(self, nc):
        self.nc = nc
        self.ops = []
        self.last_write = {}
        self.readers = {}
        self._n = 0

    def add(self, eng, fn, reads=(), writes=(), kind="c", key=None):
        idx = len(self.ops)
        raw, war = set(), set()
        for t in reads:
            w = self.last_write.get(t)
            if w is not None:
                raw.add(w)
        for t in writes:
            w = self.last_write.get(t)
            if w is not None:
                war.add(w)
            r = self.readers.get(t)
            if r:
                war.update(r[0].values())
                war.update(r[1])
        for t in writes:
            self.last_write[t] = idx
            self.readers[t] = ({}, [])
        for t in reads:
            r = self.readers.setdefault(t, ({}, []))
            if kind == "c":
                r[0][eng] = idx
            else:
                r[1].append(idx)
        raw.discard(idx)
        war.discard(idx)
        if kind != "c" and key is None:
            key = "dma_%s" % eng
        self.ops.append(dict(eng=eng, fn=fn, raw=raw, war=war - raw, kind=kind, key=key, sig=False))
        return idx

    def pe(self, fn, r=(), w=()):
        return self.add("pe", fn, r, w)

    def act(self, fn, r=(), w=()):
        return self.add("act", fn, r, w)

    def dve(self, fn, r=(), w=()):
        return self.add("dve", fn, r, w)

    def pool(self, fn, r=(), w=()):
        return self.add("pool", fn, r, w)

    def dma(self, eng, fn, r=(), w=(), key=None):
        return self.add(eng, fn, r, w, kind="d", key=key)

    def _needed(self, o, d, is_raw):
        if d["kind"] != "c":
            return True
        if o["kind"] != "c":
            return True
        if d["eng"] != o["eng"]:
            return True
        if o["eng"] == "pe":
            return False
        return is_raw

    def emit(self):
        nc = self.nc
        ops = self.ops
        for o in ops:
            dl = []
            for di in o["raw"]:
                if self._needed(o, ops[di], True):
                    dl.append(di)
            for di in o["war"]:
                if self._needed(o, ops[di], False):
                    dl.append(di)
            o["dl"] = dl
            for di in dl:
                ops[di]["sig"] = True
        cnt = {}
        for o in ops:
            if o["kind"] == "bar":
                continue
            if o["kind"] == "c":
                if o["sig"]:
                    k = "eng_" + o["eng"]
                    cnt[k] = cnt.get(k, 0) + 1
                    o["sv"] = (k, cnt[k])
            else:
                k = o["key"]
                inc = 16 if o["kind"] == "d" else 1
                cnt[k] = cnt.get(k, 0) + inc
                o["sv"] = (k, cnt[k])
                o["inc"] = inc
        sems = {k: nc.alloc_semaphore("s_" + k) for k in cnt}
        self.sem_final = cnt
        per_eng = {e: [] for e in self.ENGS}
        for o in ops:
            per_eng[o["eng"]].append(o)
        handles = dict(pe="tensor", act="scalar", dve="vector", pool="gpsimd", sp="sync")

        def run_engine(ename):
            def body(eng):
                known = {}
                if ename == "sp":
                    self.pid = eng.partition_id()
                for o in per_eng[ename]:
                    need = {}
                    for di in o["dl"]:
                        k, v = ops[di]["sv"]
                        if need.get(k, 0) < v:
                            need[k] = v
                    for k, v in need.items():
                        if known.get(k, 0) < v:
                            eng.wait_ge(sems[k], v)
                            known[k] = v
                    if o["kind"] == "bar":
                        continue
                    ins = o["fn"](eng)
                    if o["kind"] == "c":
                        if o["sig"]:
                            ins.then_inc(sems[o["sv"][0]], 1)
                    else:
                        ins.then_inc(sems[o["sv"][0]], o["inc"])
                if ename in self.final_wait_engs:
                    for k, v in cnt.items():
                        if known.get(k, 0) < v:
                            eng.wait_ge(sems[k], v)
            return body

        self.final_wait_engs = ("sp",)
        with nc.Block() as block:
            for ename in self.ENGS:
                if per_eng[ename] or ename in self.final_wait_engs:
                    getattr(block, handles[ename])(run_engine(ename))
        return len(ops)

    def barrier(self):
        last = {}
        dmas = []
        for i, o in enumerate(self.ops):
            if o["kind"] == "c":
                last[o["eng"]] = i
            elif o["kind"] in ("d", "cc") and i >= getattr(self, "_bar_from", 0):
                dmas.append(i)
        deps = set(last.values()) | set(dmas)
        self._bar_from = len(self.ops)
        for e in self.ENGS:
            self.ops.append(dict(eng=e, fn=None, raw=set(deps), war=set(), kind="bar", key=None, sig=False))
        self.last_write = {}
        self.readers = {}


SEQ, BATCH, DM = 8192, 2, 1024
NT = 2048
DFF = 2816
ALPHA = 4.0 ** 0.25
LN_EPS = 1e-5
RMS_EPS = 1e-6
SC0 = 192.0 ** -0.5
SC1 = 0.125
LAMBDA_INIT = 0.8 - 0.6 * float(np.exp(-0.3))
SB_BASE, SB_TOP = 16512, 229344 - 1024


class SBA:
    def __init__(self, nc):
        self.nc, self.off, self.n = nc, SB_BASE, 0

    def alloc(self, shape, dtype, name="t"):
        sz = int(np.prod(shape[1:])) * (2 if dtype == BF16 else 4)
        sz = (sz + 63) // 64 * 64
        assert self.off + sz <= SB_TOP, ("SBUF overflow", name, self.off, sz)
        t = self.nc.alloc_sbuf_tensor_at("%s_%d" % (name, self.n), list(shape), dtype, offset=self.off)
        self.off += sz
        self.n += 1
        return t


class Ctx:
    pass


def _transposes(P, ps_bf, src, n, ident, rtok, wtok, width=128):
    for j in range(n):
        P.pe(lambda e, j=j: e.transpose(ps_bf[0:width, j * 128:(j + 1) * 128], src[:, j * width:(j + 1) * width] if width == 128 else src, ident[:, :]),
             r=[rtok, "ident"], w=[wtok])


def load_weight(P, C, dst, src, K, F, wtok, srctok, cast_eng="pool", q="sp"):
    kmax = max(1, C.stg_elems // F)
    k0 = 0
    while k0 < K:
        ks = min(kmax, K - k0)
        s = C.stg_i % len(C.stg)
        C.stg_i += 1
        st = C.stg[s]
        stv = st[:, 0:ks * F].rearrange("p (k f) -> p k f", f=F)
        P.dma(q, lambda e, stv=stv, k0=k0, ks=ks: e.dma_start(out=stv, in_=src[:, k0:k0 + ks, :]),
              r=[srctok], w=[("stg", s)], key="k_stg%d" % s)
        P.add(cast_eng, lambda e, stv=stv, k0=k0, ks=ks: e.tensor_copy(out=dst[:, k0:k0 + ks, :], in_=stv),
              reads=[("stg", s)], writes=[wtok])
        k0 += ks


def phase_a0(nc, P, C, T):
    _ = (T.x, T.g_q, T.g_kv, T.cs_tok, T.w_in0)
    sb = SBA(nc)
    ps = C.ps
    ident = C.ident
    w_in_b = sb.alloc([128, 8, 704], BF16, "w_in_b")
    gq = sb.alloc([128, 384], F32, "gq")
    gkv = sb.alloc([128, 256], F32, "gkv")
    cs = sb.alloc([128, 16, 64], F32, "cs")
    latT = sb.alloc([128, 6, NT], BF16, "latT")
    C.stg = [sb.alloc([128, 4096], F32, "stg") for _ in range(2)]
    C.stg_elems, C.stg_i = 4096, 0
    xf = [sb.alloc([128, 1024], F32, "xf") for _ in range(2)]
    xb = [sb.alloc([128, 1024], BF16, "xb") for _ in range(2)]
    xT = [sb.alloc([128, 8, 128], BF16, "xT") for _ in range(2)]
    lat = [sb.alloc([128, 704], BF16, "lat") for _ in range(2)]
    kr = [sb.alloc([128, 64], F32, "kr") for _ in range(2)]
    tmp = [sb.alloc([128, 4, 32], F32, "tmp") for _ in range(2)]
    junk = sb.alloc([128, 384], F32, "junk")
    st = [sb.alloc([128, 8], F32, "st") for _ in range(2)]

    P.dma("sp", lambda e: e.dma_start(out=gq[:, :], in_=T.g_q[:, :]), r=[], w=["constA"], key="k_c0")
    P.dma("sp", lambda e: e.dma_start(out=gkv[:, :], in_=T.g_kv[:, :]), r=[], w=["constA"], key="k_c0")
    P.dma("sp", lambda e: e.dma_start(out=cs[:, :, :], in_=T.cs_tok[:, :, :]), r=[], w=["constA"], key="k_c0")
    load_weight(P, C, w_in_b, T.w_in0.ap().rearrange("(k p) f -> p k f", p=128), 8, 704, "w_in_b", "w_in0")

    def pre(i):
        b = i % 2
        P.dma("sp", lambda e, i=i, b=b: e.dma_start(out=xf[b][:, :], in_=T.x[i * 128:(i + 1) * 128, :]),
              r=[], w=[("xf", b)], key="k_xf%d" % b)
        P.act(lambda e, b=b: e.copy(out=xb[b][:, :], in_=xf[b][:, :]), r=[("xf", b)], w=[("xb", b)])
        pT = ps[4 + b][:, :].bitcast(BF16)
        for j in range(8):
            P.pe(lambda e, j=j, b=b, pT=pT: e.transpose(pT[:, j * 128:(j + 1) * 128], xb[b][:, j * 128:(j + 1) * 128], ident[:, :]),
                 r=[("xb", b), "ident"], w=[("ps", 4 + b)])
        P.dve(lambda e, b=b, pT=pT: e.tensor_copy(out=xT[b][:, :, :], in_=pT.rearrange("p (k t) -> p k t", t=128)),
              r=[("ps", 4 + b)], w=[("xT", b)])
        ph1, ph2 = ps[b], ps[2 + b]
        for k in range(8):
            P.pe(lambda e, k=k, b=b, ph1=ph1: e.matmul(ph1[:, 0:384], lhsT=xT[b][:, k, :], rhs=w_in_b[:, k, 0:384], start=(k == 0), stop=(k == 7)),
                 r=[("xT", b), "w_in_b"], w=[("ps", b)])
        for k in range(8):
            P.pe(lambda e, k=k, b=b, ph2=ph2: e.matmul(ph2[:, 0:320], lhsT=xT[b][:, k, :], rhs=w_in_b[:, k, 384:704], start=(k == 0), stop=(k == 7)),
                 r=[("xT", b), "w_in_b"], w=[("ps", 2 + b)])

    def post(i):
        b = i % 2
        ph1, ph2 = ps[b], ps[2 + b]
        P.act(lambda e, b=b, ph1=ph1: e.activation(out=junk[:, 0:384], in_=ph1[:, 0:384], func=AF.Square, accum_out=st[b][:, 0:1]),
              r=[("ps", b)], w=["junk", ("st", b)])
        P.act(lambda e, b=b, ph2=ph2: e.activation(out=junk[:, 0:256], in_=ph2[:, 0:256], func=AF.Square, accum_out=st[b][:, 1:2]),
              r=[("ps", 2 + b)], w=["junk", ("st", b)])
        P.act(lambda e, b=b, ph2=ph2: e.copy(out=kr[b][:, :], in_=ph2[:, 256:320]), r=[("ps", 2 + b)], w=[("kr", b)])
        P.dve(lambda e, b=b: e.tensor_scalar(out=st[b][:, 2:3], in0=st[b][:, 0:1], scalar1=1.0 / 384, scalar2=RMS_EPS, op0=ALU.mult, op1=ALU.add),
              r=[("st", b)], w=[("st2", b)])
        P.dve(lambda e, b=b: e.tensor_scalar(out=st[b][:, 3:4], in0=st[b][:, 1:2], scalar1=1.0 / 256, scalar2=RMS_EPS, op0=ALU.mult, op1=ALU.add),
              r=[("st", b), ("st2", b)], w=[("st2", b)])
        P.act(lambda e, b=b: e.activation(out=st[b][:, 4:6], in_=st[b][:, 2:4], func=AF.Sqrt), r=[("st2", b)], w=[("st3", b)])
        P.dve(lambda e, b=b: e.reciprocal(out=st[b][:, 6:8], in_=st[b][:, 4:6]), r=[("st3", b)], w=[("st4", b)])
        P.dve(lambda e, b=b, ph1=ph1: e.scalar_tensor_tensor(out=lat[b][:, 0:384], in0=ph1[:, 0:384], scalar=st[b][:, 6:7], in1=gq[:, :], op0=ALU.mult, op1=ALU.mult),
              r=[("ps", b), ("st4", b), "constA"], w=[("lat", b)])
        P.dve(lambda e, b=b, ph2=ph2: e.scalar_tensor_tensor(out=lat[b][:, 384:640], in0=ph2[:, 0:256], scalar=st[b][:, 7:8], in1=gkv[:, :], op0=ALU.mult, op1=ALU.mult),
              r=[("ps", 2 + b), ("st4", b), "constA"], w=[("lat", b)])
        cosv, sinv = cs[:, i, 0:32], cs[:, i, 32:64]
        P.dve(lambda e, b=b, cosv=cosv: e.tensor_tensor(out=tmp[b][:, 0, :], in0=kr[b][:, 0:32], in1=cosv, op=ALU.mult), r=[("kr", b), "constA"], w=[("tmp", b)])
        P.dve(lambda e, b=b, sinv=sinv: e.tensor_tensor(out=tmp[b][:, 1, :], in0=kr[b][:, 32:64], in1=sinv, op=ALU.mult), r=[("kr", b), "constA", ("tmp", b)], w=[("tmp", b)])
        P.dve(lambda e, b=b, sinv=sinv: e.tensor_tensor(out=tmp[b][:, 2, :], in0=kr[b][:, 0:32], in1=sinv, op=ALU.mult), r=[("kr", b), "constA", ("tmp", b)], w=[("tmp", b)])
        P.dve(lambda e, b=b, cosv=cosv: e.tensor_tensor(out=tmp[b][:, 3, :], in0=kr[b][:, 32:64], in1=cosv, op=ALU.mult), r=[("kr", b), "constA", ("tmp", b)], w=[("tmp", b)])
        P.dve(lambda e, b=b: e.tensor_tensor(out=lat[b][:, 640:672], in0=tmp[b][:, 0, :], in1=tmp[b][:, 1, :], op=ALU.subtract), r=[("tmp", b)], w=[("lat", b)])
        P.dve(lambda e, b=b: e.tensor_tensor(out=lat[b][:, 672:704], in0=tmp[b][:, 2, :], in1=tmp[b][:, 3, :], op=ALU.add), r=[("tmp", b), ("lat", b)], w=[("lat", b)])
        pT2 = ps[6 + b][:, :].bitcast(BF16)
        for j in range(5):
            P.pe(lambda e, j=j, b=b, pT2=pT2: e.transpose(pT2[:, j * 128:(j + 1) * 128], lat[b][:, j * 128:(j + 1) * 128], ident[:, :]),
                 r=[("lat", b), "ident"], w=[("ps", 6 + b)])
        P.pe(lambda e, b=b, pT2=pT2: e.transpose(pT2[0:64, 640:768], lat[b][:, 640:704], ident[:, :]),
             r=[("lat", b), "ident"], w=[("ps", 6 + b)])
        P.act(lambda e, i=i, b=b, pT2=pT2: e.copy(out=latT[:, 0:5, i * 128:(i + 1) * 128], in_=pT2[:, 0:640].rearrange("p (k t) -> p k t", t=128)),
              r=[("ps", 6 + b)], w=["latT"])
        P.act(lambda e, i=i, b=b, pT2=pT2: e.copy(out=latT[0:64, 5, i * 128:(i + 1) * 128], in_=pT2[0:64, 640:768]),
              r=[("ps", 6 + b)], w=["latT"])
    def ship(c):
        for j in range(5):
            P.dma("sp", lambda e, j=j, c=c: e.dma_start(out=T.latT_in[c][j * 128:(j + 1) * 128, :], in_=latT[:, j, c * 512:(c + 1) * 512]), r=["latT"], w=[("latT_in", c)], key="k_lst%d" % c)
        P.dma("sp", lambda e, c=c: e.dma_start(out=T.latT_in[c][640:704, :], in_=latT[0:64, 5, c * 512:(c + 1) * 512]), r=["latT"], w=[("latT_in", c)], key="k_lst%d" % c)
        P.add("pool", lambda e, c=c: e.collective_compute("AllGather", ALU.bypass, replica_groups=[[0, 1, 2, 3], [4, 5, 6, 7]],
                                                           ins=[T.latT_in[c].ap()], outs=[T.latT_all[c].ap()]),
              reads=[("latT_in", c)], writes=["latT_all"], kind="cc", key="k_cc")

    for i in range(16):
        pre(i)
        if i >= 1:
            post(i - 1)
            if (i - 1) % 4 == 3:
                ship((i - 1) // 4)
    post(15)
    ship(3)
    P.barrier()


def phase_b0(nc, P, C, T):
    _ = (T.cc_f, T.ss_f, T.w_uq_c, T.w_ukv_c)
    sb = SBA(nc)
    ps = C.ps
    ident = C.ident
    wq = sb.alloc([128, 3, 512], BF16, "wq")
    wkv = sb.alloc([128, 2, 512], BF16, "wkv")
    KnT = [sb.alloc([128, SEQ], BF16, "KnT") for _ in range(2)]
    KrT = sb.alloc([128, SEQ], BF16, "KrT")
    Vaug = sb.alloc([128, 2, 64, 129], BF16, "Vaug")
    C.stg = [sb.alloc([128, 2048], F32, "stg") for _ in range(2)]
    C.stg_elems, C.stg_i = 2048, 0
    cqT = [sb.alloc([128, 3, 512], BF16, "cqT") for _ in range(2)]
    ckvT = [sb.alloc([128, 2, 512], BF16, "ckvT") for _ in range(2)]
    ccf = [sb.alloc([64, 512], F32, "ccf") for _ in range(2)]
    ssf = [sb.alloc([64, 512], F32, "ssf") for _ in range(2)]
    QnT = [[sb.alloc([128, 512], BF16, "QnT") for _ in range(2)] for _ in range(2)]
    QrT = [[sb.alloc([128, 512], BF16, "QrT") for _ in range(2)] for _ in range(2)]
    r1 = [sb.alloc([64, 512], F32, "r1") for _ in range(2)]
    r2 = [sb.alloc([64, 512], F32, "r2") for _ in range(2)]
    PT = [sb.alloc([128, 512], BF16, "PT") for _ in range(6)]
    rl = [sb.alloc([128, 4], F32, "rl") for _ in range(2)]
    ob = [sb.alloc([128, 128], BF16, "ob") for _ in range(2)]
    OTs = [sb.alloc([128, 512], BF16, "OTs") for _ in range(2)]
    Pacc = [sb.alloc([128, 512], F32, "Pacc") for _ in range(2)]
    OTf = [sb.alloc([128, 512], F32, "OTf") for _ in range(2)]

    load_weight(P, C, wq, T.w_uq_c.ap().rearrange("(k p) f -> p k f", p=128), 3, 512, "wq", "w_uq_c")
    load_weight(P, C, wkv, T.w_ukv_c.ap().rearrange("(k p) f -> p k f", p=128), 2, 512, "wkv", "w_ukv_c")
    P.pool(lambda e: e.memset(Vaug[:, :, :, 128:129], 1.0), r=[], w=["Vones"])
    P.pool(lambda e: e.memset(KrT[64:128, :], 0.0), r=[], w=["Kzero"])
    for bt_ in range(2):
        for hl_ in range(2):
            P.pool(lambda e, bt_=bt_, hl_=hl_: e.memset(QrT[bt_][hl_][64:128, :], 0.0), r=[], w=[("Qzero", bt_, hl_)])
    def lat_rows(r, f0, n):
        k = f0 // 256
        rows_k = (256, 256, 192)[k]
        return T.latT_all[k], r * rows_k + f0 % 256

    for tt_ in range(16):
        P.dma("sp", lambda e, tt_=tt_: e.dma_start(out=KrT[0:64, tt_ * 512:(tt_ + 1) * 512], in_=T.latT_all[tt_ % 4][(tt_ // 4) * 704 + 640:(tt_ // 4) * 704 + 704, :]),
              r=["latT_all"], w=["KrT"], key="k_krt")

    pj = [0]

    def pbank():
        b = 5 + (pj[0] % 3)
        pj[0] += 1
        return b

    sidx = [0]
    for t in range(16):
        bt = t % 2
        r, c0 = t // 4, (t % 4) * 512
        lt = T.latT_all[t % 4]
        P.dma("sp", lambda e, lt=lt, r=r, bt=bt: e.dma_start(out=cqT[bt][:, :, :], in_=lt[r * 704:r * 704 + 384, :].rearrange("(j p) c -> p j c", p=128)),
              r=["latT_all"], w=[("cqT", bt)], key="k_cq%d" % bt)
        P.dma("sp", lambda e, lt=lt, r=r, bt=bt: e.dma_start(out=ckvT[bt][:, :, :], in_=lt[r * 704 + 384:r * 704 + 640, :].rearrange("(j p) c -> p j c", p=128)),
              r=["latT_all"], w=[("ckvT", bt)], key="k_ckv%d" % bt)
        P.dma("sp", lambda e, t=t, bt=bt: e.dma_start(out=ccf[bt][:, :], in_=T.cc_f[:, t * 512:(t + 1) * 512]), r=[], w=[("ccf", bt)], key="k_ccf%d" % bt)
        P.dma("sp", lambda e, t=t, bt=bt: e.dma_start(out=ssf[bt][:, :], in_=T.ss_f[:, t * 512:(t + 1) * 512]), r=[], w=[("ssf", bt)], key="k_ssf%d" % bt)
        for s in range(4):
            pb = pbank()
            for j in range(2):
                P.pe(lambda e, s=s, j=j, pb=pb, bt=bt: e.matmul(ps[pb][:, 0:256], lhsT=ckvT[bt][:, j, s * 128:(s + 1) * 128], rhs=wkv[:, j, 256:512], start=(j == 0), stop=(j == 1)),
                     r=[("ckvT", bt), "wkv"], w=[("ps", pb)])
            P.act(lambda e, s=s, pb=pb, t=t: e.copy(out=Vaug[:, :, 4 * t + s, 0:128], in_=ps[pb][:, 0:256].rearrange("p (h d) -> p h d", d=128)),
                  r=[("ps", pb)], w=[("V", 4 * t + s)])
        for hl in range(2):
            pb = pbank()
            for j in range(2):
                P.pe(lambda e, j=j, pb=pb, bt=bt, hl=hl: e.matmul(ps[pb][:, :], lhsT=wkv[:, j, hl * 128:(hl + 1) * 128], rhs=ckvT[bt][:, j, :], start=(j == 0), stop=(j == 1)),
                     r=[("ckvT", bt), "wkv"], w=[("ps", pb)])
            P.dve(lambda e, pb=pb, hl=hl, t=t: e.tensor_copy(out=KnT[hl][:, t * 512:(t + 1) * 512], in_=ps[pb][:, :]),
                  r=[("ps", pb)], w=[("KnT", hl, t)])
            pb = pbank()
            for j in range(3):
                P.pe(lambda e, j=j, pb=pb, bt=bt, hl=hl: e.matmul(ps[pb][:, :], lhsT=wq[:, j, hl * 256:hl * 256 + 128], rhs=cqT[bt][:, j, :], start=(j == 0), stop=(j == 2)),
                     r=[("cqT", bt), "wq"], w=[("ps", pb)])
            P.act(lambda e, pb=pb, hl=hl, bt=bt: e.copy(out=QnT[bt][hl][:, :], in_=ps[pb][:, :]), r=[("ps", pb)], w=[("QnT", bt, hl)])
            pa = pbank()
            for j in range(3):
                P.pe(lambda e, j=j, pa=pa, bt=bt, hl=hl: e.matmul(ps[pa][0:64, :], lhsT=wq[:, j, hl * 256 + 128:hl * 256 + 192], rhs=cqT[bt][:, j, :], start=(j == 0), stop=(j == 2)),
                     r=[("cqT", bt), "wq"], w=[("ps", pa)])
            P.dve(lambda e, pa=pa, bt=bt, hl=hl: e.tensor_tensor(out=r1[hl][:, :], in0=ps[pa][0:64, :], in1=ccf[bt][:, :], op=ALU.mult),
                  r=[("ps", pa), ("ccf", bt)], w=[("r1", hl)])
            pb2 = pbank()
            for j in range(3):
                P.pe(lambda e, j=j, pb2=pb2, bt=bt, hl=hl: e.matmul(ps[pb2][0:64, :], lhsT=wq[:, j, hl * 256 + 192:hl * 256 + 256], rhs=cqT[bt][:, j, :], start=(j == 0), stop=(j == 2)),
                     r=[("cqT", bt), "wq"], w=[("ps", pb2)])
            P.dve(lambda e, pb2=pb2, bt=bt, hl=hl: e.tensor_tensor(out=r2[hl][:, :], in0=ps[pb2][0:64, :], in1=ssf[bt][:, :], op=ALU.mult),
                  r=[("ps", pb2), ("ssf", bt)], w=[("r2", hl)])
            P.pool(lambda e, bt=bt, hl=hl: e.tensor_tensor(out=QrT[bt][hl][0:64, :], in0=r1[hl][:, :], in1=r2[hl][:, :], op=ALU.add),
                   r=[("r1", hl), ("r2", hl)], w=[("QrT", bt, hl)])
        for hl in range(2):
            nJ = 4 * (t + 1)
            tiles = list(range(nJ))
            meta = {}

            def qk(J, hl=hl, t=t, bt=bt):
                n = sidx[0]
                sidx[0] += 1
                sbk, pbf = n % 3, n % 6
                m = max(0, J - 4 * t)
                q0 = 128 * m
                meta[J] = (sbk, pbf, m, q0)
                P.pe(lambda e: e.matmul(ps[sbk][:, q0:512], lhsT=KnT[hl][:, J * 128:(J + 1) * 128], rhs=QnT[bt][hl][:, q0:512], start=True, stop=False),
                     r=[("KnT", hl, J // 4), ("QnT", bt, hl)], w=[("ps", sbk)])
                P.pe(lambda e: e.matmul(ps[sbk][:, q0:512], lhsT=KrT[:, J * 128:(J + 1) * 128], rhs=QrT[bt][hl][:, q0:512], start=False, stop=True),
                     r=["KrT", "Kzero", ("QrT", bt, hl), ("Qzero", bt, hl)], w=[("ps", sbk)])

            def ex(J, t=t):
                sbk, pbf, m, q0 = meta[J]
                P.act(lambda e: e.activation(out=PT[pbf][:, q0:512], in_=ps[sbk][:, q0:512], func=AF.Exp, scale=SC0),
                      r=[("ps", sbk)], w=[("PT", pbf)])
                if J >= 4 * t:
                    P.pool(lambda e: e.memset(PT[pbf][64:128, q0:q0 + 64], 0.0), r=[("PT", pbf)], w=[("PT", pbf)])

            ehl = (2 * t + hl) % 2
            acc = 3 + ehl
            pa = Pacc[ehl]

            def av(J, hl=hl, nJ=nJ, acc=acc, pa=pa, ehl=ehl):
                sbk, pbf, m, q0 = meta[J]
                P.pe(lambda e: e.matmul(ps[acc][:, q0:512], lhsT=Vaug[:, hl, J, 0:128], rhs=PT[pbf][:, q0:512], start=(J == 0), stop=(J == nJ - 1)),
                     r=[("PT", pbf), ("V", J)], w=[("ps", acc)])
                SPL = 320
                if J == 0:
                    P.dve(lambda e: e.tensor_copy(out=pa[:, 0:SPL], in_=PT[pbf][:, 0:SPL]), r=[("PT", pbf)], w=[("PaccD", ehl)])
                    P.pool(lambda e: e.tensor_copy(out=pa[:, SPL:512], in_=PT[pbf][:, SPL:512]), r=[("PT", pbf)], w=[("PaccP", ehl)])
                else:
                    if q0 < SPL:
                        P.dve(lambda e: e.tensor_tensor(out=pa[:, q0:SPL], in0=pa[:, q0:SPL], in1=PT[pbf][:, q0:SPL], op=ALU.add),
                              r=[("PT", pbf), ("PaccD", ehl)], w=[("PaccD", ehl)])
                    q1 = max(q0, SPL)
                    P.pool(lambda e: e.tensor_tensor(out=pa[:, q1:512], in0=pa[:, q1:512], in1=PT[pbf][:, q1:512], op=ALU.add),
                           r=[("PT", pbf), ("PaccP", ehl)], w=[("PaccP", ehl)])

            qk(tiles[0])
            if nJ > 1:
                qk(tiles[1])
            for i, J in enumerate(tiles):
                ex(J)
                if i + 2 < nJ:
                    qk(tiles[i + 2])
                av(J)
            eb = ehl
            P.act(lambda e, eb=eb, acc=acc: e.copy(out=OTf[eb][:, :], in_=ps[acc][:, :]), r=[("ps", acc)], w=[("OTf", eb)])
            pl = pbank()
            for qs in range(4):
                P.pe(lambda e, qs=qs, pl=pl, pa=pa: e.matmul(ps[pl][:, qs:qs + 1], lhsT=pa[:, qs * 128:(qs + 1) * 128], rhs=C.ones_f[:, 0:1], start=True, stop=True, skip_group_check=True),
                     r=[("PaccD", ehl), ("PaccP", ehl), "ident"], w=[("ps", pl)])
            ptr = pbank()
            for qs in range(4):
                P.pe(lambda e, qs=qs, ptr=ptr, eb=eb: e.transpose(ps[ptr][:, qs * 128:(qs + 1) * 128], OTf[eb][:, qs * 128:(qs + 1) * 128], C.ident_f[:, :]),
                     r=[("OTf", eb), "ident"], w=[("ps", ptr)])
            P.dve(lambda e, eb=eb, pl=pl: e.reciprocal(out=rl[eb][:, 0:4], in_=ps[pl][:, 0:4]), r=[("ps", pl)], w=[("rl", eb)])
            pbT = pbank()
            pT = ps[pbT][:, :].bitcast(BF16)
            for qs in range(4):
                ob_i = qs % 2
                P.dve(lambda e, qs=qs, ptr=ptr, eb=eb, ob_i=ob_i: e.tensor_scalar(out=ob[ob_i][:, :], in0=ps[ptr][:, qs * 128:(qs + 1) * 128], scalar1=rl[eb][:, qs:qs + 1], scalar2=None, op0=ALU.mult),
                      r=[("ps", ptr), ("rl", eb)], w=[("ob", ob_i)])
                P.pe(lambda e, qs=qs, ob_i=ob_i, pT=pT: e.transpose(pT[:, qs * 128:(qs + 1) * 128], ob[ob_i][:, :], ident[:, :]),
                     r=[("ob", ob_i), "ident"], w=[("ps", pbT)])
            P.act(lambda e, eb=eb, pT=pT: e.copy(out=OTs[eb][:, :], in_=pT[:, 0:512]), r=[("ps", pbT)], w=[("OTs", eb)])
            P.dma("sp", lambda e, eb=eb, hl=hl, t=t: e.dma_start(out=T.OT_in[t // 4, hl * 128:(hl + 1) * 128, (t % 4) * 512:(t % 4 + 1) * 512], in_=OTs[eb][:, :]),
                  r=[("OTs", eb)], w=[("OT_in", t // 4, eb)], key="k_ot%d_%d" % (t // 4, eb))
            if t % 4 == 3 and hl == 1:
                P.add("pool", lambda e, c=t // 4: e.collective_compute("AllGather", ALU.bypass, replica_groups=[[0, 1, 2, 3], [4, 5, 6, 7]],
                                                               ins=[T.OT_in[c]], outs=[T.OT_all[c]]),
                      reads=[("OT_in", t // 4, 0), ("OT_in", t // 4, 1)], writes=["OT_all"], kind="cc", key="k_cc")
    P.barrier()


def phase_b1(nc, P, C, T):
    _ = (T.w1_c, T.lamv, T.g_sub, T.bias1, T.F1, T.qaug)
    sb = SBA(nc)
    ps = C.ps
    ident = C.ident
    w1 = sb.alloc([128, 8, 768], BF16, "w1")
    KT = [[sb.alloc([128, SEQ], BF16, "KT") for _ in range(2)] for _ in range(2)]
    Vaug = sb.alloc([128, 2, 64, 129], BF16, "Vaug1")
    C.stg = [sb.alloc([128, 2048], F32, "stg") for _ in range(2)]
    C.stg_elems, C.stg_i = 2048, 0
    xT = [sb.alloc([128, 8, 512], BF16, "xT1") for _ in range(2)]
    QT = [[[sb.alloc([128, 512], BF16, "QT") for _ in range(2)] for _ in range(2)] for _ in range(2)]
    PT = [sb.alloc([128, 512], BF16, "PT1") for _ in range(4)]
    bias1 = sb.alloc([128, 2, 67], F32, "bias1")
    F1 = sb.alloc([128, 2, 128], F32, "F1")
    lamv = sb.alloc([128, 4, 64], F32, "lamv")
    gsub = sb.alloc([128, 128], F32, "gsub")
    lm = sb.alloc([128, 8], F32, "lm")
    junk = sb.alloc([128, 128], F32, "junk1")
    rl = [sb.alloc([128, 8], F32, "rl1") for _ in range(2)]
    o1 = [sb.alloc([128, 128], F32, "o1") for _ in range(2)]
    oo = [sb.alloc([128, 128], F32, "oo") for _ in range(2)]
    ob = [sb.alloc([128, 128], BF16, "ob1") for _ in range(2)]
    OTs = [sb.alloc([128, 512], BF16, "OTs1") for _ in range(2)]

    P.dma("sp", lambda e: e.dma_start(out=bias1[:, :, :], in_=T.bias1[:, :, :]), r=[], w=["constB"], key="k_c0")
    P.dma("sp", lambda e: e.dma_start(out=F1[:, :, :], in_=T.F1[:, :, :]), r=[], w=["constB"], key="k_c0")
    P.dma("sp", lambda e: e.dma_start(out=lamv[:, :, :], in_=T.lamv[:, :, :]), r=[], w=["constB"], key="k_c0")
    P.dma("sp", lambda e: e.dma_start(out=gsub[:, :], in_=T.g_sub[:, :]), r=[], w=["constB"], key="k_c0")
    for mp in range(2):
        for bt in range(2):
            for hl in range(2):
                P.dma("sp", lambda e, mp=mp, bt=bt, hl=hl: e.dma_start(out=QT[mp][bt][hl][64:66, :], in_=T.qaug[:, hl, :]), r=[], w=["constB"], key="k_c0")
        for hl in range(2):
            P.pool(lambda e, mp=mp, hl=hl: e.memset(KT[mp][hl][64:66, :], 1.0), r=[], w=[("Kones", mp, hl)])
    P.pool(lambda e: e.memset(Vaug[:, :, :, 128:129], 1.0), r=[], w=["Vones"])
    load_weight(P, C, w1, T.w1_c.ap().rearrange("(k p) f -> p k f", p=128), 8, 768, "w1", "w1_c")
    P.dve(lambda e: e.scalar_tensor_tensor(out=junk[:, 0:64], in0=lamv[:, 0, :], scalar=1.0, in1=lamv[:, 1, :], op0=ALU.mult, op1=ALU.mult, accum_out=lm[:, 0:1]),
          r=["constB"], w=["junk", "lm0"])
    P.dve(lambda e: e.scalar_tensor_tensor(out=junk[:, 0:64], in0=lamv[:, 2, :], scalar=1.0, in1=lamv[:, 3, :], op0=ALU.mult, op1=ALU.mult, accum_out=lm[:, 1:2]),
          r=["constB", "junk", "lm0"], w=["junk", "lm0"])
    P.act(lambda e: e.activation(out=lm[:, 2:4], in_=lm[:, 0:2], func=AF.Exp), r=["lm0"], w=["lm1"])
    P.dve(lambda e: e.tensor_tensor(out=lm[:, 4:5], in0=lm[:, 2:3], in1=lm[:, 3:4], op=ALU.subtract), r=["lm1"], w=["lm2"])
    P.dve(lambda e: e.tensor_scalar(out=lm[:, 5:6], in0=lm[:, 4:5], scalar1=-1.0, scalar2=-LAMBDA_INIT, op0=ALU.mult, op1=ALU.add), r=["lm2"], w=["neglam"])
    P.dve(lambda e: e.tensor_scalar(out=gsub[:, :], in0=gsub[:, :], scalar1=1.0 - LAMBDA_INIT, scalar2=None, op0=ALU.mult), r=["constB"], w=["gsub2"])

    pj = [0]

    def pbank():
        b = 6 + (pj[0] % 2)
        pj[0] += 1
        return b

    sidx = [0]
    for t in range(16):
        bt = t % 2
        r, c0 = t // 4, (t % 4) * 512
        P.dma("sp", lambda e, r=r, t=t, bt=bt: e.dma_start(out=xT[bt][:, :, :], in_=T.x1T_all[t % 4][r * 1024:(r + 1) * 1024, :].rearrange("(k p) c -> p k c", p=128)),
              r=["x1T_all"], w=[("xT", bt)], key="k_lat%d" % bt)
        for s in range(4):
            pb = pbank()
            for k in range(8):
                P.pe(lambda e, s=s, k=k, pb=pb, bt=bt: e.matmul(ps[pb][:, 0:256], lhsT=xT[bt][:, k, s * 128:(s + 1) * 128], rhs=w1[:, k, 512:768], start=(k == 0), stop=(k == 7)),
                     r=[("xT", bt), "w1"], w=[("ps", pb)])
            P.act(lambda e, s=s, pb=pb, t=t: e.copy(out=Vaug[:, :, 4 * t + s, 0:128], in_=ps[pb][:, 0:256].rearrange("p (h d) -> p h d", d=128)),
                  r=[("ps", pb)], w=[("V", 4 * t + s)])
        for hl in range(2):
            pb = pbank()
            for k in range(8):
                P.pe(lambda e, k=k, pb=pb, bt=bt, hl=hl: e.matmul(ps[pb][:, :], lhsT=w1[:, k, 256 + hl * 128:256 + (hl + 1) * 128], rhs=xT[bt][:, k, :], start=(k == 0), stop=(k == 7)),
                     r=[("xT", bt), "w1"], w=[("ps", pb)])
            P.act(lambda e, pb=pb, hl=hl, t=t: e.copy(out=KT[0][hl][0:64, t * 512:(t + 1) * 512], in_=ps[pb][0:64, :]), r=[("ps", pb)], w=[("K", 0, hl, t)])
            P.dve(lambda e, pb=pb, hl=hl, t=t: e.tensor_copy(out=KT[1][hl][0:64, t * 512:(t + 1) * 512], in_=ps[pb][64:128, :]), r=[("ps", pb)], w=[("K", 1, hl, t)])
            pb = pbank()
            for k in range(8):
                P.pe(lambda e, k=k, pb=pb, bt=bt, hl=hl: e.matmul(ps[pb][:, :], lhsT=w1[:, k, hl * 128:(hl + 1) * 128], rhs=xT[bt][:, k, :], start=(k == 0), stop=(k == 7)),
                     r=[("xT", bt), "w1"], w=[("ps", pb)])
            P.act(lambda e, pb=pb, hl=hl, bt=bt: e.copy(out=QT[0][bt][hl][0:64, :], in_=ps[pb][0:64, :]), r=[("ps", pb)], w=[("Q", 0, bt, hl)])
            P.dve(lambda e, pb=pb, hl=hl, bt=bt: e.tensor_copy(out=QT[1][bt][hl][0:64, :], in_=ps[pb][64:128, :]), r=[("ps", pb)], w=[("Q", 1, bt, hl)])
        for hl in range(2):
            nJ = 4 * (t + 1)
            tiles = [(J, mp) for J in range(nJ) for mp in range(2)]
            meta = {}

            def qk(tl, hl=hl, t=t, bt=bt):
                J, mp = tl
                n = sidx[0]
                sidx[0] += 1
                sbk, pbf = n % 3, n % 4
                m = max(0, J - 4 * t)
                q0 = 128 * m
                meta[tl] = (sbk, pbf, m, q0)
                P.pe(lambda e: e.matmul(ps[sbk][:, q0:512], lhsT=KT[mp][hl][0:66, J * 128:(J + 1) * 128], rhs=QT[mp][bt][hl][0:66, q0:512], start=True, stop=True),
                     r=[("K", mp, hl, J // 4), ("Kones", mp, hl), ("Q", mp, bt, hl), "constB"], w=[("ps", sbk)])

            def ex(tl, hl=hl, t=t):
                J, mp = tl
                sbk, pbf, m, q0 = meta[tl]
                idx = (4 * t - J) + 3
                P.act(lambda e: e.activation(out=PT[pbf][:, q0:512], in_=ps[sbk][:, q0:512], func=AF.Exp, bias=bias1[:, hl, idx:idx + 1], scale=SC1),
                      r=[("ps", sbk), "constB"], w=[("PT", pbf)])
                if J >= 4 * t:
                    P.pool(lambda e: e.tensor_tensor(out=PT[pbf][:, q0:q0 + 128], in0=PT[pbf][:, q0:q0 + 128], in1=F1[:, hl, :], op=ALU.mult),
                           r=[("PT", pbf), "constB"], w=[("PT", pbf)])

            def av(tl, hl=hl, nJ=nJ):
                J, mp = tl
                sbk, pbf, m, q0 = meta[tl]
                for qs in range(m, 4):
                    a = mp * 4 + qs
                    bank, off = 3 + a // 3, (a % 3) * 129
                    P.pe(lambda e, qs=qs, bank=bank, off=off, a=a: e.matmul(ps[bank][:, off:off + 129], lhsT=PT[pbf][:, qs * 128:(qs + 1) * 128], rhs=Vaug[:, hl, J, :],
                                                                      start=(J == 0 and a % 3 == 0), stop=(J == nJ - 1), skip_group_check=True),
                         r=[("PT", pbf), ("V", J), "Vones"], w=[("ps", bank)])

            qk(tiles[0])
            qk(tiles[1])
            for i, tl in enumerate(tiles):
                ex(tl)
                if i + 2 < len(tiles):
                    qk(tiles[i + 2])
                av(tl)
            eb = (2 * t + hl) % 2
            pbT = pbank()
            pT = ps[pbT][:, :].bitcast(BF16)
            for qs in range(4):
                a1, a2 = qs, 4 + qs
                b1_, f1_ = 3 + a1 // 3, (a1 % 3) * 129
                b2_, f2_ = 3 + a2 // 3, (a2 % 3) * 129
                q2 = qs % 2
                P.dve(lambda e, b1_=b1_, f1_=f1_, q2=q2: e.reciprocal(out=rl[q2][:, 0:1], in_=ps[b1_][:, f1_ + 128:f1_ + 129]), r=[("ps", b1_)], w=[("rl", q2)])
                P.dve(lambda e, b2_=b2_, f2_=f2_, q2=q2: e.reciprocal(out=rl[q2][:, 1:2], in_=ps[b2_][:, f2_ + 128:f2_ + 129]), r=[("ps", b2_), ("rl", q2)], w=[("rl", q2)])
                P.dve(lambda e, q2=q2: e.tensor_tensor(out=rl[q2][:, 2:3], in0=rl[q2][:, 1:2], in1=lm[:, 5:6], op=ALU.mult), r=[("rl", q2), "neglam"], w=[("rl2", q2)])
                P.dve(lambda e, b1_=b1_, f1_=f1_, q2=q2: e.tensor_scalar(out=o1[q2][:, :], in0=ps[b1_][:, f1_:f1_ + 128], scalar1=rl[q2][:, 0:1], scalar2=None, op0=ALU.mult),
                      r=[("ps", b1_), ("rl", q2)], w=[("o1", q2)])
                P.dve(lambda e, b2_=b2_, f2_=f2_, q2=q2: e.scalar_tensor_tensor(out=oo[q2][:, :], in0=ps[b2_][:, f2_:f2_ + 128], scalar=rl[q2][:, 2:3], in1=o1[q2][:, :], op0=ALU.mult, op1=ALU.add),
                      r=[("ps", b2_), ("rl2", q2), ("o1", q2)], w=[("oo", q2)])
                P.act(lambda e, q2=q2: e.activation(out=junk[:, :], in_=oo[q2][:, :], func=AF.Square, accum_out=rl[q2][:, 3:4]), r=[("oo", q2)], w=["junk", ("rl3", q2)])
                P.dve(lambda e, q2=q2: e.tensor_scalar(out=rl[q2][:, 4:5], in0=rl[q2][:, 3:4], scalar1=1.0 / 128, scalar2=RMS_EPS, op0=ALU.mult, op1=ALU.add), r=[("rl3", q2)], w=[("rl4", q2)])
                P.act(lambda e, q2=q2: e.activation(out=rl[q2][:, 5:6], in_=rl[q2][:, 4:5], func=AF.Sqrt), r=[("rl4", q2)], w=[("rl5", q2)])
                P.dve(lambda e, q2=q2: e.reciprocal(out=rl[q2][:, 6:7], in_=rl[q2][:, 5:6]), r=[("rl5", q2)], w=[("rl6", q2)])
                P.dve(lambda e, q2=q2: e.scalar_tensor_tensor(out=ob[q2][:, :], in0=oo[q2][:, :], scalar=rl[q2][:, 6:7], in1=gsub[:, :], op0=ALU.mult, op1=ALU.mult),
                      r=[("oo", q2), ("rl6", q2), "gsub2"], w=[("ob", q2)])
                P.pe(lambda e, qs=qs, q2=q2, pT=pT: e.transpose(pT[:, qs * 128:(qs + 1) * 128], ob[q2][:, :], ident[:, :]), r=[("ob", q2), "ident"], w=[("ps", pbT)])
            P.act(lambda e, eb=eb, pT=pT: e.copy(out=OTs[eb][:, :], in_=pT[:, 0:512]), r=[("ps", pbT)], w=[("OTs", eb)])
            P.dma("sp", lambda e, eb=eb, hl=hl, t=t: e.dma_start(out=T.OT_in[t // 4, hl * 128:(hl + 1) * 128, (t % 4) * 512:(t % 4 + 1) * 512], in_=OTs[eb][:, :]),
                  r=[("OTs", eb)], w=[("OT_in", t // 4, eb)], key="k_ot%d_%d" % (t // 4, eb))
            if t % 4 == 3 and hl == 1:
                P.add("pool", lambda e, c=t // 4: e.collective_compute("AllGather", ALU.bypass, replica_groups=[[0, 1, 2, 3], [4, 5, 6, 7]],
                                                               ins=[T.OT_in[c]], outs=[T.OT_all[c]]),
                      reads=[("OT_in", t // 4, 0), ("OT_in", t // 4, 1)], writes=["OT_all"], kind="cc", key="k_cc")
    P.barrier()


def _layernorm(P, C, src, dst, g, b, st, mv, tag, srctok, dsttok, eng2="pool"):
    for hf in range(2):
        P.dve(lambda e, hf=hf: e.bn_stats(out=st[:, hf * 6:(hf + 1) * 6], in_=src[:, hf * 512:(hf + 1) * 512]),
              r=[srctok] + ([(tag, "st")] if hf else []), w=[(tag, "st")])
    P.dve(lambda e: e.bn_aggr(out=mv[:, 0:2], in_=st[:, 0:12]), r=[(tag, "st")], w=[(tag, "mv")])
    P.dve(lambda e: e.tensor_scalar(out=mv[:, 2:3], in0=mv[:, 1:2], scalar1=LN_EPS, scalar2=None, op0=ALU.add), r=[(tag, "mv")], w=[(tag, "mv2")])
    P.act(lambda e: e.activation(out=mv[:, 3:4], in_=mv[:, 2:3], func=AF.Sqrt), r=[(tag, "mv2")], w=[(tag, "mv3")])
    P.dve(lambda e: e.reciprocal(out=mv[:, 4:5], in_=mv[:, 3:4]), r=[(tag, "mv3")], w=[(tag, "mv4")])
    P.dve(lambda e: e.tensor_scalar(out=dst[:, :], in0=src[:, :], scalar1=mv[:, 0:1], scalar2=mv[:, 4:5], op0=ALU.subtract, op1=ALU.mult),
          r=[srctok, (tag, "mv"), (tag, "mv4")], w=[dsttok])
    P.add(eng2, lambda e: e.tensor_tensor(out=dst[:, :], in0=dst[:, :], in1=g[:, :], op=ALU.mult), reads=[dsttok, "lnp"], writes=[dsttok])
    P.add(eng2, lambda e: e.tensor_tensor(out=dst[:, :], in0=dst[:, :], in1=b[:, :], op=ALU.add), reads=[dsttok, "lnp"], writes=[dsttok])


def phase_c(nc, P, C, T, L):
    ps = C.ps
    ident = C.ident
    KB = 1024
    base = SB_BASE

    def at(off_kb, shape, dtype, name):
        C.cn += 1
        return nc.alloc_sbuf_tensor_at("%s_c%d" % (name, C.cn), list(shape), dtype, offset=base + int(off_kb * KB))

    res_src = T.x if L == 0 else T.h1
    out_dst = T.h1 if L == 0 else T.out
    w_o = T.w_o0 if L == 0 else T.w_o1
    lng = [T.ln_mix_g[L], T.ln_mix_b[L], T.ln_ffn_g[L], T.ln_ffn_b[L]]
    w_gu, w_dn = T.w_gu[L], T.w_dn[L]

    XmT = at(0, [128, 8, NT], BF16, "XmT")
    HT = at(32, [128, 22, NT], BF16, "HT")
    wd_b = at(120, [128, 22, 1024], BF16, "wd_b")
    ln2g = at(164, [128, 1024], F32, "ln2g")
    ln2b = at(168, [128, 1024], F32, "ln2b")
    small = at(172, [128, 64], F32, "small")
    OTm = at(32, [128, 8, NT], BF16, "OTm")
    wo_b = at(64, [128, 8, 1024], BF16, "wo_b")
    ln1g = at(80, [128, 1024], F32, "ln1g")
    ln1b = at(84, [128, 1024], F32, "ln1b")
    xres = [at(88 + 4 * i, [128, 1024], F32, "xres") for i in range(2)]
    y = [at(96 + 4 * i, [128, 1024], F32, "y") for i in range(2)]
    hm = [at(104 + 4 * i, [128, 1024], F32, "hm") for i in range(2)]
    hb = [at(112 + 2 * i, [128, 1024], BF16, "hb") for i in range(2)]
    C.stg = [at(120 + 16 * i, [128, 4096], F32, "stg") for i in range(2)]
    C.stg_elems, C.stg_i = 4096, 0
    st = [small[:, 0:12], small[:, 16:28]]
    mv = [small[:, 32:40], small[:, 40:48]]

    def dyn_load(e, h):
        r = P.pid % 4
        return e.dma_start(out=OTm[:, h, :], in_=T.OT_all[bass.ds(r, 1), h * 128:(h + 1) * 128, :].rearrange("o p c -> (o p) c"))

    for h in range(8):
        P.dma("sp", lambda e, h=h: dyn_load(e, h), r=["OT_all"], w=["OTm"], key="k_otm")
    P.dma("sp", lambda e: e.dma_start(out=ln1g[:, :], in_=lng[0][:, :]), r=[], w=["lnp"], key="k_c0")
    P.dma("sp", lambda e: e.dma_start(out=ln1b[:, :], in_=lng[1][:, :]), r=[], w=["lnp"], key="k_c0")
    load_weight(P, C, wo_b, w_o.ap().rearrange("(k p) f -> p k f", p=128), 8, 1024, "wo_b", "w_o", q="act")

    def pre1(i):
        b = i % 2
        P.dma("sp", lambda e, i=i, b=b: e.dma_start(out=xres[b][:, :], in_=res_src[i * 128:(i + 1) * 128, :]), r=["res_src"], w=[("xres", b)], key="k_xr%d" % b)
        for hf in range(2):
            pb = 2 * b + hf
            for h in range(8):
                P.pe(lambda e, h=h, hf=hf, pb=pb, i=i: e.matmul(ps[pb][:, :], lhsT=OTm[:, h, i * 128:(i + 1) * 128], rhs=wo_b[:, h, hf * 512:(hf + 1) * 512], start=(h == 0), stop=(h == 7)),
                     r=["OTm", "wo_b"], w=[("ps", pb)])

    def y1(i):
        b = i % 2
        for hf in range(2):
            pb = 2 * b + hf
            P.dve(lambda e, hf=hf, pb=pb, b=b: e.scalar_tensor_tensor(out=y[b][:, hf * 512:(hf + 1) * 512], in0=xres[b][:, hf * 512:(hf + 1) * 512], scalar=ALPHA, in1=ps[pb][:, :], op0=ALU.mult, op1=ALU.add),
                  r=[("xres", b), ("ps", pb)], w=[("y", b)])

    def post1(i):
        b = i % 2
        _layernorm(P, C, y[b], hm[b], ln1g, ln1b, st[b], mv[b], ("ln", b), ("y", b), ("hm", b))
        P.act(lambda e, b=b: e.copy(out=hb[b][:, :], in_=hm[b][:, :]), r=[("hm", b)], w=[("hb", b)])
        pT = ps[4 + b][:, :].bitcast(BF16)
        for j in range(8):
            P.pe(lambda e, j=j, b=b, pT=pT: e.transpose(pT[:, j * 128:(j + 1) * 128], hb[b][:, j * 128:(j + 1) * 128], ident[:, :]),
                 r=[("hb", b), "ident"], w=[("ps", 4 + b)])
        P.act(lambda e, i=i, b=b, pT=pT: e.copy(out=XmT[:, :, i * 128:(i + 1) * 128], in_=pT.rearrange("p (k t) -> p k t", t=128)),
              r=[("ps", 4 + b)], w=["XmT"])
        P.dma("sp", lambda e, i=i, b=b: e.dma_start(out=T.hmid[i * 128:(i + 1) * 128, :], in_=hm[b][:, :]), r=[("hm", b)], w=["hmid"], key="k_hm%d" % b)

    for i in range(16):
        pre1(i)
        if i >= 1:
            post1(i - 1)
        y1(i)
    post1(15)
    P.barrier()

    wgu = [at(172.5 + 4 * i, [128, 8, 256], BF16, "wgu") for i in range(2)]
    stgA = [at(180.5 + 8 * i, [128, 2048], F32, "stgA") for i in range(2)]
    stgB = [at(196.5 + 4 * i, [128, 1024], F32, "stgB") for i in range(2)]
    C.sg = [at(164 + 2 * i, [128, 512], F32, "sg") for i in range(2)]
    gsrc = w_gu.ap().rearrange("(k p) f -> p k f", p=128)
    wd_src = w_dn.ap().rearrange("(j p) f -> p j f", p=128)
    for j in range(22):
        bj = j % 2
        s = j % 2
        stv = stgA[s][:, :].rearrange("p (k f) -> p k f", f=256)
        P.dma("sp", lambda e, j=j, stv=stv: e.dma_start(out=stv[:, :, 0:128], in_=gsrc[:, :, j * 128:(j + 1) * 128]), r=["w_gu"], w=[("stgA", s)], key="k_stg%d" % s)
        P.dma("sp", lambda e, j=j, stv=stv: e.dma_start(out=stv[:, :, 128:256], in_=gsrc[:, :, DFF + j * 128:DFF + (j + 1) * 128]), r=["w_gu"], w=[("stgA", s)], key="k_stg%d" % s)
        P.pool(lambda e, bj=bj, stv=stv: e.tensor_copy(out=wgu[bj][:, :, :], in_=stv), r=[("stgA", s)], w=[("wgu", bj)])
        stw = stgB[s][:, :]
        P.dma("sp", lambda e, j=j, stw=stw: e.dma_start(out=stw, in_=wd_src[:, j, :]), r=["w_dn"], w=[("stgB", s)], key="k_stgb%d" % s)
        P.pool(lambda e, j=j, stw=stw: e.tensor_copy(out=wd_b[:, j, :], in_=stw), r=[("stgB", s)], w=["wd_b"])
        for tt in range(4):
            n = (j * 4 + tt) % 2
            pg, pu = 2 * n, 2 * n + 1
            for k in range(8):
                P.pe(lambda e, k=k, bj=bj, tt=tt, pg=pg: e.matmul(ps[pg][:, :], lhsT=wgu[bj][:, k, 0:128], rhs=XmT[:, k, tt * 512:(tt + 1) * 512], start=(k == 0), stop=(k == 7)),
                     r=[("wgu", bj), "XmT"], w=[("ps", pg)])
            for k in range(8):
                P.pe(lambda e, k=k, bj=bj, tt=tt, pu=pu: e.matmul(ps[pu][:, :], lhsT=wgu[bj][:, k, 128:256], rhs=XmT[:, k, tt * 512:(tt + 1) * 512], start=(k == 0), stop=(k == 7)),
                     r=[("wgu", bj), "XmT"], w=[("ps", pu)])
            P.act(lambda e, n=n, pg=pg: e.activation(out=C.sg[n][:, :], in_=ps[pg][:, :], func=AF.Silu), r=[("ps", pg)], w=[("sg", n)])
            P.dve(lambda e, n=n, pu=pu, j=j, tt=tt: e.tensor_tensor(out=HT[:, j, tt * 512:(tt + 1) * 512], in0=C.sg[n][:, :], in1=ps[pu][:, :], op=ALU.mult),
                  r=[("sg", n), ("ps", pu)], w=["HT"])
    P.barrier()

    xres2 = [at(0 + 4 * i, [128, 1024], F32, "xres2") for i in range(2)]
    y2 = [at(8 + 4 * i, [128, 1024], F32, "y2") for i in range(2)]
    o2 = [at(16 + 4 * i, [128, 1024], F32, "o2") for i in range(2)]
    hb2 = [at(24 + 2 * i, [128, 1024], BF16, "hb2") for i in range(2)]
    if L == 0:
        X1T = at(172.5, [128, 8, NT], BF16, "X1T")
    P.dma("sp", lambda e: e.dma_start(out=ln2g[:, :], in_=lng[2][:, :]), r=[], w=["lnp"], key="k_c0")
    P.dma("sp", lambda e: e.dma_start(out=ln2b[:, :], in_=lng[3][:, :]), r=[], w=["lnp"], key="k_c0")
    def pre3(i):
        b = i % 2
        P.dma("sp", lambda e, i=i, b=b: e.dma_start(out=xres2[b][:, :], in_=T.hmid[i * 128:(i + 1) * 128, :]), r=["hmid"], w=[("xres2", b)], key="k_xr%d" % b)
        for hf in range(2):
            pb = 2 * b + hf
            for j in range(22):
                P.pe(lambda e, j=j, hf=hf, pb=pb, i=i: e.matmul(ps[pb][:, :], lhsT=HT[:, j, i * 128:(i + 1) * 128], rhs=wd_b[:, j, hf * 512:(hf + 1) * 512], start=(j == 0), stop=(j == 21)),
                     r=["HT", "wd_b"], w=[("ps", pb)])

    def y3(i):
        b = i % 2
        for hf in range(2):
            pb = 2 * b + hf
            P.dve(lambda e, hf=hf, pb=pb, b=b: e.scalar_tensor_tensor(out=y2[b][:, hf * 512:(hf + 1) * 512], in0=xres2[b][:, hf * 512:(hf + 1) * 512], scalar=ALPHA, in1=ps[pb][:, :], op0=ALU.mult, op1=ALU.add),
                  r=[("xres2", b), ("ps", pb)], w=[("y2", b)])

    def post3(i):
        b = i % 2
        _layernorm(P, C, y2[b], o2[b], ln2g, ln2b, st[b], mv[b], ("ln", b), ("y2", b), ("o2", b), eng2=("dve" if L == 0 else "pool"))
        P.dma("sp", lambda e, i=i, b=b: e.dma_start(out=out_dst[i * 128:(i + 1) * 128, :], in_=o2[b][:, :]), r=[("o2", b)], w=["out_dst"], key="k_o2%d" % b)
        if L == 0:
            P.act(lambda e, b=b: e.copy(out=hb2[b][:, :], in_=o2[b][:, :]), r=[("o2", b)], w=[("hb2", b)])
            pT = ps[4 + b][:, :].bitcast(BF16)
            for j in range(8):
                P.pe(lambda e, j=j, b=b, pT=pT: e.transpose(pT[:, j * 128:(j + 1) * 128], hb2[b][:, j * 128:(j + 1) * 128], ident[:, :]),
                     r=[("hb2", b), "ident"], w=[("ps", 4 + b)])
            P.act(lambda e, i=i, b=b, pT=pT: e.copy(out=X1T[:, :, i * 128:(i + 1) * 128], in_=pT.rearrange("p (k t) -> p k t", t=128)),
                  r=[("ps", 4 + b)], w=["X1T"])
    def ship3(c):
        for k in range(8):
            P.dma("sp", lambda e, k=k, c=c: e.dma_start(out=T.x1T_in[c][k * 128:(k + 1) * 128, :], in_=X1T[:, k, c * 512:(c + 1) * 512]), r=["X1T"], w=[("x1T_in", c)], key="k_x1st%d" % c)
        P.add("pool", lambda e, c=c: e.collective_compute("AllGather", ALU.bypass, replica_groups=[[0, 1, 2, 3], [4, 5, 6, 7]],
                                                           ins=[T.x1T_in[c].ap()], outs=[T.x1T_all[c].ap()]),
              reads=[("x1T_in", c)], writes=["x1T_all"], kind="cc", key="k_cc")

    for i in range(16):
        pre3(i)
        if i >= 1:
            post3(i - 1)
            if L == 0 and (i - 1) % 4 == 3:
                ship3((i - 1) // 4)
        y3(i)
    post3(15)
    if L == 0:
        ship3(3)
    P.barrier()


NLAYERS = 2


def build(nlayers=NLAYERS, phases=None):
    nc = bass.Bass("TRN2", target_bir_lowering=False)
    specs = dict(x=([NT, DM], F32), w_in0=([DM, 704], F32), g_q=([128, 384], F32), g_kv=([128, 256], F32),
                 cs_tok=([128, 16, 64], F32), cc_f=([64, SEQ], F32), ss_f=([64, SEQ], F32), w_uq_c=([384, 512], F32),
                 w_ukv_c=([256, 512], F32), w_o0=([DM, DM], F32), w_o1=([DM, DM], F32), ident_in=([128, 128], BF16), identf_in=([128, 128], F32),
                 w1_c=([DM, 768], F32), lamv=([128, 4, 64], F32), g_sub=([128, 128], F32), bias1=([128, 2, 67], F32),
                 F1=([128, 2, 128], F32), qaug=([2, 2, 512], BF16))
    for l in range(2):
        for nm in ("ln_mix_g", "ln_mix_b", "ln_ffn_g", "ln_ffn_b"):
            specs["%s%d" % (nm, l)] = ([128, DM], F32)
        specs["w_gu%d" % l] = ([DM, 2 * DFF], F32)
        specs["w_dn%d" % l] = ([DFF, DM], F32)

    class Lazy:
        def __init__(self):
            self.__dict__["names"] = []

        def __getattr__(self, name):
            if name in specs:
                t = nc.dram_tensor(name, list(specs[name][0]), specs[name][1], kind="ExternalInput")
                self.__dict__[name] = t
                self.names.append(name)
                return t
            if name in ("ln_mix_g", "ln_mix_b", "ln_ffn_g", "ln_ffn_b", "w_gu", "w_dn"):
                outer = self

                class Idx:
                    def __getitem__(self, l):
                        return getattr(outer, "%s%d" % (name, l))
                return Idx()
            raise AttributeError(name)

    T = Lazy()
    T.out = nc.dram_tensor("out", [NT, DM], F32, kind="ExternalOutput")
    T.latT_in = [nc.dram_tensor("latT_in%d" % k, [704, 512], BF16) for k in range(4)]
    T.latT_all = [nc.dram_tensor("latT_all%d" % k, [4 * 704, 512], BF16) for k in range(4)]
    T.OT_in = nc.dram_tensor("OT_in", [4, 256, NT], BF16)
    T.OT_all = nc.dram_tensor("OT_all", [4, 1024, NT], BF16)
    T.hmid = nc.dram_tensor("hmid", [NT, DM], F32)
    if nlayers == 1:
        T.h1 = T.out
    else:
        T.h1 = nc.dram_tensor("h1", [NT, DM], F32)
    T.x1T_in = [nc.dram_tensor("x1T_in%d" % k, [DM, 512], BF16) for k in range(4)]
    T.x1T_all = [nc.dram_tensor("x1T_all%d" % k, [4 * DM, 512], BF16) for k in range(4)]

    C = Ctx()
    C.cn = 0
    C.ps = [nc.alloc_psum_tensor("ps%d" % i, [128, 512], F32) for i in range(8)]
    C.ident = nc.alloc_sbuf_tensor_at("ident", [128, 128], BF16, offset=SB_TOP + 768)
    C.ident_f = nc.alloc_sbuf_tensor_at("ident_f", [128, 128], F32, offset=SB_TOP + 256)
    C.ones_f = nc.alloc_sbuf_tensor_at("ones_f", [128, 16], F32, offset=SB_TOP + 128)
    P = Prog(nc)
    _ = (T.ident_in, T.x, T.identf_in)
    P.dma("sp", lambda e: e.dma_start(out=C.ident[:, :], in_=T.ident_in[:, :]), r=[], w=["ident"], key="k_id")
    P.dma("sp", lambda e: e.dma_start(out=C.ident_f[:, :], in_=T.identf_in[:, :]), r=[], w=["ident"], key="k_id")
    P.pool(lambda e: e.memset(C.ones_f[:, :], 1.0), r=[], w=["ones_f"])
    if phases is None:
        phases = ["a0", "b0", "c0"] + (["b1", "c1"] if nlayers == 2 else [])
    if "a0" in phases:
        phase_a0(nc, P, C, T)
    if "b0" in phases:
        phase_b0(nc, P, C, T)
    if "c0" in phases:
        phase_c(nc, P, C, T, 0)
    if "b1" in phases:
        phase_b1(nc, P, C, T)
    if "c1" in phases:
        phase_c(nc, P, C, T, 1)
    if ("c1" if nlayers == 2 else "c0") not in phases:
        P.dma("sp", lambda e: e.dma_start(out=T.out[0:128, :], in_=T.x[0:128, :]), r=[], w=["out_dst"], key="k_c0")
    n = P.emit()
    nc.input_names = T.names
    return nc, n


def _host_inputs(inputs):
    f32 = np.float32
    x = np.ascontiguousarray(inputs["x"], dtype=f32).reshape(BATCH * SEQ, DM)
    rep = lambda v, n=128: np.ascontiguousarray(np.broadcast_to(np.asarray(v, dtype=f32).reshape(1, -1), (n, np.asarray(v).size)))
    inv_freq = (np.float32(10000.0) ** (-np.arange(0, 64, 2, dtype=f32) / np.float32(64))).astype(f32)
    ang = (np.arange(SEQ, dtype=f32)[:, None] * inv_freq[None, :]).astype(f32)
    cos, sin = np.cos(ang).astype(f32), np.sin(ang).astype(f32)
    cc_f = np.ascontiguousarray(np.concatenate([cos.T, cos.T], 0))
    ss_f = np.ascontiguousarray(np.concatenate([-sin.T, sin.T], 0))
    w_uq = np.asarray(inputs["mla_w_uq"][0], dtype=f32)
    w_ukv = np.asarray(inputs["mla_w_ukv"][0], dtype=f32)
    w_in1 = np.asarray(inputs["diff_w_in"][0], dtype=f32)
    ident = np.eye(128, dtype=f32).astype(ml_dtypes.bfloat16)
    lamv = np.stack([rep(inputs["diff_lam_q1"][0]), rep(inputs["diff_lam_k1"][0]), rep(inputs["diff_lam_q2"][0]), rep(inputs["diff_lam_k2"][0])], 1)
    common = dict(
        w_in0=np.ascontiguousarray(inputs["mla_w_in"][0], dtype=f32),
        g_q=rep(inputs["mla_g_q"][0]), g_kv=rep(inputs["mla_g_kv"][0]),
        cc_f=cc_f, ss_f=ss_f,
        w_o0=np.ascontiguousarray(inputs["mla_w_o"][0], dtype=f32),
        w_o1=np.ascontiguousarray(inputs["diff_w_o"][0], dtype=f32),
        ident_in=ident, identf_in=np.eye(128, dtype=f32), lamv=np.ascontiguousarray(lamv), g_sub=rep(inputs["diff_g_sub"][0]),
    )
    for l in range(2):
        common["ln_mix_g%d" % l] = rep(inputs["ln_mix_g"][l])
        common["ln_mix_b%d" % l] = rep(inputs["ln_mix_b"][l])
        common["ln_ffn_g%d" % l] = rep(inputs["ln_ffn_g"][l])
        common["ln_ffn_b%d" % l] = rep(inputs["ln_ffn_b"][l])
        common["w_gu%d" % l] = np.ascontiguousarray(inputs["ffn_w_gu"][l], dtype=f32)
        common["w_dn%d" % l] = np.ascontiguousarray(inputs["ffn_w_down"][l], dtype=f32)
    kk = np.arange(128, dtype=np.float64)
    maps = []
    for c in range(NCORES):
        g = c % 4
        m = dict(common)
        m["x"] = np.ascontiguousarray(x[c * NT:(c + 1) * NT])
        pos = g * NT + np.arange(NT)
        cs = np.concatenate([cos[pos], sin[pos]], 1)
        m["cs_tok"] = np.ascontiguousarray(cs.reshape(16, 128, 64).transpose(1, 0, 2))
        cols_q, cols_kv_k, cols_kv_v, cq1, ck1, cv1 = [], [], [], [], [], []
        for hl in range(2):
            h = 2 * g + hl
            b0 = h * 192
            cols_q += list(range(b0, b0 + 128)) + list(range(b0 + 128, b0 + 192)) + list(range(b0 + 160, b0 + 192)) + list(range(b0 + 128, b0 + 160))
            cols_kv_k += list(range(h * 256, h * 256 + 128))
            cols_kv_v += list(range(h * 256 + 128, h * 256 + 256))
            cq1 += list(range(h * 128, (h + 1) * 128))
            ck1 += list(range(1024 + h * 128, 1024 + (h + 1) * 128))
            cv1 += list(range(2048 + h * 128, 2048 + (h + 1) * 128))
        m["w_uq_c"] = np.ascontiguousarray(w_uq[:, cols_q])
        m["w_ukv_c"] = np.ascontiguousarray(w_ukv[:, cols_kv_k + cols_kv_v])
        m["w1_c"] = np.ascontiguousarray(w_in1[:, cq1 + ck1 + cv1])
        bias1 = np.zeros((128, 2, 67), f32)
        F1 = np.zeros((128, 2, 128), f32)
        qaug = np.zeros((2, 2, 512), f32)
        qq = np.arange(512)
        for hl in range(2):
            slope = 2.0 ** (-(2 * g + hl + 1))
            for idx in range(67):
                bias1[:, hl, idx] = slope * (kk - 128.0 * (idx - 3))
            K_, Q_ = np.meshgrid(np.arange(128), np.arange(128), indexing="ij")
            same = (K_ // 64) == (Q_ // 64)
            Fm = np.where(K_ // 64 > Q_ // 64, 0.0, np.where(same & (K_ > Q_), np.exp(-2.0 * slope * (K_ - Q_)), 1.0))
            F1[:, hl, :] = Fm
            qaug[0, hl] = -8.0 * slope * (qq % 256)
            qaug[1, hl] = -8.0 * slope * 256.0 * (qq // 256)
        m["bias1"], m["F1"], m["qaug"] = bias1, F1, qaug.astype(ml_dtypes.bfloat16)
        maps.append(m)
    return maps


_CACHE = {}


def kernel(**inputs):
    maps = _host_inputs(inputs)
    if "nc" not in _CACHE:
        _CACHE["nc"] = build(NLAYERS)[0]
    nc = _CACHE["nc"]
    res = run_bass_kernel_spmd(nc, maps, core_ids=list(range(NCORES)))
    out = np.concatenate([np.asarray(res.results[c]["out"], dtype=np.float32) for c in range(NCORES)], 0)
    return out.reshape(BATCH, SEQ, DM)
```

```python
import numpy as np
import ml_dtypes
import concourse.bass as bass
import concourse.mybir as mybir
from concourse.bass_utils import run_bass_kernel_spmd

F32 = mybir.dt.float32
BF16 = mybir.dt.bfloat16
ALU = mybir.AluOpType
AF = mybir.ActivationFunctionType

NCORES = 8


class Prog:
    ENGS = ("pe", "act", "dve", "pool", "sp")

    def __init__(self, nc):
        self.nc = nc
        self.ops = []
        self.last_write = {}
        self.readers = {}
        self._n = 0

    def add(self, eng, fn, reads=(), writes=(), kind="c", key=None):
        idx = len(self.ops)
        raw, war = set(), set()
        for t in reads:
            w = self.last_write.get(t)
            if w is not None:
                raw.add(w)
        for t in writes:
            w = self.last_write.get(t)
            if w is not None:
                war.add(w)
            r = self.readers.get(t)
            if r:
                war.update(r[0].values())
                war.update(r[1])
        for t in writes:
            self.last_write[t] = idx
            self.readers[t] = ({}, [])
        for t in reads:
            r = self.readers.setdefault(t, ({}, []))
            if kind == "c":
                r[0][eng] = idx
            else:
                r[1].append(idx)
        raw.discard(idx)
        war.discard(idx)
        if kind != "c" and key is None:
            key = "dma_%s" % eng
        self.ops.append(dict(eng=eng, fn=fn, raw=raw, war=war - raw, kind=kind, key=key, sig=False))
        return idx

    def pe(self, fn, r=(), w=()):
        return self.add("pe", fn, r, w)

    def act(self, fn, r=(), w=()):
        return self.add("act", fn, r, w)

    def dve(self, fn, r=(), w=()):
        return self.add("dve", fn, r, w)

    def pool(self, fn, r=(), w=()):
        return self.add("pool", fn, r, w)

    def dma(self, eng, fn, r=(), w=(), key=None):
        return self.add(eng, fn, r, w, kind="d", key=key)

    def _needed(self, o, d, is_raw):
        if d["kind"] != "c":
            return True
        if o["kind"] != "c":
            return True
        if d["eng"] != o["eng"]:
            return True
        if o["eng"] == "pe":
            return False
        return is_raw

    def emit(self):
        nc = self.nc
        ops = self.ops
        for o in ops:
            dl = []
            for di in o["raw"]:
                if self._needed(o, ops[di], True):
                    dl.append(di)
            for di in o["war"]:
                if self._needed(o, ops[di], False):
                    dl.append(di)
            o["dl"] = dl
            for di in dl:
                ops[di]["sig"] = True
        cnt = {}
        for o in ops:
            if o["kind"] == "bar":
                continue
            if o["kind"] == "c":
                if o["sig"]:
                    k = "eng_" + o["eng"]
                    cnt[k] = cnt.get(k, 0) + 1
                    o["sv"] = (k, cnt[k])
            else:
                k = o["key"]
                inc = 16 if o["kind"] == "d" else 1
                cnt[k] = cnt.get(k, 0) + inc
                o["sv"] = (k, cnt[k])
                o["inc"] = inc
        sems = {k: nc.alloc_semaphore("s_" + k) for k in cnt}
        self.sem_final = cnt
        per_eng = {e: [] for e in self.ENGS}
        for o in ops:
            per_eng[o["eng"]].append(o)
        handles = dict(pe="tensor", act="scalar", dve="vector", pool="gpsimd", sp="sync")

        def run_engine(ename):
            def body(eng):
                known = {}
                if ename == "sp":
                    self.pid = eng.partition_id()
                for o in per_eng[ename]:
                    need = {}
                    for di in o["dl"]:
                        k, v = ops[di]["sv"]
                        if need.get(k, 0) < v:
                            need[k] = v
                    for k, v in need.items():
                        if known.get(k, 0) < v:
                            eng.wait_ge(sems[k], v)
                            known[k] = v
                    if o["kind"] == "bar":
                        continue
                    ins = o["fn"](eng)
                    if o["kind"] == "c":
                        if o["sig"]:
                            ins.then_inc(sems[o["sv"][0]], 1)
                    else:
                        ins.then_inc(sems[o["sv"][0]], o["inc"])
                if ename in self.final_wait_engs:
                    for k, v in cnt.items():
                        if known.get(k, 0) < v:
                            eng.wait_ge(sems[k], v)
            return body

        self.final_wait_engs = ("sp",)
        with nc.Block() as block:
            for ename in self.ENGS:
                if per_eng[ename] or ename in self.final_wait_engs:
                    getattr(block, handles[ename])(run_engine(ename))
        return len(ops)

    def barrier(self):
        last = {}
        dmas = []
        for i, o in enumerate(self.ops):
            if o["kind"] == "c":
                last[o["eng"]] = i
            elif o["kind"] in ("d", "cc") and i >= getattr(self, "_bar_from", 0):
                dmas.append(i)
        deps = set(last.values()) | set(dmas)
        self._bar_from = len(self.ops)
        for e in self.ENGS:
            self.ops.append(dict(eng=e, fn=None, raw=set(deps), war=set(), kind="bar", key=None, sig=False))
        self.last_write = {}
        self.readers = {}


SEQ, BATCH, DM = 8192, 2, 1024
NT = 2048
DFF = 2816
ALPHA = 4.0 ** 0.25
LN_EPS = 1e-5
RMS_EPS = 1e-6
SC0 = 192.0 ** -0.5
SC1 = 0.125
LAMBDA_INIT = 0.8 - 0.6 * float(np.exp(-0.3))
SB_BASE, SB_TOP = 16512, 229344 - 1024


class SBA:
    def __init__(self, nc):
        self.nc, self.off, self.n = nc, SB_BASE, 0

    def alloc(self, shape, dtype, name="t"):
        sz = int(np.prod(shape[1:])) * (2 if dtype == BF16 else 4)
        sz = (sz + 63) // 64 * 64
        assert self.off + sz <= SB_TOP, ("SBUF overflow", name, self.off, sz)
        t = self.nc.alloc_sbuf_tensor_at("%s_%d" % (name, self.n), list(shape), dtype, offset=self.off)
        self.off += sz
        self.n += 1
        return t


class Ctx:
    pass


def _transposes(P, ps_bf, src, n, ident, rtok, wtok, width=128):
    for j in range(n):
        P.pe(lambda e, j=j: e.transpose(ps_bf[0:width, j * 128:(j + 1) * 128], src[:, j * width:(j + 1) * width] if width == 128 else src, ident[:, :]),
             r=[rtok, "ident"], w=[wtok])


def load_weight(P, C, dst, src, K, F, wtok, srctok, cast_eng="pool", q="sp"):
    kmax = max(1, C.stg_elems // F)
    k0 = 0
    while k0 < K:
        ks = min(kmax, K - k0)
        s = C.stg_i % len(C.stg)
        C.stg_i += 1
        st = C.stg[s]
        stv = st[:, 0:ks * F].rearrange("p (k f) -> p k f", f=F)
        P.dma(q, lambda e, stv=stv, k0=k0, ks=ks: e.dma_start(out=stv, in_=src[:, k0:k0 + ks, :]),
              r=[srctok], w=[("stg", s)], key="k_stg%d" % s)
        P.add(cast_eng, lambda e, stv=stv, k0=k0, ks=ks: e.tensor_copy(out=dst[:, k0:k0 + ks, :], in_=stv),
              reads=[("stg", s)], writes=[wtok])
        k0 += ks


def phase_a0(nc, P, C, T):
    _ = (T.x, T.g_q, T.g_kv, T.cs_tok, T.w_in0)
    sb = SBA(nc)
    ps = C.ps
    ident = C.ident
    w_in_b = sb.alloc([128, 8, 704], BF16, "w_in_b")
    gq = sb.alloc([128, 384], F32, "gq")
    gkv = sb.alloc([128, 256], F32, "gkv")
    cs = sb.alloc([128, 16, 64], F32, "cs")
    latT = sb.alloc([128, 6, NT], BF16, "latT")
    C.stg = [sb.alloc([128, 4096], F32, "stg") for _ in range(2)]
    C.stg_elems, C.stg_i = 4096, 0
    xf = [sb.alloc([128, 1024], F32, "xf") for _ in range(2)]
    xb = [sb.alloc([128, 1024], BF16, "xb") for _ in range(2)]
    xT = [sb.alloc([128, 8, 128], BF16, "xT") for _ in range(2)]
    lat = [sb.alloc([128, 704], BF16, "lat") for _ in range(2)]
    kr = [sb.alloc([128, 64], F32, "kr") for _ in range(2)]
    tmp = [sb.alloc([128, 4, 32], F32, "tmp") for _ in range(2)]
    junk = sb.alloc([128, 384], F32, "junk")
    st = [sb.alloc([128, 8], F32, "st") for _ in range(2)]

    P.dma("sp", lambda e: e.dma_start(out=gq[:, :], in_=T.g_q[:, :]), r=[], w=["constA"], key="k_c0")
    P.dma("sp", lambda e: e.dma_start(out=gkv[:, :], in_=T.g_kv[:, :]), r=[], w=["constA"], key="k_c0")
    P.dma("sp", lambda e: e.dma_start(out=cs[:, :, :], in_=T.cs_tok[:, :, :]), r=[], w=["constA"], key="k_c0")
    load_weight(P, C, w_in_b, T.w_in0.ap().rearrange("(k p) f -> p k f", p=128), 8, 704, "w_in_b", "w_in0")

    def pre(i):
        b = i % 2
        P.dma("sp", lambda e, i=i, b=b: e.dma_start(out=xf[b][:, :], in_=T.x[i * 128:(i + 1) * 128, :]),
              r=[], w=[("xf", b)], key="k_xf%d" % b)
        P.act(lambda e, b=b: e.copy(out=xb[b][:, :], in_=xf[b][:, :]), r=[("xf", b)], w=[("xb", b)])
        pT = ps[4 + b][:, :].bitcast(BF16)
        for j in range(8):
            P.pe(lambda e, j=j, b=b, pT=pT: e.transpose(pT[:, j * 128:(j + 1) * 128], xb[b][:, j * 128:(j + 1) * 128], ident[:, :]),
                 r=[("xb", b), "ident"], w=[("ps", 4 + b)])
        P.dve(lambda e, b=b, pT=pT: e.tensor_copy(out=xT[b][:, :, :], in_=pT.rearrange("p (k t) -> p k t", t=128)),
              r=[("ps", 4 + b)], w=[("xT", b)])
        ph1, ph2 = ps[b], ps[2 + b]
        for k in range(8):
            P.pe(lambda e, k=k, b=b, ph1=ph1: e.matmul(ph1[:, 0:384], lhsT=xT[b][:, k, :], rhs=w_in_b[:, k, 0:384], start=(k == 0), stop=(k == 7)),
                 r=[("xT", b), "w_in_b"], w=[("ps", b)])
        for k in range(8):
            P.pe(lambda e, k=k, b=b, ph2=ph2: e.matmul(ph2[:, 0:320], lhsT=xT[b][:, k, :], rhs=w_in_b[:, k, 384:704], start=(k == 0), stop=(k == 7)),
                 r=[("xT", b), "w_in_b"], w=[("ps", 2 + b)])

    def post(i):
        b = i % 2
        ph1, ph2 = ps[b], ps[2 + b]
        P.act(lambda e, b=b, ph1=ph1: e.activation(out=junk[:, 0:384], in_=ph1[:, 0:384], func=AF.Square, accum_out=st[b][:, 0:1]),
              r=[("ps", b)], w=["junk", ("st", b)])
        P.act(lambda e, b=b, ph2=ph2: e.activation(out=junk[:, 0:256], in_=ph2[:, 0:256], func=AF.Square, accum_out=st[b][:, 1:2]),
              r=[("ps", 2 + b)], w=["junk", ("st", b)])
        P.act(lambda e, b=b, ph2=ph2: e.copy(out=kr[b][:, :], in_=ph2[:, 256:320]), r=[("ps", 2 + b)], w=[("kr", b)])
        P.dve(lambda e, b=b: e.tensor_scalar(out=st[b][:, 2:3], in0=st[b][:, 0:1], scalar1=1.0 / 384, scalar2=RMS_EPS, op0=ALU.mult, op1=ALU.add),
              r=[("st", b)], w=[("st2", b)])
        P.dve(lambda e, b=b: e.tensor_scalar(out=st[b][:, 3:4], in0=st[b][:, 1:2], scalar1=1.0 / 256, scalar2=RMS_EPS, op0=ALU.mult, op1=ALU.add),
              r=[("st", b), ("st2", b)], w=[("st2", b)])
        P.act(lambda e, b=b: e.activation(out=st[b][:, 4:6], in_=st[b][:, 2:4], func=AF.Sqrt), r=[("st2", b)], w=[("st3", b)])
        P.dve(lambda e, b=b: e.reciprocal(out=st[b][:, 6:8], in_=st[b][:, 4:6]), r=[("st3", b)], w=[("st4", b)])
        P.dve(lambda e, b=b, ph1=ph1: e.scalar_tensor_tensor(out=lat[b][:, 0:384], in0=ph1[:, 0:384], scalar=st[b][:, 6:7], in1=gq[:, :], op0=ALU.mult, op1=ALU.mult),
              r=[("ps", b), ("st4", b), "constA"], w=[("lat", b)])
        P.dve(lambda e, b=b, ph2=ph2: e.scalar_tensor_tensor(out=lat[b][:, 384:640], in0=ph2[:, 0:256], scalar=st[b][:, 7:8], in1=gkv[:, :], op0=ALU.mult, op1=ALU.mult),
              r=[("ps", 2 + b), ("st4", b), "constA"], w=[("lat", b)])
        cosv, sinv = cs[:, i, 0:32], cs[:, i, 32:64]
        P.dve(lambda e, b=b, cosv=cosv: e.tensor_tensor(out=tmp[b][:, 0, :], in0=kr[b][:, 0:32], in1=cosv, op=ALU.mult), r=[("kr", b), "constA"], w=[("tmp", b)])
        P.dve(lambda e, b=b, sinv=sinv: e.tensor_tensor(out=tmp[b][:, 1, :], in0=kr[b][:, 32:64], in1=sinv, op=ALU.mult), r=[("kr", b), "constA", ("tmp", b)], w=[("tmp", b)])
        P.dve(lambda e, b=b, sinv=sinv: e.tensor_tensor(out=tmp[b][:, 2, :], in0=kr[b][:, 0:32], in1=sinv, op=ALU.mult), r=[("kr", b), "constA", ("tmp", b)], w=[("tmp", b)])
        P.dve(lambda e, b=b, cosv=cosv: e.tensor_tensor(out=tmp[b][:, 3, :], in0=kr[b][:, 32:64], in1=cosv, op=ALU.mult), r=[("kr", b), "constA", ("tmp", b)], w=[("tmp", b)])
        P.dve(lambda e, b=b: e.tensor_tensor(out=lat[b][:, 640:672], in0=tmp[b][:, 0, :], in1=tmp[b][:, 1, :], op=ALU.subtract), r=[("tmp", b)], w=[("lat", b)])
        P.dve(lambda e, b=b: e.tensor_tensor(out=lat[b][:, 672:704], in0=tmp[b][:, 2, :], in1=tmp[b][:, 3, :], op=ALU.add), r=[("tmp", b), ("lat", b)], w=[("lat", b)])
        pT2 = ps[6 + b][:, :].bitcast(BF16)
        for j in range(5):
            P.pe(lambda e, j=j, b=b, pT2=pT2: e.transpose(pT2[:, j * 128:(j + 1) * 128], lat[b][:, j * 128:(j + 1) * 128], ident[:, :]),
                 r=[("lat", b), "ident"], w=[("ps", 6 + b)])
        P.pe(lambda e, b=b, pT2=pT2: e.transpose(pT2[0:64, 640:768], lat[b][:, 640:704], ident[:, :]),
             r=[("lat", b), "ident"], w=[("ps", 6 + b)])
        P.act(lambda e, i=i, b=b, pT2=pT2: e.copy(out=latT[:, 0:5, i * 128:(i + 1) * 128], in_=pT2[:, 0:640].rearrange("p (k t) -> p k t", t=128)),
              r=[("ps", 6 + b)], w=["latT"])
        P.act(lambda e, i=i, b=b, pT2=pT2: e.copy(out=latT[0:64, 5, i * 128:(i + 1) * 128], in_=pT2[0:64, 640:768]),
              r=[("ps", 6 + b)], w=["latT"])
    def ship(c):
        for j in range(5):
            P.dma("sp", lambda e, j=j, c=c: e.dma_start(out=T.latT_in[c][j * 128:(j + 1) * 128, :], in_=latT[:, j, c * 512:(c + 1) * 512]), r=["latT"], w=[("latT_in", c)], key="k_lst%d" % c)
        P.dma("sp", lambda e, c=c: e.dma_start(out=T.latT_in[c][640:704, :], in_=latT[0:64, 5, c * 512:(c + 1) * 512]), r=["latT"], w=[("latT_in", c)], key="k_lst%d" % c)
        P.add("pool", lambda e, c=c: e.collective_compute("AllGather", ALU.bypass, replica_groups=[[0, 1, 2, 3], [4, 5, 6, 7]],
                                                           ins=[T.latT_in[c].ap()], outs=[T.latT_all[c].ap()]),
              reads=[("latT_in", c)], writes=["latT_all"], kind="cc", key="k_cc")

    for i in range(16):
        pre(i)
        if i >= 1:
            post(i - 1)
            if (i - 1) % 4 == 3:
                ship((i - 1) // 4)
    post(15)
    ship(3)
    P.barrier()


def phase_b0(nc, P, C, T):
    _ = (T.cc_f, T.ss_f, T.w_uq_c, T.w_ukv_c)
    sb = SBA(nc)
    ps = C.ps
    ident = C.ident
    wq = sb.alloc([128, 3, 512], BF16, "wq")
    wkv = sb.alloc([128, 2, 512], BF16, "wkv")
    KnT = [sb.alloc([128, SEQ], BF16, "KnT") for _ in range(2)]
    KrT = sb.alloc([128, SEQ], BF16, "KrT")
    Vaug = sb.alloc([128, 2, 64, 129], BF16, "Vaug")
    C.stg = [sb.alloc([128, 2048], F32, "stg") for _ in range(2)]
    C.stg_elems, C.stg_i = 2048, 0
    cqT = [sb.alloc([128, 3, 512], BF16, "cqT") for _ in range(2)]
    ckvT = [sb.alloc([128, 2, 512], BF16, "ckvT") for _ in range(2)]
    ccf = [sb.alloc([64, 512], F32, "ccf") for _ in range(2)]
    ssf = [sb.alloc([64, 512], F32, "ssf") for _ in range(2)]
    QnT = [[sb.alloc([128, 512], BF16, "QnT") for _ in range(2)] for _ in range(2)]
    QrT = [[sb.alloc([128, 512], BF16, "QrT") for _ in range(2)] for _ in range(2)]
    r1 = [sb.alloc([64, 512], F32, "r1") for _ in range(2)]
    r2 = [sb.alloc([64, 512], F32, "r2") for _ in range(2)]
    PT = [sb.alloc([128, 512], BF16, "PT") for _ in range(6)]
    rl = [sb.alloc([128, 4], F32, "rl") for _ in range(2)]
    ob = [sb.alloc([128, 128], BF16, "ob") for _ in range(2)]
    OTs = [sb.alloc([128, 512], BF16, "OTs") for _ in range(2)]
    Pacc = [sb.alloc([128, 512], F32, "Pacc") for _ in range(2)]
    OTf = [sb.alloc([128, 512], F32, "OTf") for _ in range(2)]

    load_weight(P, C, wq, T.w_uq_c.ap().rearrange("(k p) f -> p k f", p=128), 3, 512, "wq", "w_uq_c")
    load_weight(P, C, wkv, T.w_ukv_c.ap().rearrange("(k p) f -> p k f", p=128), 2, 512, "wkv", "w_ukv_c")
    P.pool(lambda e: e.memset(Vaug[:, :, :, 128:129], 1.0), r=[], w=["Vones"])
    P.pool(lambda e: e.memset(KrT[64:128, :], 0.0), r=[], w=["Kzero"])
    for bt_ in range(2):
        for hl_ in range(2):
            P.pool(lambda e, bt_=bt_, hl_=hl_: e.memset(QrT[bt_][hl_][64:128, :], 0.0), r=[], w=[("Qzero", bt_, hl_)])
    def lat_rows(r, f0, n):
        k = f0 // 256
        rows_k = (256, 256, 192)[k]
        return T.latT_all[k], r * rows_k + f0 % 256

    for tt_ in range(16):
        P.dma("sp", lambda e, tt_=tt_: e.dma_start(out=KrT[0:64, tt_ * 512:(tt_ + 1) * 512], in_=T.latT_all[tt_ % 4][(tt_ // 4) * 704 + 640:(tt_ // 4) * 704 + 704, :]),
              r=["latT_all"], w=["KrT"], key="k_krt")

    pj = [0]

    def pbank():
        b = 5 + (pj[0] % 3)
        pj[0] += 1
        return b

    sidx = [0]
    for t in range(16):
        bt = t % 2
        r, c0 = t // 4, (t % 4) * 512
        lt = T.latT_all[t % 4]
        P.dma("sp", lambda e, lt=lt, r=r, bt=bt: e.dma_start(out=cqT[bt][:, :, :], in_=lt[r * 704:r * 704 + 384, :].rearrange("(j p) c -> p j c", p=128)),
              r=["latT_all"], w=[("cqT", bt)], key="k_cq%d" % bt)
        P.dma("sp", lambda e, lt=lt, r=r, bt=bt: e.dma_start(out=ckvT[bt][:, :, :], in_=lt[r * 704 + 384:r * 704 + 640, :].rearrange("(j p) c -> p j c", p=128)),
              r=["latT_all"], w=[("ckvT", bt)], key="k_ckv%d" % bt)
        P.dma("sp", lambda e, t=t, bt=bt: e.dma_start(out=ccf[bt][:, :], in_=T.cc_f[:, t * 512:(t + 1) * 512]), r=[], w=[("ccf", bt)], key="k_ccf%d" % bt)
        P.dma("sp", lambda e, t=t, bt=bt: e.dma_start(out=ssf[bt][:, :], in_=T.ss_f[:, t * 512:(t + 1) * 512]), r=[], w=[("ssf", bt)], key="k_ssf%d" % bt)
        for s in range(4):
            pb = pbank()
            for j in range(2):
                P.pe(lambda e, s=s, j=j, pb=pb, bt=bt: e.matmul(ps[pb][:, 0:256], lhsT=ckvT[bt][:, j, s * 128:(s + 1) * 128], rhs=wkv[:, j, 256:512], start=(j == 0), stop=(j == 1)),
                     r=[("ckvT", bt), "wkv"], w=[("ps", pb)])
            P.act(lambda e, s=s, pb=pb, t=t: e.copy(out=Vaug[:, :, 4 * t + s, 0:128], in_=ps[pb][:, 0:256].rearrange("p (h d) -> p h d", d=128)),
                  r=[("ps", pb)], w=[("V", 4 * t + s)])
        for hl in range(2):
            pb = pbank()
            for j in range(2):
                P.pe(lambda e, j=j, pb=pb, bt=bt, hl=hl: e.matmul(ps[pb][:, :], lhsT=wkv[:, j, hl * 128:(hl + 1) * 128], rhs=ckvT[bt][:, j, :], start=(j == 0), stop=(j == 1)),
                     r=[("ckvT", bt), "wkv"], w=[("ps", pb)])
            P.dve(lambda e, pb=pb, hl=hl, t=t: e.tensor_copy(out=KnT[hl][:, t * 512:(t + 1) * 512], in_=ps[pb][:, :]),
                  r=[("ps", pb)], w=[("KnT", hl, t)])
            pb = pbank()
            for j in range(3):
                P.pe(lambda e, j=j, pb=pb, bt=bt, hl=hl: e.matmul(ps[pb][:, :], lhsT=wq[:, j, hl * 256:hl * 256 + 128], rhs=cqT[bt][:, j, :], start=(j == 0), stop=(j == 2)),
                     r=[("cqT", bt), "wq"], w=[("ps", pb)])
            P.act(lambda e, pb=pb, hl=hl, bt=bt: e.copy(out=QnT[bt][hl][:, :], in_=ps[pb][:, :]), r=[("ps", pb)], w=[("QnT", bt, hl)])
            pa = pbank()
            for j in range(3):
                P.pe(lambda e, j=j, pa=pa, bt=bt, hl=hl: e.matmul(ps[pa][0:64, :], lhsT=wq[:, j, hl * 256 + 128:hl * 256 + 192], rhs=cqT[bt][:, j, :], start=(j == 0), stop=(j == 2)),
                     r=[("cqT", bt), "wq"], w=[("ps", pa)])
            P.dve(lambda e, pa=pa, bt=bt, hl=hl: e.tensor_tensor(out=r1[hl][:, :], in0=ps[pa][0:64, :], in1=ccf[bt][:, :], op=ALU.mult),
                  r=[("ps", pa), ("ccf", bt)], w=[("r1", hl)])
            pb2 = pbank()
            for j in range(3):
                P.pe(lambda e, j=j, pb2=pb2, bt=bt, hl=hl: e.matmul(ps[pb2][0:64, :], lhsT=wq[:, j, hl * 256 + 192:hl * 256 + 256], rhs=cqT[bt][:, j, :], start=(j == 0), stop=(j == 2)),
                     r=[("cqT", bt), "wq"], w=[("ps", pb2)])
            P.dve(lambda e, pb2=pb2, bt=bt, hl=hl: e.tensor_tensor(out=r2[hl][:, :], in0=ps[pb2][0:64, :], in1=ssf[bt][:, :], op=ALU.mult),
                  r=[("ps", pb2), ("ssf", bt)], w=[("r2", hl)])
            P.pool(lambda e, bt=bt, hl=hl: e.tensor_tensor(out=QrT[bt][hl][0:64, :], in0=r1[hl][:, :], in1=r2[hl][:, :], op=ALU.add),
                   r=[("r1", hl), ("r2", hl)], w=[("QrT", bt, hl)])
        for hl in range(2):
            nJ = 4 * (t + 1)
            tiles = list(range(nJ))
            meta = {}

            def qk(J, hl=hl, t=t, bt=bt):
                n = sidx[0]
                sidx[0] += 1
                sbk, pbf = n % 3, n % 6
                m = max(0, J - 4 * t)
                q0 = 128 * m
                meta[J] = (sbk, pbf, m, q0)
                P.pe(lambda e: e.matmul(ps[sbk][:, q0:512], lhsT=KnT[hl][:, J * 128:(J + 1) * 128], rhs=QnT[bt][hl][:, q0:512], start=True, stop=False),
                     r=[("KnT", hl, J // 4), ("QnT", bt, hl)], w=[("ps", sbk)])
                P.pe(lambda e: e.matmul(ps[sbk][:, q0:512], lhsT=KrT[:, J * 128:(J + 1) * 128], rhs=QrT[bt][hl][:, q0:512], start=False, stop=True),
                     r=["KrT", "Kzero", ("QrT", bt, hl), ("Qzero", bt, hl)], w=[("ps", sbk)])

            def ex(J, t=t):
                sbk, pbf, m, q0 = meta[J]
                P.act(lambda e: e.activation(out=PT[pbf][:, q0:512], in_=ps[sbk][:, q0:512], func=AF.Exp, scale=SC0),
                      r=[("ps", sbk)], w=[("PT", pbf)])
                if J >= 4 * t:
                    P.pool(lambda e: e.memset(PT[pbf][64:128, q0:q0 + 64], 0.0), r=[("PT", pbf)], w=[("PT", pbf)])

            ehl = (2 * t + hl) % 2
            acc = 3 + ehl
            pa = Pacc[ehl]

            def av(J, hl=hl, nJ=nJ, acc=acc, pa=pa, ehl=ehl):
                sbk, pbf, m, q0 = meta[J]
                P.pe(lambda e: e.matmul(ps[acc][:, q0:512], lhsT=Vaug[:, hl, J, 0:128], rhs=PT[pbf][:, q0:512], start=(J == 0), stop=(J == nJ - 1)),
                     r=[("PT", pbf), ("V", J)], w=[("ps", acc)])
                SPL = 320
                if J == 0:
                    P.dve(lambda e: e.tensor_copy(out=pa[:, 0:SPL], in_=PT[pbf][:, 0:SPL]), r=[("PT", pbf)], w=[("PaccD", ehl)])
                    P.pool(lambda e: e.tensor_copy(out=pa[:, SPL:512], in_=PT[pbf][:, SPL:512]), r=[("PT", pbf)], w=[("PaccP", ehl)])
                else:
                    if q0 < SPL:
                        P.dve(lambda e: e.tensor_tensor(out=pa[:, q0:SPL], in0=pa[:, q0:SPL], in1=PT[pbf][:, q0:SPL], op=ALU.add),
                              r=[("PT", pbf), ("PaccD", ehl)], w=[("PaccD", ehl)])
                    q1 = max(q0, SPL)
                    P.pool(lambda e: e.tensor_tensor(out=pa[:, q1:512], in0=pa[:, q1:512], in1=PT[pbf][:, q1:512], op=ALU.add),
                           r=[("PT", pbf), ("PaccP", ehl)], w=[("PaccP", ehl)])

            qk(tiles[0])
            if nJ > 1:
                qk(tiles[1])
            for i, J in enumerate(tiles):
                ex(J)
                if i + 2 < nJ:
                    qk(tiles[i + 2])
                av(J)
            eb = ehl
            P.act(lambda e, eb=eb, acc=acc: e.copy(out=OTf[eb][:, :], in_=ps[acc][:, :]), r=[("ps", acc)], w=[("OTf", eb)])
            pl = pbank()
            for qs in range(4):
                P.pe(lambda e, qs=qs, pl=pl, pa=pa: e.matmul(ps[pl][:, qs:qs + 1], lhsT=pa[:, qs * 128:(qs + 1) * 128], rhs=C.ones_f[:, 0:1], start=True, stop=True, skip_group_check=True),
                     r=[("PaccD", ehl), ("PaccP", ehl), "ident"], w=[("ps", pl)])
            ptr = pbank()
            for qs in range(4):
                P.pe(lambda e, qs=qs, ptr=ptr, eb=eb: e.transpose(ps[ptr][:, qs * 128:(qs + 1) * 128], OTf[eb][:, qs * 128:(qs + 1) * 128], C.ident_f[:, :]),
                     r=[("OTf", eb), "ident"], w=[("ps", ptr)])
            P.dve(lambda e, eb=eb, pl=pl: e.reciprocal(out=rl[eb][:, 0:4], in_=ps[pl][:, 0:4]), r=[("ps", pl)], w=[("rl", eb)])
            pbT = pbank()
            pT = ps[pbT][:, :].bitcast(BF16)
            for qs in range(4):
                ob_i = qs % 2
                P.dve(lambda e, qs=qs, ptr=ptr, eb=eb, ob_i=ob_i: e.tensor_scalar(out=ob[ob_i][:, :], in0=ps[ptr][:, qs * 128:(qs + 1) * 128], scalar1=rl[eb][:, qs:qs + 1], scalar2=None, op0=ALU.mult),
                      r=[("ps", ptr), ("rl", eb)], w=[("ob", ob_i)])
                P.pe(lambda e, qs=qs, ob_i=ob_i, pT=pT: e.transpose(pT[:, qs * 128:(qs + 1) * 128], ob[ob_i][:, :], ident[:, :]),
                     r=[("ob", ob_i), "ident"], w=[("ps", pbT)])
            P.act(lambda e, eb=eb, pT=pT: e.copy(out=OTs[eb][:, :], in_=pT[:, 0:512]), r=[("ps", pbT)], w=[("OTs", eb)])
            P.dma("sp", lambda e, eb=eb, hl=hl, t=t: e.dma_start(out=T.OT_in[t // 4, hl * 128:(hl + 1) * 128, (t % 4) * 512:(t % 4 + 1) * 512], in_=OTs[eb][:, :]),
                  r=[("OTs", eb)], w=[("OT_in", t // 4, eb)], key="k_ot%d_%d" % (t // 4, eb))
            if t % 4 == 3 and hl == 1:
                P.add("pool", lambda e, c=t // 4: e.collective_compute("AllGather", ALU.bypass, replica_groups=[[0, 1, 2, 3], [4, 5, 6, 7]],
                                                               ins=[T.OT_in[c]], outs=[T.OT_all[c]]),
                      reads=[("OT_in", t // 4, 0), ("OT_in", t // 4, 1)], writes=["OT_all"], kind="cc", key="k_cc")
    P.barrier()


def phase_b1(nc, P, C, T):
    _ = (T.w1_c, T.lamv, T.g_sub, T.F1, T.qaug, T.kaug, T.ones2)
    sb = SBA(nc)
    ps = C.ps
    ident = C.ident
    w1 = sb.alloc([128, 8, 768], BF16, "w1")
    KT = [[sb.alloc([128, SEQ], BF16, "KT") for _ in range(2)] for _ in range(2)]
    Vaug = sb.alloc([128, 2, 64, 129], BF16, "Vaug1")
    C.stg = [sb.alloc([128, 2048], F32, "stg") for _ in range(2)]
    C.stg_elems, C.stg_i = 2048, 0
    xT = [sb.alloc([128, 8, 512], BF16, "xT1") for _ in range(2)]
    QTall = sb.alloc([128, 2, 2, 2, 512], BF16, "QTall")
    QT = [[[QTall[:, bt_, mp_, hl_, :] for hl_ in range(2)] for bt_ in range(2)] for mp_ in range(2)]
    PT = [sb.alloc([128, 512], BF16, "PT1") for _ in range(4)]
    F1 = sb.alloc([128, 2, 128], F32, "F1")
    lamv = sb.alloc([128, 4, 64], F32, "lamv")
    gsub = sb.alloc([128, 128], F32, "gsub")
    lm = sb.alloc([128, 8], F32, "lm")
    junk = sb.alloc([128, 128], F32, "junk1")
    rl = [sb.alloc([128, 8], F32, "rl1") for _ in range(2)]
    o1 = [sb.alloc([128, 128], F32, "o1") for _ in range(2)]
    oo = [sb.alloc([128, 128], F32, "oo") for _ in range(2)]
    ob = [sb.alloc([128, 128], BF16, "ob1") for _ in range(2)]
    OTs = [sb.alloc([128, 512], BF16, "OTs1") for _ in range(2)]

    P.dma("sp", lambda e: e.dma_start(out=F1[:, :, :], in_=T.F1[:, :, :]), r=[], w=["constB"], key="k_c0")
    P.dma("sp", lambda e: e.dma_start(out=lamv[:, :, :], in_=T.lamv[:, :, :]), r=[], w=["constB"], key="k_c0")
    P.dma("sp", lambda e: e.dma_start(out=gsub[:, :], in_=T.g_sub[:, :]), r=[], w=["constB"], key="k_c0")
    for mp in range(2):
        for hl in range(2):
            P.dma("sp", lambda e, mp=mp, hl=hl: e.dma_start(out=KT[mp][hl][64:66, :], in_=T.kaug[hl, :, :]), r=[], w=["constB"], key="k_c0")
            P.dma("sp", lambda e, mp=mp, hl=hl: e.dma_start(out=KT[mp][hl][66:68, :], in_=T.ones2[:, :]), r=[], w=["constB"], key="k_c0")
    P.pool(lambda e: e.memset(QTall[64:66, :, :, :, :], 1.0), r=[], w=["Qones"])
    P.pool(lambda e: e.memset(Vaug[:, :, :, 128:129], 1.0), r=[], w=["Vones"])
    load_weight(P, C, w1, T.w1_c.ap().rearrange("(k p) f -> p k f", p=128), 8, 768, "w1", "w1_c")
    P.dve(lambda e: e.scalar_tensor_tensor(out=junk[:, 0:64], in0=lamv[:, 0, :], scalar=1.0, in1=lamv[:, 1, :], op0=ALU.mult, op1=ALU.mult, accum_out=lm[:, 0:1]),
          r=["constB"], w=["junk", "lm0"])
    P.dve(lambda e: e.scalar_tensor_tensor(out=junk[:, 0:64], in0=lamv[:, 2, :], scalar=1.0, in1=lamv[:, 3, :], op0=ALU.mult, op1=ALU.mult, accum_out=lm[:, 1:2]),
          r=["constB", "junk", "lm0"], w=["junk", "lm0"])
    P.act(lambda e: e.activation(out=lm[:, 2:4], in_=lm[:, 0:2], func=AF.Exp), r=["lm0"], w=["lm1"])
    P.dve(lambda e: e.tensor_tensor(out=lm[:, 4:5], in0=lm[:, 2:3], in1=lm[:, 3:4], op=ALU.subtract), r=["lm1"], w=["lm2"])
    P.dve(lambda e: e.tensor_scalar(out=lm[:, 5:6], in0=lm[:, 4:5], scalar1=-1.0, scalar2=-LAMBDA_INIT, op0=ALU.mult, op1=ALU.add), r=["lm2"], w=["neglam"])
    P.dve(lambda e: e.tensor_scalar(out=gsub[:, :], in0=gsub[:, :], scalar1=1.0 - LAMBDA_INIT, scalar2=None, op0=ALU.mult), r=["constB"], w=["gsub2"])

    pj = [0]

    def pbank():
        b = 6 + (pj[0] % 2)
        pj[0] += 1
        return b

    sidx = [0]
    for t in range(16):
        bt = t % 2
        r, c0 = t // 4, (t % 4) * 512
        P.dma("sp", lambda e, t=t, bt=bt: e.dma_start(out=QTall[66:68, bt, :, :, :], in_=T.qaug[:, t, :, :, :]), r=[], w=[("Qaug", bt)], key="k_qa%d" % bt)
        P.dma("sp", lambda e, r=r, t=t, bt=bt: e.dma_start(out=xT[bt][:, :, :], in_=T.x1T_all[t % 4][r * 1024:(r + 1) * 1024, :].rearrange("(k p) c -> p k c", p=128)),
              r=["x1T_all"], w=[("xT", bt)], key="k_lat%d" % bt)
        for s in range(4):
            pb = pbank()
            for k in range(8):
                P.pe(lambda e, s=s, k=k, pb=pb, bt=bt: e.matmul(ps[pb][:, 0:256], lhsT=xT[bt][:, k, s * 128:(s + 1) * 128], rhs=w1[:, k, 512:768], start=(k == 0), stop=(k == 7)),
                     r=[("xT", bt), "w1"], w=[("ps", pb)])
            P.act(lambda e, s=s, pb=pb, t=t: e.copy(out=Vaug[:, :, 4 * t + s, 0:128], in_=ps[pb][:, 0:256].rearrange("p (h d) -> p h d", d=128)),
                  r=[("ps", pb)], w=[("V", 4 * t + s)])
        for hl in range(2):
            pb = pbank()
            for k in range(8):
                P.pe(lambda e, k=k, pb=pb, bt=bt, hl=hl: e.matmul(ps[pb][:, :], lhsT=w1[:, k, 256 + hl * 128:256 + (hl + 1) * 128], rhs=xT[bt][:, k, :], start=(k == 0), stop=(k == 7)),
                     r=[("xT", bt), "w1"], w=[("ps", pb)])
            P.act(lambda e, pb=pb, hl=hl, t=t: e.copy(out=KT[0][hl][0:64, t * 512:(t + 1) * 512], in_=ps[pb][0:64, :]), r=[("ps", pb)], w=[("K", 0, hl, t)])
            P.dve(lambda e, pb=pb, hl=hl, t=t: e.tensor_copy(out=KT[1][hl][0:64, t * 512:(t + 1) * 512], in_=ps[pb][64:128, :]), r=[("ps", pb)], w=[("K", 1, hl, t)])
            pb = pbank()
            for k in range(8):
                P.pe(lambda e, k=k, pb=pb, bt=bt, hl=hl: e.matmul(ps[pb][:, :], lhsT=w1[:, k, hl * 128:(hl + 1) * 128], rhs=xT[bt][:, k, :], start=(k == 0), stop=(k == 7)),
                     r=[("xT", bt), "w1"], w=[("ps", pb)])
            P.act(lambda e, pb=pb, hl=hl, bt=bt: e.copy(out=QT[0][bt][hl][0:64, :], in_=ps[pb][0:64, :]), r=[("ps", pb)], w=[("Q", 0, bt, hl)])
            P.dve(lambda e, pb=pb, hl=hl, bt=bt: e.tensor_copy(out=QT[1][bt][hl][0:64, :], in_=ps[pb][64:128, :]), r=[("ps", pb)], w=[("Q", 1, bt, hl)])
        for hl in range(2):
            nJ = 4 * (t + 1)
            tiles = [(J, mp) for J in range(nJ) for mp in range(2)]
            meta = {}

            def qk(tl, hl=hl, t=t, bt=bt):
                J, mp = tl
                n = sidx[0]
                sidx[0] += 1
                sbk, pbf = n % 3, n % 4
                m = max(0, J - 4 * t)
                q0 = 128 * m
                meta[tl] = (sbk, pbf, m, q0)
                P.pe(lambda e: e.matmul(ps[sbk][:, q0:512], lhsT=KT[mp][hl][0:68, J * 128:(J + 1) * 128], rhs=QT[mp][bt][hl][0:68, q0:512], start=True, stop=True),
                     r=[("K", mp, hl, J // 4), ("Q", mp, bt, hl), ("Qaug", bt), "Qones", "constB"], w=[("ps", sbk)])

            def ex(tl, hl=hl, t=t):
                J, mp = tl
                sbk, pbf, m, q0 = meta[tl]
                P.act(lambda e: e.activation(out=PT[pbf][:, q0:512], in_=ps[sbk][:, q0:512], func=AF.Exp, scale=SC1),
                      r=[("ps", sbk)], w=[("PT", pbf)])
                if J >= 4 * t:
                    P.pool(lambda e: e.tensor_tensor(out=PT[pbf][:, q0:q0 + 128], in0=PT[pbf][:, q0:q0 + 128], in1=F1[:, hl, :], op=ALU.mult),
                           r=[("PT", pbf), "constB"], w=[("PT", pbf)])

            def av(tl, hl=hl, nJ=nJ):
                J, mp = tl
                sbk, pbf, m, q0 = meta[tl]
                for qs in range(m, 4):
                    a = mp * 4 + qs
                    bank, off = 3 + a // 3, (a % 3) * 129
                    P.pe(lambda e, qs=qs, bank=bank, off=off, a=a: e.matmul(ps[bank][:, off:off + 129], lhsT=PT[pbf][:, qs * 128:(qs + 1) * 128], rhs=Vaug[:, hl, J, :],
                                                                      start=(J == 0 and a % 3 == 0), stop=(J == nJ - 1), skip_group_check=True),
                         r=[("PT", pbf), ("V", J), "Vones"], w=[("ps", bank)])

            qk(tiles[0])
            qk(tiles[1])
            for i, tl in enumerate(tiles):
                ex(tl)
                if i + 2 < len(tiles):
                    qk(tiles[i + 2])
                av(tl)
            eb = (2 * t + hl) % 2
            pbT = pbank()
            pT = ps[pbT][:, :].bitcast(BF16)
            for qs in range(4):
                a1, a2 = qs, 4 + qs
                b1_, f1_ = 3 + a1 // 3, (a1 % 3) * 129
                b2_, f2_ = 3 + a2 // 3, (a2 % 3) * 129
                q2 = qs % 2
                P.dve(lambda e, b1_=b1_, f1_=f1_, q2=q2: e.reciprocal(out=rl[q2][:, 0:1], in_=ps[b1_][:, f1_ + 128:f1_ + 129]), r=[("ps", b1_)], w=[("rl", q2)])
                P.dve(lambda e, b2_=b2_, f2_=f2_, q2=q2: e.reciprocal(out=rl[q2][:, 1:2], in_=ps[b2_][:, f2_ + 128:f2_ + 129]), r=[("ps", b2_), ("rl", q2)], w=[("rl", q2)])
                P.dve(lambda e, q2=q2: e.tensor_tensor(out=rl[q2][:, 2:3], in0=rl[q2][:, 1:2], in1=lm[:, 5:6], op=ALU.mult), r=[("rl", q2), "neglam"], w=[("rl2", q2)])
                P.dve(lambda e, b1_=b1_, f1_=f1_, q2=q2: e.tensor_scalar(out=o1[q2][:, :], in0=ps[b1_][:, f1_:f1_ + 128], scalar1=rl[q2][:, 0:1], scalar2=None, op0=ALU.mult),
                      r=[("ps", b1_), ("rl", q2)], w=[("o1", q2)])
                P.dve(lambda e, b2_=b2_, f2_=f2_, q2=q2: e.scalar_tensor_tensor(out=oo[q2][:, :], in0=ps[b2_][:, f2_:f2_ + 128], scalar=rl[q2][:, 2:3], in1=o1[q2][:, :], op0=ALU.mult, op1=ALU.add),
                      r=[("ps", b2_), ("rl2", q2), ("o1", q2)], w=[("oo", q2)])
                P.act(lambda e, q2=q2: e.activation(out=junk[:, :], in_=oo[q2][:, :], func=AF.Square, accum_out=rl[q2][:, 3:4]), r=[("oo", q2)], w=["junk", ("rl3", q2)])
                P.dve(lambda e, q2=q2: e.tensor_scalar(out=rl[q2][:, 4:5], in0=rl[q2][:, 3:4], scalar1=1.0 / 128, scalar2=RMS_EPS, op0=ALU.mult, op1=ALU.add), r=[("rl3", q2)], w=[("rl4", q2)])
                P.act(lambda e, q2=q2: e.activation(out=rl[q2][:, 5:6], in_=rl[q2][:, 4:5], func=AF.Sqrt), r=[("rl4", q2)], w=[("rl5", q2)])
                P.dve(lambda e, q2=q2: e.reciprocal(out=rl[q2][:, 6:7], in_=rl[q2][:, 5:6]), r=[("rl5", q2)], w=[("rl6", q2)])
                P.dve(lambda e, q2=q2: e.scalar_tensor_tensor(out=ob[q2][:, :], in0=oo[q2][:, :], scalar=rl[q2][:, 6:7], in1=gsub[:, :], op0=ALU.mult, op1=ALU.mult),
                      r=[("oo", q2), ("rl6", q2), "gsub2"], w=[("ob", q2)])
                P.pe(lambda e, qs=qs, q2=q2, pT=pT: e.transpose(pT[:, qs * 128:(qs + 1) * 128], ob[q2][:, :], ident[:, :]), r=[("ob", q2), "ident"], w=[("ps", pbT)])
            P.act(lambda e, eb=eb, pT=pT: e.copy(out=OTs[eb][:, :], in_=pT[:, 0:512]), r=[("ps", pbT)], w=[("OTs", eb)])
            P.dma("sp", lambda e, eb=eb, hl=hl, t=t: e.dma_start(out=T.OT_in[t // 4, hl * 128:(hl + 1) * 128, (t % 4) * 512:(t % 4 + 1) * 512], in_=OTs[eb][:, :]),
                  r=[("OTs", eb)], w=[("OT_in", t // 4, eb)], key="k_ot%d_%d" % (t // 4, eb))
            if t % 4 == 3 and hl == 1:
                P.add("pool", lambda e, c=t // 4: e.collective_compute("AllGather", ALU.bypass, replica_groups=[[0, 1, 2, 3], [4, 5, 6, 7]],
                                                               ins=[T.OT_in[c]], outs=[T.OT_all[c]]),
                      reads=[("OT_in", t // 4, 0), ("OT_in", t // 4, 1)], writes=["OT_all"], kind="cc", key="k_cc")
    P.barrier()


def _layernorm(P, C, src, dst, g, b, st, mv, tag, srctok, dsttok, eng2="pool"):
    for hf in range(2):
        P.dve(lambda e, hf=hf: e.bn_stats(out=st[:, hf * 6:(hf + 1) * 6], in_=src[:, hf * 512:(hf + 1) * 512]),
              r=[srctok] + ([(tag, "st")] if hf else []), w=[(tag, "st")])
    P.dve(lambda e: e.bn_aggr(out=mv[:, 0:2], in_=st[:, 0:12]), r=[(tag, "st")], w=[(tag, "mv")])
    P.dve(lambda e: e.tensor_scalar(out=mv[:, 2:3], in0=mv[:, 1:2], scalar1=LN_EPS, scalar2=None, op0=ALU.add), r=[(tag, "mv")], w=[(tag, "mv2")])
    P.act(lambda e: e.activation(out=mv[:, 3:4], in_=mv[:, 2:3], func=AF.Sqrt), r=[(tag, "mv2")], w=[(tag, "mv3")])
    P.dve(lambda e: e.reciprocal(out=mv[:, 4:5], in_=mv[:, 3:4]), r=[(tag, "mv3")], w=[(tag, "mv4")])
    P.dve(lambda e: e.tensor_scalar(out=dst[:, :], in0=src[:, :], scalar1=mv[:, 0:1], scalar2=mv[:, 4:5], op0=ALU.subtract, op1=ALU.mult),
          r=[srctok, (tag, "mv"), (tag, "mv4")], w=[dsttok])
    P.add(eng2, lambda e: e.tensor_tensor(out=dst[:, :], in0=dst[:, :], in1=g[:, :], op=ALU.mult), reads=[dsttok, "lnp"], writes=[dsttok])
    P.add(eng2, lambda e: e.tensor_tensor(out=dst[:, :], in0=dst[:, :], in1=b[:, :], op=ALU.add), reads=[dsttok, "lnp"], writes=[dsttok])


def phase_c(nc, P, C, T, L):
    ps = C.ps
    ident = C.ident
    KB = 1024
    base = SB_BASE

    def at(off_kb, shape, dtype, name):
        C.cn += 1
        return nc.alloc_sbuf_tensor_at("%s_c%d" % (name, C.cn), list(shape), dtype, offset=base + int(off_kb * KB))

    res_src = T.x if L == 0 else T.h1
    out_dst = T.h1 if L == 0 else T.out
    w_o = T.w_o0 if L == 0 else T.w_o1
    lng = [T.ln_mix_g[L], T.ln_mix_b[L], T.ln_ffn_g[L], T.ln_ffn_b[L]]
    w_gu, w_dn = T.w_gu[L], T.w_dn[L]

    XmT = at(0, [128, 8, NT], BF16, "XmT")
    HT = at(32, [128, 22, NT], BF16, "HT")
    wd_b = at(120, [128, 22, 1024], BF16, "wd_b")
    ln2g = at(164, [128, 1024], F32, "ln2g")
    ln2b = at(168, [128, 1024], F32, "ln2b")
    small = at(172, [128, 64], F32, "small")
    OTm = at(32, [128, 8, NT], BF16, "OTm")
    wo_b = at(64, [128, 8, 1024], BF16, "wo_b")
    ln1g = at(80, [128, 1024], F32, "ln1g")
    ln1b = at(84, [128, 1024], F32, "ln1b")
    xres = [at(88 + 4 * i, [128, 1024], F32, "xres") for i in range(2)]
    y = [at(96 + 4 * i, [128, 1024], F32, "y") for i in range(2)]
    hm = [at(104 + 4 * i, [128, 1024], F32, "hm") for i in range(2)]
    hb = [at(112 + 2 * i, [128, 1024], BF16, "hb") for i in range(2)]
    C.stg = [at(120 + 16 * i, [128, 4096], F32, "stg") for i in range(2)]
    C.stg_elems, C.stg_i = 4096, 0
    st = [small[:, 0:12], small[:, 16:28]]
    mv = [small[:, 32:40], small[:, 40:48]]

    def dyn_load(e, h):
        r = P.pid % 4
        return e.dma_start(out=OTm[:, h, :], in_=T.OT_all[bass.ds(r, 1), h * 128:(h + 1) * 128, :].rearrange("o p c -> (o p) c"))

    for h in range(8):
        P.dma("sp", lambda e, h=h: dyn_load(e, h), r=["OT_all"], w=["OTm"], key="k_otm")
    P.dma("sp", lambda e: e.dma_start(out=ln1g[:, :], in_=lng[0][:, :]), r=[], w=["lnp"], key="k_c0")
    P.dma("sp", lambda e: e.dma_start(out=ln1b[:, :], in_=lng[1][:, :]), r=[], w=["lnp"], key="k_c0")
    load_weight(P, C, wo_b, w_o.ap().rearrange("(k p) f -> p k f", p=128), 8, 1024, "wo_b", "w_o", q="act")

    def pre1(i):
        b = i % 2
        P.dma("sp", lambda e, i=i, b=b: e.dma_start(out=xres[b][:, :], in_=res_src[i * 128:(i + 1) * 128, :]), r=["res_src"], w=[("xres", b)], key="k_xr%d" % b)
        for hf in range(2):
            pb = 2 * b + hf
            for h in range(8):
                P.pe(lambda e, h=h, hf=hf, pb=pb, i=i: e.matmul(ps[pb][:, :], lhsT=OTm[:, h, i * 128:(i + 1) * 128], rhs=wo_b[:, h, hf * 512:(hf + 1) * 512], start=(h == 0), stop=(h == 7)),
                     r=["OTm", "wo_b"], w=[("ps", pb)])

    def y1(i):
        b = i % 2
        for hf in range(2):
            pb = 2 * b + hf
            P.dve(lambda e, hf=hf, pb=pb, b=b: e.scalar_tensor_tensor(out=y[b][:, hf * 512:(hf + 1) * 512], in0=xres[b][:, hf * 512:(hf + 1) * 512], scalar=ALPHA, in1=ps[pb][:, :], op0=ALU.mult, op1=ALU.add),
                  r=[("xres", b), ("ps", pb)], w=[("y", b)])

    def post1(i):
        b = i % 2
        _layernorm(P, C, y[b], hm[b], ln1g, ln1b, st[b], mv[b], ("ln", b), ("y", b), ("hm", b))
        P.act(lambda e, b=b: e.copy(out=hb[b][:, :], in_=hm[b][:, :]), r=[("hm", b)], w=[("hb", b)])
        pT = ps[4 + b][:, :].bitcast(BF16)
        for j in range(8):
            P.pe(lambda e, j=j, b=b, pT=pT: e.transpose(pT[:, j * 128:(j + 1) * 128], hb[b][:, j * 128:(j + 1) * 128], ident[:, :]),
                 r=[("hb", b), "ident"], w=[("ps", 4 + b)])
        P.act(lambda e, i=i, b=b, pT=pT: e.copy(out=XmT[:, :, i * 128:(i + 1) * 128], in_=pT.rearrange("p (k t) -> p k t", t=128)),
              r=[("ps", 4 + b)], w=["XmT"])
        P.dma("sp", lambda e, i=i, b=b: e.dma_start(out=T.hmid[i * 128:(i + 1) * 128, :], in_=hm[b][:, :]), r=[("hm", b)], w=["hmid"], key="k_hm%d" % b)

    for i in range(16):
        pre1(i)
        if i >= 1:
            post1(i - 1)
        y1(i)
    post1(15)
    P.barrier()

    wgu = [at(172.5 + 4 * i, [128, 8, 256], BF16, "wgu") for i in range(2)]
    stgA = [at(180.5 + 8 * i, [128, 2048], F32, "stgA") for i in range(2)]
    stgB = [at(196.5 + 4 * i, [128, 1024], F32, "stgB") for i in range(2)]
    C.sg = [at(164 + 2 * i, [128, 512], F32, "sg") for i in range(2)]
    gsrc = w_gu.ap().rearrange("(k p) f -> p k f", p=128)
    wd_src = w_dn.ap().rearrange("(j p) f -> p j f", p=128)
    for j in range(22):
        bj = j % 2
        s = j % 2
        stv = stgA[s][:, :].rearrange("p (k f) -> p k f", f=256)
        P.dma("sp", lambda e, j=j, stv=stv: e.dma_start(out=stv[:, :, 0:128], in_=gsrc[:, :, j * 128:(j + 1) * 128]), r=["w_gu"], w=[("stgA", s)], key="k_stg%d" % s)
        P.dma("sp", lambda e, j=j, stv=stv: e.dma_start(out=stv[:, :, 128:256], in_=gsrc[:, :, DFF + j * 128:DFF + (j + 1) * 128]), r=["w_gu"], w=[("stgA", s)], key="k_stg%d" % s)
        P.pool(lambda e, bj=bj, stv=stv: e.tensor_copy(out=wgu[bj][:, :, :], in_=stv), r=[("stgA", s)], w=[("wgu", bj)])
        stw = stgB[s][:, :]
        P.dma("sp", lambda e, j=j, stw=stw: e.dma_start(out=stw, in_=wd_src[:, j, :]), r=["w_dn"], w=[("stgB", s)], key="k_stgb%d" % s)
        P.pool(lambda e, j=j, stw=stw: e.tensor_copy(out=wd_b[:, j, :], in_=stw), r=[("stgB", s)], w=["wd_b"])
        for tt in range(4):
            n = (j * 4 + tt) % 2
            pg, pu = 2 * n, 2 * n + 1
            for k in range(8):
                P.pe(lambda e, k=k, bj=bj, tt=tt, pg=pg: e.matmul(ps[pg][:, :], lhsT=wgu[bj][:, k, 0:128], rhs=XmT[:, k, tt * 512:(tt + 1) * 512], start=(k == 0), stop=(k == 7)),
                     r=[("wgu", bj), "XmT"], w=[("ps", pg)])
            for k in range(8):
                P.pe(lambda e, k=k, bj=bj, tt=tt, pu=pu: e.matmul(ps[pu][:, :], lhsT=wgu[bj][:, k, 128:256], rhs=XmT[:, k, tt * 512:(tt + 1) * 512], start=(k == 0), stop=(k == 7)),
                     r=[("wgu", bj), "XmT"], w=[("ps", pu)])
            P.act(lambda e, n=n, pg=pg: e.activation(out=C.sg[n][:, :], in_=ps[pg][:, :], func=AF.Silu), r=[("ps", pg)], w=[("sg", n)])
            P.dve(lambda e, n=n, pu=pu, j=j, tt=tt: e.tensor_tensor(out=HT[:, j, tt * 512:(tt + 1) * 512], in0=C.sg[n][:, :], in1=ps[pu][:, :], op=ALU.mult),
                  r=[("sg", n), ("ps", pu)], w=["HT"])
    P.barrier()

    xres2 = [at(0 + 4 * i, [128, 1024], F32, "xres2") for i in range(2)]
    y2 = [at(8 + 4 * i, [128, 1024], F32, "y2") for i in range(2)]
    o2 = [at(16 + 4 * i, [128, 1024], F32, "o2") for i in range(2)]
    hb2 = [at(24 + 2 * i, [128, 1024], BF16, "hb2") for i in range(2)]
    if L == 0:
        X1T = at(172.5, [128, 8, NT], BF16, "X1T")
    P.dma("sp", lambda e: e.dma_start(out=ln2g[:, :], in_=lng[2][:, :]), r=[], w=["lnp"], key="k_c0")
    P.dma("sp", lambda e: e.dma_start(out=ln2b[:, :], in_=lng[3][:, :]), r=[], w=["lnp"], key="k_c0")
    def pre3(i):
        b = i % 2
        P.dma("sp", lambda e, i=i, b=b: e.dma_start(out=xres2[b][:, :], in_=T.hmid[i * 128:(i + 1) * 128, :]), r=["hmid"], w=[("xres2", b)], key="k_xr%d" % b)
        for hf in range(2):
            pb = 2 * b + hf
            for j in range(22):
                P.pe(lambda e, j=j, hf=hf, pb=pb, i=i: e.matmul(ps[pb][:, :], lhsT=HT[:, j, i * 128:(i + 1) * 128], rhs=wd_b[:, j, hf * 512:(hf + 1) * 512], start=(j == 0), stop=(j == 21)),
                     r=["HT", "wd_b"], w=[("ps", pb)])

    def y3(i):
        b = i % 2
        for hf in range(2):
            pb = 2 * b + hf
            P.dve(lambda e, hf=hf, pb=pb, b=b: e.scalar_tensor_tensor(out=y2[b][:, hf * 512:(hf + 1) * 512], in0=xres2[b][:, hf * 512:(hf + 1) * 512], scalar=ALPHA, in1=ps[pb][:, :], op0=ALU.mult, op1=ALU.add),
                  r=[("xres2", b), ("ps", pb)], w=[("y2", b)])

    def post3(i):
        b = i % 2
        _layernorm(P, C, y2[b], o2[b], ln2g, ln2b, st[b], mv[b], ("ln", b), ("y2", b), ("o2", b), eng2=("dve" if L == 0 else "pool"))
        P.dma("sp", lambda e, i=i, b=b: e.dma_start(out=out_dst[i * 128:(i + 1) * 128, :], in_=o2[b][:, :]), r=[("o2", b)], w=["out_dst"], key="k_o2%d" % b)
        if L == 0:
            P.act(lambda e, b=b: e.copy(out=hb2[b][:, :], in_=o2[b][:, :]), r=[("o2", b)], w=[("hb2", b)])
            pT = ps[4 + b][:, :].bitcast(BF16)
            for j in range(8):
                P.pe(lambda e, j=j, b=b, pT=pT: e.transpose(pT[:, j * 128:(j + 1) * 128], hb2[b][:, j * 128:(j + 1) * 128], ident[:, :]),
                     r=[("hb2", b), "ident"], w=[("ps", 4 + b)])
            P.act(lambda e, i=i, b=b, pT=pT: e.copy(out=X1T[:, :, i * 128:(i + 1) * 128], in_=pT.rearrange("p (k t) -> p k t", t=128)),
                  r=[("ps", 4 + b)], w=["X1T"])
    def ship3(c):
        for k in range(8):
            P.dma("sp", lambda e, k=k, c=c: e.dma_start(out=T.x1T_in[c][k * 128:(k + 1) * 128, :], in_=X1T[:, k, c * 512:(c + 1) * 512]), r=["X1T"], w=[("x1T_in", c)], key="k_x1st%d" % c)
        P.add("pool", lambda e, c=c: e.collective_compute("AllGather", ALU.bypass, replica_groups=[[0, 1, 2, 3], [4, 5, 6, 7]],
                                                           ins=[T.x1T_in[c].ap()], outs=[T.x1T_all[c].ap()]),
              reads=[("x1T_in", c)], writes=["x1T_all"], kind="cc", key="k_cc")

    for i in range(16):
        pre3(i)
        if i >= 1:
            post3(i - 1)
            if L == 0 and (i - 1) % 4 == 3:
                ship3((i - 1) // 4)
        y3(i)
    post3(15)
    if L == 0:
        ship3(3)
    P.barrier()


NLAYERS = 2


def build(nlayers=NLAYERS, phases=None):
    nc = bass.Bass("TRN2", target_bir_lowering=False)
    specs = dict(x=([NT, DM], F32), w_in0=([DM, 704], F32), g_q=([128, 384], F32), g_kv=([128, 256], F32),
                 cs_tok=([128, 16, 64], F32), cc_f=([64, SEQ], F32), ss_f=([64, SEQ], F32), w_uq_c=([384, 512], F32),
                 w_ukv_c=([256, 512], F32), w_o0=([DM, DM], F32), w_o1=([DM, DM], F32), ident_in=([128, 128], BF16), identf_in=([128, 128], F32),
                 w1_c=([DM, 768], F32), lamv=([128, 4, 64], F32), g_sub=([128, 128], F32), F1=([128, 2, 128], F32), qaug=([2, 16, 2, 2, 512], BF16), kaug=([2, 2, SEQ], BF16), ones2=([2, SEQ], BF16))
    for l in range(2):
        for nm in ("ln_mix_g", "ln_mix_b", "ln_ffn_g", "ln_ffn_b"):
            specs["%s%d" % (nm, l)] = ([128, DM], F32)
        specs["w_gu%d" % l] = ([DM, 2 * DFF], F32)
        specs["w_dn%d" % l] = ([DFF, DM], F32)

    class Lazy:
        def __init__(self):
            self.__dict__["names"] = []

        def __getattr__(self, name):
            if name in specs:
                t = nc.dram_tensor(name, list(specs[name][0]), specs[name][1], kind="ExternalInput")
                self.__dict__[name] = t
                self.names.append(name)
                return t
            if name in ("ln_mix_g", "ln_mix_b", "ln_ffn_g", "ln_ffn_b", "w_gu", "w_dn"):
                outer = self

                class Idx:
                    def __getitem__(self, l):
                        return getattr(outer, "%s%d" % (name, l))
                return Idx()
            raise AttributeError(name)

    T = Lazy()
    T.out = nc.dram_tensor("out", [NT, DM], F32, kind="ExternalOutput")
    T.latT_in = [nc.dram_tensor("latT_in%d" % k, [704, 512], BF16) for k in range(4)]
    T.latT_all = [nc.dram_tensor("latT_all%d" % k, [4 * 704, 512], BF16) for k in range(4)]
    T.OT_in = nc.dram_tensor("OT_in", [4, 256, NT], BF16)
    T.OT_all = nc.dram_tensor("OT_all", [4, 1024, NT], BF16)
    T.hmid = nc.dram_tensor("hmid", [NT, DM], F32)
    if nlayers == 1:
        T.h1 = T.out
    else:
        T.h1 = nc.dram_tensor("h1", [NT, DM], F32)
    T.x1T_in = [nc.dram_tensor("x1T_in%d" % k, [DM, 512], BF16) for k in range(4)]
    T.x1T_all = [nc.dram_tensor("x1T_all%d" % k, [4 * DM, 512], BF16) for k in range(4)]

    C = Ctx()
    C.cn = 0
    C.ps = [nc.alloc_psum_tensor("ps%d" % i, [128, 512], F32) for i in range(8)]
    C.ident = nc.alloc_sbuf_tensor_at("ident", [128, 128], BF16, offset=SB_TOP + 768)
    C.ident_f = nc.alloc_sbuf_tensor_at("ident_f", [128, 128], F32, offset=SB_TOP + 256)
    C.ones_f = nc.alloc_sbuf_tensor_at("ones_f", [128, 16], F32, offset=SB_TOP + 128)
    P = Prog(nc)
    _ = (T.ident_in, T.x, T.identf_in)
    P.dma("sp", lambda e: e.dma_start(out=C.ident[:, :], in_=T.ident_in[:, :]), r=[], w=["ident"], key="k_id")
    P.dma("sp", lambda e: e.dma_start(out=C.ident_f[:, :], in_=T.identf_in[:, :]), r=[], w=["ident"], key="k_id")
    P.pool(lambda e: e.memset(C.ones_f[:, :], 1.0), r=[], w=["ones_f"])
    if phases is None:
        phases = ["a0", "b0", "c0"] + (["b1", "c1"] if nlayers == 2 else [])
    if "a0" in phases:
        phase_a0(nc, P, C, T)
    if "b0" in phases:
        phase_b0(nc, P, C, T)
    if "c0" in phases:
        phase_c(nc, P, C, T, 0)
    if "b1" in phases:
        phase_b1(nc, P, C, T)
    if "c1" in phases:
        phase_c(nc, P, C, T, 1)
    if ("c1" if nlayers == 2 else "c0") not in phases:
        P.dma("sp", lambda e: e.dma_start(out=T.out[0:128, :], in_=T.x[0:128, :]), r=[], w=["out_dst"], key="k_c0")
    n = P.emit()
    nc.input_names = T.names
    return nc, n


def _host_inputs(inputs):
    f32 = np.float32
    x = np.ascontiguousarray(inputs["x"], dtype=f32).reshape(BATCH * SEQ, DM)
    rep = lambda v, n=128: np.ascontiguousarray(np.broadcast_to(np.asarray(v, dtype=f32).reshape(1, -1), (n, np.asarray(v).size)))
    inv_freq = (np.float32(10000.0) ** (-np.arange(0, 64, 2, dtype=f32) / np.float32(64))).astype(f32)
    ang = (np.arange(SEQ, dtype=f32)[:, None] * inv_freq[None, :]).astype(f32)
    cos, sin = np.cos(ang).astype(f32), np.sin(ang).astype(f32)
    cc_f = np.ascontiguousarray(np.concatenate([cos.T, cos.T], 0))
    ss_f = np.ascontiguousarray(np.concatenate([-sin.T, sin.T], 0))
    w_uq = np.asarray(inputs["mla_w_uq"][0], dtype=f32)
    w_ukv = np.asarray(inputs["mla_w_ukv"][0], dtype=f32)
    w_in1 = np.asarray(inputs["diff_w_in"][0], dtype=f32)
    ident = np.eye(128, dtype=f32).astype(ml_dtypes.bfloat16)
    lamv = np.stack([rep(inputs["diff_lam_q1"][0]), rep(inputs["diff_lam_k1"][0]), rep(inputs["diff_lam_q2"][0]), rep(inputs["diff_lam_k2"][0])], 1)
    common = dict(
        w_in0=np.ascontiguousarray(inputs["mla_w_in"][0], dtype=f32),
        g_q=rep(inputs["mla_g_q"][0]), g_kv=rep(inputs["mla_g_kv"][0]),
        cc_f=cc_f, ss_f=ss_f,
        w_o0=np.ascontiguousarray(inputs["mla_w_o"][0], dtype=f32),
        w_o1=np.ascontiguousarray(inputs["diff_w_o"][0], dtype=f32),
        ident_in=ident, identf_in=np.eye(128, dtype=f32), lamv=np.ascontiguousarray(lamv), g_sub=rep(inputs["diff_g_sub"][0]),
    )
    for l in range(2):
        common["ln_mix_g%d" % l] = rep(inputs["ln_mix_g"][l])
        common["ln_mix_b%d" % l] = rep(inputs["ln_mix_b"][l])
        common["ln_ffn_g%d" % l] = rep(inputs["ln_ffn_g"][l])
        common["ln_ffn_b%d" % l] = rep(inputs["ln_ffn_b"][l])
        common["w_gu%d" % l] = np.ascontiguousarray(inputs["ffn_w_gu"][l], dtype=f32)
        common["w_dn%d" % l] = np.ascontiguousarray(inputs["ffn_w_down"][l], dtype=f32)
    kk = np.arange(128, dtype=np.float64)
    maps = []
    for c in range(NCORES):
        g = c % 4
        m = dict(common)
        m["x"] = np.ascontiguousarray(x[c * NT:(c + 1) * NT])
        pos = g * NT + np.arange(NT)
        cs = np.concatenate([cos[pos], sin[pos]], 1)
        m["cs_tok"] = np.ascontiguousarray(cs.reshape(16, 128, 64).transpose(1, 0, 2))
        cols_q, cols_kv_k, cols_kv_v, cq1, ck1, cv1 = [], [], [], [], [], []
        for hl in range(2):
            h = 2 * g + hl
            b0 = h * 192
            cols_q += list(range(b0, b0 + 128)) + list(range(b0 + 128, b0 + 192)) + list(range(b0 + 160, b0 + 192)) + list(range(b0 + 128, b0 + 160))
            cols_kv_k += list(range(h * 256, h * 256 + 128))
            cols_kv_v += list(range(h * 256 + 128, h * 256 + 256))
            cq1 += list(range(h * 128, (h + 1) * 128))
            ck1 += list(range(1024 + h * 128, 1024 + (h + 1) * 128))
            cv1 += list(range(2048 + h * 128, 2048 + (h + 1) * 128))
        m["w_uq_c"] = np.ascontiguousarray(w_uq[:, cols_q])
        m["w_ukv_c"] = np.ascontiguousarray(w_ukv[:, cols_kv_k + cols_kv_v])
        m["w1_c"] = np.ascontiguousarray(w_in1[:, cq1 + ck1 + cv1])
        F1 = np.zeros((128, 2, 128), f32)
        qaug = np.zeros((2, 16, 2, 2, 512), f32)
        kaug = np.zeros((2, 2, SEQ), f32)
        qq = np.arange(512)
        kp = np.arange(SEQ)
        for hl in range(2):
            slope = 2.0 ** (-(2 * g + hl + 1))
            K_, Q_ = np.meshgrid(np.arange(128), np.arange(128), indexing="ij")
            same = (K_ // 64) == (Q_ // 64)
            Fm = np.where(K_ // 64 > Q_ // 64, 0.0, np.where(same & (K_ > Q_), np.exp(-2.0 * slope * (K_ - Q_)), 1.0))
            F1[:, hl, :] = Fm
            kaug[hl, 0] = 8.0 * slope * (kp % 128)
            kaug[hl, 1] = 8.0 * slope * 128.0 * (kp // 128)
            for t in range(16):
                qaug[0, t, :, hl, :] = -8.0 * slope * (qq % 256)
                qaug[1, t, :, hl, :] = -8.0 * slope * 256.0 * (qq // 256 + 2 * t)
        m["F1"], m["qaug"], m["kaug"] = F1, qaug.astype(ml_dtypes.bfloat16), kaug.astype(ml_dtypes.bfloat16)
        m["ones2"] = np.ones((2, SEQ), f32).astype(ml_dtypes.bfloat16)
        maps.append(m)
    return maps


_CACHE = {}


def kernel(**inputs):
    maps = _host_inputs(inputs)
    if "nc" not in _CACHE:
        _CACHE["nc"] = build(NLAYERS)[0]
    nc = _CACHE["nc"]
    res = run_bass_kernel_spmd(nc, maps, core_ids=list(range(NCORES)))
    out = np.concatenate([np.asarray(res.results[c]["out"], dtype=np.float32) for c in range(NCORES)], 0)
    return out.reshape(BATCH, SEQ, DM)
```

```python
import numpy as np
import ml_dtypes
import concourse.bass as bass
import concourse.mybir as mybir
from concourse.bass_utils import run_bass_kernel_spmd

F32 = mybir.dt.float32
BF16 = mybir.dt.bfloat16
ALU = mybir.AluOpType
AF = mybir.ActivationFunctionType

NCORES = 8


class Prog:
    ENGS = ("pe", "act", "dve", "pool", "sp")

    def __init__(self, nc):
        self.nc = nc
        self.ops = []
        self.last_write = {}
        self.readers = {}
        self._n = 0

    def add(self, eng, fn, reads=(), writes=(), kind="c", key=None):
        idx = len(self.ops)
        raw, war = set(), set()
        for t in reads:
            w = self.last_write.get(t)
            if w is not None:
                raw.add(w)
        for t in writes:
            w = self.last_write.get(t)
            if w is not None:
                war.add(w)
            r = self.readers.get(t)
            if r:
                war.update(r[0].values())
                war.update(r[1])
        for t in writes:
            self.last_write[t] = idx
            self.readers[t] = ({}, [])
        for t in reads:
            r = self.readers.setdefault(t, ({}, []))
            if kind == "c":
                r[0][eng] = idx
            else:
                r[1].append(idx)
        raw.discard(idx)
        war.discard(idx)
        if kind != "c" and key is None:
            key = "dma_%s" % eng
        self.ops.append(dict(eng=eng, fn=fn, raw=raw, war=war - raw, kind=kind, key=key, sig=False))
        return idx

    def pe(self, fn, r=(), w=()):
        return self.add("pe", fn, r, w)

    def act(self, fn, r=(), w=()):
        return self.add("act", fn, r, w)

    def dve(self, fn, r=(), w=()):
        return self.add("dve", fn, r, w)

    def pool(self, fn, r=(), w=()):
        return self.add("pool", fn, r, w)

    def dma(self, eng, fn, r=(), w=(), key=None):
        return self.add(eng, fn, r, w, kind="d", key=key)

    def _needed(self, o, d, is_raw):
        if d["kind"] != "c":
            return True
        if o["kind"] != "c":
            return True
        if d["eng"] != o["eng"]:
            return True
        if o["eng"] == "pe":
            return False
        return is_raw

    def emit(self):
        nc = self.nc
        ops = self.ops
        for o in ops:
            dl = []
            for di in o["raw"]:
                if self._needed(o, ops[di], True):
                    dl.append(di)
            for di in o["war"]:
                if self._needed(o, ops[di], False):
                    dl.append(di)
            o["dl"] = dl
            for di in dl:
                ops[di]["sig"] = True
        cnt = {}
        for o in ops:
            if o["kind"] == "bar":
                continue
            if o["kind"] == "c":
                if o["sig"]:
                    k = "eng_" + o["eng"]
                    cnt[k] = cnt.get(k, 0) + 1
                    o["sv"] = (k, cnt[k])
            else:
                k = o["key"]
                inc = 16 if o["kind"] == "d" else 1
                cnt[k] = cnt.get(k, 0) + inc
                o["sv"] = (k, cnt[k])
                o["inc"] = inc
        sems = {k: nc.alloc_semaphore("s_" + k) for k in cnt}
        self.sem_final = cnt
        per_eng = {e: [] for e in self.ENGS}
        for o in ops:
            per_eng[o["eng"]].append(o)
        handles = dict(pe="tensor", act="scalar", dve="vector", pool="gpsimd", sp="sync")

        def run_engine(ename):
            def body(eng):
                known = {}
                if ename == "sp":
                    self.pid = eng.partition_id()
                for o in per_eng[ename]:
                    need = {}
                    for di in o["dl"]:
                        k, v = ops[di]["sv"]
                        if need.get(k, 0) < v:
                            need[k] = v
                    for k, v in need.items():
                        if known.get(k, 0) < v:
                            eng.wait_ge(sems[k], v)
                            known[k] = v
                    if o["kind"] == "bar":
                        continue
                    ins = o["fn"](eng)
                    if o["kind"] == "c":
                        if o["sig"]:
                            ins.then_inc(sems[o["sv"][0]], 1)
                    else:
                        ins.then_inc(sems[o["sv"][0]], o["inc"])
                if ename in self.final_wait_engs:
                    for k, v in cnt.items():
                        if known.get(k, 0) < v:
                            eng.wait_ge(sems[k], v)
            return body

        self.final_wait_engs = ("sp",)
        with nc.Block() as block:
            for ename in self.ENGS:
                if per_eng[ename] or ename in self.final_wait_engs:
                    getattr(block, handles[ename])(run_engine(ename))
        return len(ops)

    def barrier(self):
        last = {}
        dmas = []
        for i, o in enumerate(self.ops):
            if o["kind"] == "c":
                last[o["eng"]] = i
            elif o["kind"] in ("d", "cc") and i >= getattr(self, "_bar_from", 0):
                dmas.append(i)
        deps = set(last.values()) | set(dmas)
        self._bar_from = len(self.ops)
        for e in self.ENGS:
            self.ops.append(dict(eng=e, fn=None, raw=set(deps), war=set(), kind="bar", key=None, sig=False))
        self.last_write = {}
        self.readers = {}


SEQ, BATCH, DM = 8192, 2, 1024
NT = 2048
DFF = 2816
ALPHA = 4.0 ** 0.25
LN_EPS = 1e-5
RMS_EPS = 1e-6
SC0 = 192.0 ** -0.5
SC1 = 0.125
LAMBDA_INIT = 0.8 - 0.6 * float(np.exp(-0.3))
SB_BASE, SB_TOP = 16512, 229344 - 1024


class SBA:
    def __init__(self, nc):
        self.nc, self.off, self.n = nc, SB_BASE, 0

    def alloc(self, shape, dtype, name="t"):
        sz = int(np.prod(shape[1:])) * (2 if dtype == BF16 else 4)
        sz = (sz + 63) // 64 * 64
        assert self.off + sz <= SB_TOP, ("SBUF overflow", name, self.off, sz)
        t = self.nc.alloc_sbuf_tensor_at("%s_%d" % (name, self.n), list(shape), dtype, offset=self.off)
        self.off += sz
        self.n += 1
        return t


class Ctx:
    pass


def _transposes(P, ps_bf, src, n, ident, rtok, wtok, width=128):
    for j in range(n):
        P.pe(lambda e, j=j: e.transpose(ps_bf[0:width, j * 128:(j + 1) * 128], src[:, j * width:(j + 1) * width] if width == 128 else src, ident[:, :]),
             r=[rtok, "ident"], w=[wtok])


def load_weight(P, C, dst, src, K, F, wtok, srctok, cast_eng="pool", q="sp"):
    kmax = max(1, C.stg_elems // F)
    k0 = 0
    while k0 < K:
        ks = min(kmax, K - k0)
        s = C.stg_i % len(C.stg)
        C.stg_i += 1
        st = C.stg[s]
        stv = st[:, 0:ks * F].rearrange("p (k f) -> p k f", f=F)
        P.dma(q, lambda e, stv=stv, k0=k0, ks=ks: e.dma_start(out=stv, in_=src[:, k0:k0 + ks, :]),
              r=[srctok], w=[("stg", s)], key="k_stg%d" % s)
        P.add(cast_eng, lambda e, stv=stv, k0=k0, ks=ks: e.tensor_copy(out=dst[:, k0:k0 + ks, :], in_=stv),
              reads=[("stg", s)], writes=[wtok])
        k0 += ks


def phase_a0(nc, P, C, T):
    _ = (T.x, T.g_q, T.g_kv, T.cs_tok, T.w_in0)
    sb = SBA(nc)
    ps = C.ps
    ident = C.ident
    w_in_b = sb.alloc([128, 8, 704], BF16, "w_in_b")
    gq = sb.alloc([128, 384], F32, "gq")
    gkv = sb.alloc([128, 256], F32, "gkv")
    cs = sb.alloc([128, 16, 64], F32, "cs")
    latT = sb.alloc([128, 6, NT], BF16, "latT")
    C.stg = [sb.alloc([128, 4096], F32, "stg") for _ in range(2)]
    C.stg_elems, C.stg_i = 4096, 0
    xf = [sb.alloc([128, 1024], F32, "xf") for _ in range(2)]
    xb = [sb.alloc([128, 1024], BF16, "xb") for _ in range(2)]
    xT = [sb.alloc([128, 8, 128], BF16, "xT") for _ in range(2)]
    lat = [sb.alloc([128, 704], BF16, "lat") for _ in range(2)]
    kr = [sb.alloc([128, 64], F32, "kr") for _ in range(2)]
    tmp = [sb.alloc([128, 4, 32], F32, "tmp") for _ in range(2)]
    junk = sb.alloc([128, 384], F32, "junk")
    st = [sb.alloc([128, 8], F32, "st") for _ in range(2)]

    P.dma("sp", lambda e: e.dma_start(out=gq[:, :], in_=T.g_q[:, :]), r=[], w=["constA"], key="k_c0")
    P.dma("sp", lambda e: e.dma_start(out=gkv[:, :], in_=T.g_kv[:, :]), r=[], w=["constA"], key="k_c0")
    P.dma("sp", lambda e: e.dma_start(out=cs[:, :, :], in_=T.cs_tok[:, :, :]), r=[], w=["constA"], key="k_c0")
    load_weight(P, C, w_in_b, T.w_in0.ap().rearrange("(k p) f -> p k f", p=128), 8, 704, "w_in_b", "w_in0")

    def pre(i):
        b = i % 2
        P.dma("sp", lambda e, i=i, b=b: e.dma_start(out=xf[b][:, :], in_=T.x[i * 128:(i + 1) * 128, :]),
              r=[], w=[("xf", b)], key="k_xf%d" % b)
        P.act(lambda e, b=b: e.copy(out=xb[b][:, :], in_=xf[b][:, :]), r=[("xf", b)], w=[("xb", b)])
        pT = ps[4 + b][:, :].bitcast(BF16)
        for j in range(8):
            P.pe(lambda e, j=j, b=b, pT=pT: e.transpose(pT[:, j * 128:(j + 1) * 128], xb[b][:, j * 128:(j + 1) * 128], ident[:, :]),
                 r=[("xb", b), "ident"], w=[("ps", 4 + b)])
        P.dve(lambda e, b=b, pT=pT: e.tensor_copy(out=xT[b][:, :, :], in_=pT.rearrange("p (k t) -> p k t", t=128)),
              r=[("ps", 4 + b)], w=[("xT", b)])
        ph1, ph2 = ps[b], ps[2 + b]
        for k in range(8):
            P.pe(lambda e, k=k, b=b, ph1=ph1: e.matmul(ph1[:, 0:384], lhsT=xT[b][:, k, :], rhs=w_in_b[:, k, 0:384], start=(k == 0), stop=(k == 7)),
                 r=[("xT", b), "w_in_b"], w=[("ps", b)])
        for k in range(8):
            P.pe(lambda e, k=k, b=b, ph2=ph2: e.matmul(ph2[:, 0:320], lhsT=xT[b][:, k, :], rhs=w_in_b[:, k, 384:704], start=(k == 0), stop=(k == 7)),
                 r=[("xT", b), "w_in_b"], w=[("ps", 2 + b)])

    def post(i):
        b = i % 2
        ph1, ph2 = ps[b], ps[2 + b]
        P.act(lambda e, b=b, ph1=ph1: e.activation(out=junk[:, 0:384], in_=ph1[:, 0:384], func=AF.Square, accum_out=st[b][:, 0:1]),
              r=[("ps", b)], w=["junk", ("st", b)])
        P.act(lambda e, b=b, ph2=ph2: e.activation(out=junk[:, 0:256], in_=ph2[:, 0:256], func=AF.Square, accum_out=st[b][:, 1:2]),
              r=[("ps", 2 + b)], w=["junk", ("st", b)])
        P.act(lambda e, b=b, ph2=ph2: e.copy(out=kr[b][:, :], in_=ph2[:, 256:320]), r=[("ps", 2 + b)], w=[("kr", b)])
        P.dve(lambda e, b=b: e.tensor_scalar(out=st[b][:, 2:3], in0=st[b][:, 0:1], scalar1=1.0 / 384, scalar2=RMS_EPS, op0=ALU.mult, op1=ALU.add),
              r=[("st", b)], w=[("st2", b)])
        P.dve(lambda e, b=b: e.tensor_scalar(out=st[b][:, 3:4], in0=st[b][:, 1:2], scalar1=1.0 / 256, scalar2=RMS_EPS, op0=ALU.mult, op1=ALU.add),
              r=[("st", b), ("st2", b)], w=[("st2", b)])
        P.act(lambda e, b=b: e.activation(out=st[b][:, 4:6], in_=st[b][:, 2:4], func=AF.Sqrt), r=[("st2", b)], w=[("st3", b)])
        P.dve(lambda e, b=b: e.reciprocal(out=st[b][:, 6:8], in_=st[b][:, 4:6]), r=[("st3", b)], w=[("st4", b)])
        P.dve(lambda e, b=b, ph1=ph1: e.scalar_tensor_tensor(out=lat[b][:, 0:384], in0=ph1[:, 0:384], scalar=st[b][:, 6:7], in1=gq[:, :], op0=ALU.mult, op1=ALU.mult),
              r=[("ps", b), ("st4", b), "constA"], w=[("lat", b)])
        P.dve(lambda e, b=b, ph2=ph2: e.scalar_tensor_tensor(out=lat[b][:, 384:640], in0=ph2[:, 0:256], scalar=st[b][:, 7:8], in1=gkv[:, :], op0=ALU.mult, op1=ALU.mult),
              r=[("ps", 2 + b), ("st4", b), "constA"], w=[("lat", b)])
        cosv, sinv = cs[:, i, 0:32], cs[:, i, 32:64]
        P.dve(lambda e, b=b, cosv=cosv: e.tensor_tensor(out=tmp[b][:, 0, :], in0=kr[b][:, 0:32], in1=cosv, op=ALU.mult), r=[("kr", b), "constA"], w=[("tmp", b)])
        P.dve(lambda e, b=b, sinv=sinv: e.tensor_tensor(out=tmp[b][:, 1, :], in0=kr[b][:, 32:64], in1=sinv, op=ALU.mult), r=[("kr", b), "constA", ("tmp", b)], w=[("tmp", b)])
        P.dve(lambda e, b=b, sinv=sinv: e.tensor_tensor(out=tmp[b][:, 2, :], in0=kr[b][:, 0:32], in1=sinv, op=ALU.mult), r=[("kr", b), "constA", ("tmp", b)], w=[("tmp", b)])
        P.dve(lambda e, b=b, cosv=cosv: e.tensor_tensor(out=tmp[b][:, 3, :], in0=kr[b][:, 32:64], in1=cosv, op=ALU.mult), r=[("kr", b), "constA", ("tmp", b)], w=[("tmp", b)])
        P.dve(lambda e, b=b: e.tensor_tensor(out=lat[b][:, 640:672], in0=tmp[b][:, 0, :], in1=tmp[b][:, 1, :], op=ALU.subtract), r=[("tmp", b)], w=[("lat", b)])
        P.dve(lambda e, b=b: e.tensor_tensor(out=lat[b][:, 672:704], in0=tmp[b][:, 2, :], in1=tmp[b][:, 3, :], op=ALU.add), r=[("tmp", b), ("lat", b)], w=[("lat", b)])
        pT2 = ps[6 + b][:, :].bitcast(BF16)
        for j in range(5):
            P.pe(lambda e, j=j, b=b, pT2=pT2: e.transpose(pT2[:, j * 128:(j + 1) * 128], lat[b][:, j * 128:(j + 1) * 128], ident[:, :]),
                 r=[("lat", b), "ident"], w=[("ps", 6 + b)])
        P.pe(lambda e, b=b, pT2=pT2: e.transpose(pT2[0:64, 640:768], lat[b][:, 640:704], ident[:, :]),
             r=[("lat", b), "ident"], w=[("ps", 6 + b)])
        P.act(lambda e, i=i, b=b, pT2=pT2: e.copy(out=latT[:, 0:5, i * 128:(i + 1) * 128], in_=pT2[:, 0:640].rearrange("p (k t) -> p k t", t=128)),
              r=[("ps", 6 + b)], w=["latT"])
        P.act(lambda e, i=i, b=b, pT2=pT2: e.copy(out=latT[0:64, 5, i * 128:(i + 1) * 128], in_=pT2[0:64, 640:768]),
              r=[("ps", 6 + b)], w=["latT"])
    def ship(c):
        for j in range(5):
            P.dma("sp", lambda e, j=j, c=c: e.dma_start(out=T.latT_in[c][j * 128:(j + 1) * 128, :], in_=latT[:, j, c * 512:(c + 1) * 512]), r=["latT"], w=[("latT_in", c)], key="k_lst%d" % c)
        P.dma("sp", lambda e, c=c: e.dma_start(out=T.latT_in[c][640:704, :], in_=latT[0:64, 5, c * 512:(c + 1) * 512]), r=["latT"], w=[("latT_in", c)], key="k_lst%d" % c)
        P.add("pool", lambda e, c=c: e.collective_compute("AllGather", ALU.bypass, replica_groups=[[0, 1, 2, 3], [4, 5, 6, 7]],
                                                           ins=[T.latT_in[c].ap()], outs=[T.latT_all[c].ap()]),
              reads=[("latT_in", c)], writes=["latT_all"], kind="cc", key="k_cc")

    for i in range(16):
        pre(i)
        if i >= 1:
            post(i - 1)
            if (i - 1) % 4 == 3:
                ship((i - 1) // 4)
    post(15)
    ship(3)
    P.barrier()


def phase_b0(nc, P, C, T):
    _ = (T.cc_f, T.ss_f, T.w_uq_c, T.w_ukv_c)
    sb = SBA(nc)
    ps = C.ps
    ident = C.ident
    wq = sb.alloc([128, 3, 512], BF16, "wq")
    wkv = sb.alloc([128, 2, 512], BF16, "wkv")
    KnT = [sb.alloc([128, SEQ], BF16, "KnT") for _ in range(2)]
    KrT = sb.alloc([128, SEQ], BF16, "KrT")
    Vaug = sb.alloc([128, 2, 64, 129], BF16, "Vaug")
    C.stg = [sb.alloc([128, 2048], F32, "stg") for _ in range(2)]
    C.stg_elems, C.stg_i = 2048, 0
    cqT = [sb.alloc([128, 3, 512], BF16, "cqT") for _ in range(2)]
    ckvT = [sb.alloc([128, 2, 512], BF16, "ckvT") for _ in range(2)]
    ccf = [sb.alloc([64, 512], F32, "ccf") for _ in range(2)]
    ssf = [sb.alloc([64, 512], F32, "ssf") for _ in range(2)]
    QnT = [[sb.alloc([128, 512], BF16, "QnT") for _ in range(2)] for _ in range(2)]
    QrT = [[sb.alloc([128, 512], BF16, "QrT") for _ in range(2)] for _ in range(2)]
    r1 = [sb.alloc([64, 512], F32, "r1") for _ in range(2)]
    r2 = [sb.alloc([64, 512], F32, "r2") for _ in range(2)]
    PT = [sb.alloc([128, 512], BF16, "PT") for _ in range(6)]
    rl = [sb.alloc([128, 4], F32, "rl") for _ in range(2)]
    ob = [sb.alloc([128, 128], BF16, "ob") for _ in range(2)]
    OTs = [sb.alloc([128, 512], BF16, "OTs") for _ in range(2)]
    Pacc = [sb.alloc([128, 512], F32, "Pacc") for _ in range(2)]
    OTf = [sb.alloc([128, 512], F32, "OTf") for _ in range(2)]

    load_weight(P, C, wq, T.w_uq_c.ap().rearrange("(k p) f -> p k f", p=128), 3, 512, "wq", "w_uq_c")
    load_weight(P, C, wkv, T.w_ukv_c.ap().rearrange("(k p) f -> p k f", p=128), 2, 512, "wkv", "w_ukv_c")
    P.pool(lambda e: e.memset(Vaug[:, :, :, 128:129], 1.0), r=[], w=["Vones"])
    P.pool(lambda e: e.memset(KrT[64:128, :], 0.0), r=[], w=["Kzero"])
    for bt_ in range(2):
        for hl_ in range(2):
            P.pool(lambda e, bt_=bt_, hl_=hl_: e.memset(QrT[bt_][hl_][64:128, :], 0.0), r=[], w=[("Qzero", bt_, hl_)])
    def lat_rows(r, f0, n):
        k = f0 // 256
        rows_k = (256, 256, 192)[k]
        return T.latT_all[k], r * rows_k + f0 % 256

    for tt_ in range(16):
        P.dma("sp", lambda e, tt_=tt_: e.dma_start(out=KrT[0:64, tt_ * 512:(tt_ + 1) * 512], in_=T.latT_all[tt_ % 4][(tt_ // 4) * 704 + 640:(tt_ // 4) * 704 + 704, :]),
              r=["latT_all"], w=["KrT"], key="k_krt")

    pj = [0]

    def pbank():
        b = 5 + (pj[0] % 3)
        pj[0] += 1
        return b

    sidx = [0]
    for t in range(16):
        bt = t % 2
        r, c0 = t // 4, (t % 4) * 512
        lt = T.latT_all[t % 4]
        P.dma("sp", lambda e, lt=lt, r=r, bt=bt: e.dma_start(out=cqT[bt][:, :, :], in_=lt[r * 704:r * 704 + 384, :].rearrange("(j p) c -> p j c", p=128)),
              r=["latT_all"], w=[("cqT", bt)], key="k_cq%d" % bt)
        P.dma("sp", lambda e, lt=lt, r=r, bt=bt: e.dma_start(out=ckvT[bt][:, :, :], in_=lt[r * 704 + 384:r * 704 + 640, :].rearrange("(j p) c -> p j c", p=128)),
              r=["latT_all"], w=[("ckvT", bt)], key="k_ckv%d" % bt)
        P.dma("sp", lambda e, t=t, bt=bt: e.dma_start(out=ccf[bt][:, :], in_=T.cc_f[:, t * 512:(t + 1) * 512]), r=[], w=[("ccf", bt)], key="k_ccf%d" % bt)
        P.dma("sp", lambda e, t=t, bt=bt: e.dma_start(out=ssf[bt][:, :], in_=T.ss_f[:, t * 512:(t + 1) * 512]), r=[], w=[("ssf", bt)], key="k_ssf%d" % bt)
        for s in range(4):
            pb = pbank()
            for j in range(2):
                P.pe(lambda e, s=s, j=j, pb=pb, bt=bt: e.matmul(ps[pb][:, 0:256], lhsT=ckvT[bt][:, j, s * 128:(s + 1) * 128], rhs=wkv[:, j, 256:512], start=(j == 0), stop=(j == 1)),
                     r=[("ckvT", bt), "wkv"], w=[("ps", pb)])
            P.act(lambda e, s=s, pb=pb, t=t: e.copy(out=Vaug[:, :, 4 * t + s, 0:128], in_=ps[pb][:, 0:256].rearrange("p (h d) -> p h d", d=128)),
                  r=[("ps", pb)], w=[("V", 4 * t + s)])
        for hl in range(2):
            pb = pbank()
            for j in range(2):
                P.pe(lambda e, j=j, pb=pb, bt=bt, hl=hl: e.matmul(ps[pb][:, :], lhsT=wkv[:, j, hl * 128:(hl + 1) * 128], rhs=ckvT[bt][:, j, :], start=(j == 0), stop=(j == 1)),
                     r=[("ckvT", bt), "wkv"], w=[("ps", pb)])
            P.dve(lambda e, pb=pb, hl=hl, t=t: e.tensor_copy(out=KnT[hl][:, t * 512:(t + 1) * 512], in_=ps[pb][:, :]),
                  r=[("ps", pb)], w=[("KnT", hl, t)])
            pb = pbank()
            for j in range(3):
                P.pe(lambda e, j=j, pb=pb, bt=bt, hl=hl: e.matmul(ps[pb][:, :], lhsT=wq[:, j, hl * 256:hl * 256 + 128], rhs=cqT[bt][:, j, :], start=(j == 0), stop=(j == 2)),
                     r=[("cqT", bt), "wq"], w=[("ps", pb)])
            P.act(lambda e, pb=pb, hl=hl, bt=bt: e.copy(out=QnT[bt][hl][:, :], in_=ps[pb][:, :]), r=[("ps", pb)], w=[("QnT", bt, hl)])
            pa = pbank()
            for j in range(3):
                P.pe(lambda e, j=j, pa=pa, bt=bt, hl=hl: e.matmul(ps[pa][0:64, :], lhsT=wq[:, j, hl * 256 + 128:hl * 256 + 192], rhs=cqT[bt][:, j, :], start=(j == 0), stop=(j == 2)),
                     r=[("cqT", bt), "wq"], w=[("ps", pa)])
            P.dve(lambda e, pa=pa, bt=bt, hl=hl: e.tensor_tensor(out=r1[hl][:, :], in0=ps[pa][0:64, :], in1=ccf[bt][:, :], op=ALU.mult),
                  r=[("ps", pa), ("ccf", bt)], w=[("r1", hl)])
            pb2 = pbank()
            for j in range(3):
                P.pe(lambda e, j=j, pb2=pb2, bt=bt, hl=hl: e.matmul(ps[pb2][0:64, :], lhsT=wq[:, j, hl * 256 + 192:hl * 256 + 256], rhs=cqT[bt][:, j, :], start=(j == 0), stop=(j == 2)),
                     r=[("cqT", bt), "wq"], w=[("ps", pb2)])
            P.dve(lambda e, pb2=pb2, bt=bt, hl=hl: e.tensor_tensor(out=r2[hl][:, :], in0=ps[pb2][0:64, :], in1=ssf[bt][:, :], op=ALU.mult),
                  r=[("ps", pb2), ("ssf", bt)], w=[("r2", hl)])
            P.pool(lambda e, bt=bt, hl=hl: e.tensor_tensor(out=QrT[bt][hl][0:64, :], in0=r1[hl][:, :], in1=r2[hl][:, :], op=ALU.add),
                   r=[("r1", hl), ("r2", hl)], w=[("QrT", bt, hl)])
        for hl in range(2):
            nJ = 4 * (t + 1)
            tiles = list(range(nJ))
            meta = {}

            def qk(J, hl=hl, t=t, bt=bt):
                n = sidx[0]
                sidx[0] += 1
                sbk, pbf = n % 3, n % 6
                m = max(0, J - 4 * t)
                q0 = 128 * m
                meta[J] = (sbk, pbf, m, q0)
                P.pe(lambda e: e.matmul(ps[sbk][:, q0:512], lhsT=KnT[hl][:, J * 128:(J + 1) * 128], rhs=QnT[bt][hl][:, q0:512], start=True, stop=False),
                     r=[("KnT", hl, J // 4), ("QnT", bt, hl)], w=[("ps", sbk)])
                P.pe(lambda e: e.matmul(ps[sbk][:, q0:512], lhsT=KrT[:, J * 128:(J + 1) * 128], rhs=QrT[bt][hl][:, q0:512], start=False, stop=True),
                     r=["KrT", "Kzero", ("QrT", bt, hl), ("Qzero", bt, hl)], w=[("ps", sbk)])

            def ex(J, t=t):
                sbk, pbf, m, q0 = meta[J]
                P.act(lambda e: e.activation(out=PT[pbf][:, q0:512], in_=ps[sbk][:, q0:512], func=AF.Exp, scale=SC0),
                      r=[("ps", sbk)], w=[("PT", pbf)])
                if J >= 4 * t:
                    P.pool(lambda e: e.memset(PT[pbf][64:128, q0:q0 + 64], 0.0), r=[("PT", pbf)], w=[("PT", pbf)])

            ehl = (2 * t + hl) % 2
            acc = 3 + ehl
            pa = Pacc[ehl]

            def av(J, hl=hl, nJ=nJ, acc=acc, pa=pa, ehl=ehl):
                sbk, pbf, m, q0 = meta[J]
                P.pe(lambda e: e.matmul(ps[acc][:, q0:512], lhsT=Vaug[:, hl, J, 0:128], rhs=PT[pbf][:, q0:512], start=(J == 0), stop=(J == nJ - 1)),
                     r=[("PT", pbf), ("V", J)], w=[("ps", acc)])
                SPL = 320
                if J == 0:
                    P.dve(lambda e: e.tensor_copy(out=pa[:, 0:SPL], in_=PT[pbf][:, 0:SPL]), r=[("PT", pbf)], w=[("PaccD", ehl)])
                    P.pool(lambda e: e.tensor_copy(out=pa[:, SPL:512], in_=PT[pbf][:, SPL:512]), r=[("PT", pbf)], w=[("PaccP", ehl)])
                else:
                    if q0 < SPL:
                        P.dve(lambda e: e.tensor_tensor(out=pa[:, q0:SPL], in0=pa[:, q0:SPL], in1=PT[pbf][:, q0:SPL], op=ALU.add),
                              r=[("PT", pbf), ("PaccD", ehl)], w=[("PaccD", ehl)])
                    q1 = max(q0, SPL)
                    P.pool(lambda e: e.tensor_tensor(out=pa[:, q1:512], in0=pa[:, q1:512], in1=PT[pbf][:, q1:512], op=ALU.add),
                           r=[("PT", pbf), ("PaccP", ehl)], w=[("PaccP", ehl)])

            qk(tiles[0])
            if nJ > 1:
                qk(tiles[1])
            for i, J in enumerate(tiles):
                ex(J)
                if i + 2 < nJ:
                    qk(tiles[i + 2])
                av(J)
            eb = ehl
            P.act(lambda e, eb=eb, acc=acc: e.copy(out=OTf[eb][:, :], in_=ps[acc][:, :]), r=[("ps", acc)], w=[("OTf", eb)])
            pl = pbank()
            for qs in range(4):
                P.pe(lambda e, qs=qs, pl=pl, pa=pa: e.matmul(ps[pl][:, qs:qs + 1], lhsT=pa[:, qs * 128:(qs + 1) * 128], rhs=C.ones_f[:, 0:1], start=True, stop=True, skip_group_check=True),
                     r=[("PaccD", ehl), ("PaccP", ehl), "ident"], w=[("ps", pl)])
            ptr = pbank()
            for qs in range(4):
                P.pe(lambda e, qs=qs, ptr=ptr, eb=eb: e.transpose(ps[ptr][:, qs * 128:(qs + 1) * 128], OTf[eb][:, qs * 128:(qs + 1) * 128], C.ident_f[:, :]),
                     r=[("OTf", eb), "ident"], w=[("ps", ptr)])
            P.dve(lambda e, eb=eb, pl=pl: e.reciprocal(out=rl[eb][:, 0:4], in_=ps[pl][:, 0:4]), r=[("ps", pl)], w=[("rl", eb)])
            pbT = pbank()
            pT = ps[pbT][:, :].bitcast(BF16)
            for qs in range(4):
                ob_i = qs % 2
                P.dve(lambda e, qs=qs, ptr=ptr, eb=eb, ob_i=ob_i: e.tensor_scalar(out=ob[ob_i][:, :], in0=ps[ptr][:, qs * 128:(qs + 1) * 128], scalar1=rl[eb][:, qs:qs + 1], scalar2=None, op0=ALU.mult),
                      r=[("ps", ptr), ("rl", eb)], w=[("ob", ob_i)])
                P.pe(lambda e, qs=qs, ob_i=ob_i, pT=pT: e.transpose(pT[:, qs * 128:(qs + 1) * 128], ob[ob_i][:, :], ident[:, :]),
                     r=[("ob", ob_i), "ident"], w=[("ps", pbT)])
            P.act(lambda e, eb=eb, pT=pT: e.copy(out=OTs[eb][:, :], in_=pT[:, 0:512]), r=[("ps", pbT)], w=[("OTs", eb)])
            P.dma("sp", lambda e, eb=eb, hl=hl, t=t: e.dma_start(out=T.OT_in[t // 4, hl * 128:(hl + 1) * 128, (t % 4) * 512:(t % 4 + 1) * 512], in_=OTs[eb][:, :]),
                  r=[("OTs", eb)], w=[("OT_in", t // 4, eb)], key="k_ot%d_%d" % (t // 4, eb))
            if t % 4 == 3 and hl == 1:
                P.add("pool", lambda e, c=t // 4: e.collective_compute("AllGather", ALU.bypass, replica_groups=[[0, 1, 2, 3], [4, 5, 6, 7]],
                                                               ins=[T.OT_in[c]], outs=[T.OT_all[c]]),
                      reads=[("OT_in", t // 4, 0), ("OT_in", t // 4, 1)], writes=["OT_all"], kind="cc", key="k_cc")
    P.barrier()


def phase_b1(nc, P, C, T):
    _ = (T.w1_c, T.lamv, T.g_sub, T.F1, T.qaug, T.kaug, T.ones2)
    sb = SBA(nc)
    ps = C.ps
    ident = C.ident
    w1 = sb.alloc([128, 8, 768], BF16, "w1")
    KT = [[sb.alloc([128, SEQ], BF16, "KT") for _ in range(2)] for _ in range(2)]
    Vaug = sb.alloc([128, 2, 64, 129], BF16, "Vaug1")
    C.stg = [sb.alloc([128, 2048], F32, "stg") for _ in range(2)]
    C.stg_elems, C.stg_i = 2048, 0
    xT = [sb.alloc([128, 8, 512], BF16, "xT1") for _ in range(2)]
    QTall = sb.alloc([128, 2, 2, 2, 512], BF16, "QTall")
    QT = [[[QTall[:, bt_, mp_, hl_, :] for hl_ in range(2)] for bt_ in range(2)] for mp_ in range(2)]
    PT = [sb.alloc([128, 512], BF16, "PT1") for _ in range(4)]
    F1 = sb.alloc([128, 2, 128], F32, "F1")
    lamv = sb.alloc([128, 4, 64], F32, "lamv")
    gsub = sb.alloc([128, 128], F32, "gsub")
    lm = sb.alloc([128, 8], F32, "lm")
    junk = sb.alloc([128, 128], F32, "junk1")
    rl = [sb.alloc([128, 8], F32, "rl1") for _ in range(2)]
    o1 = [sb.alloc([128, 128], F32, "o1") for _ in range(2)]
    oo = [sb.alloc([128, 128], F32, "oo") for _ in range(2)]
    ob = [sb.alloc([128, 128], BF16, "ob1") for _ in range(2)]
    OTs = [sb.alloc([128, 512], BF16, "OTs1") for _ in range(2)]

    P.dma("sp", lambda e: e.dma_start(out=F1[:, :, :], in_=T.F1[:, :, :]), r=[], w=["constB"], key="k_c0")
    P.dma("sp", lambda e: e.dma_start(out=lamv[:, :, :], in_=T.lamv[:, :, :]), r=[], w=["constB"], key="k_c0")
    P.dma("sp", lambda e: e.dma_start(out=gsub[:, :], in_=T.g_sub[:, :]), r=[], w=["constB"], key="k_c0")
    for mp in range(2):
        for hl in range(2):
            P.dma("sp", lambda e, mp=mp, hl=hl: e.dma_start(out=KT[mp][hl][64:66, :], in_=T.kaug[hl, :, :]), r=[], w=["constB"], key="k_c0")
            P.dma("sp", lambda e, mp=mp, hl=hl: e.dma_start(out=KT[mp][hl][66:68, :], in_=T.ones2[:, :]), r=[], w=["constB"], key="k_c0")
    P.pool(lambda e: e.memset(QTall[64:66, :, :, :, :], 1.0), r=[], w=["Qones"])
    P.pool(lambda e: e.memset(Vaug[:, :, :, 128:129], 1.0), r=[], w=["Vones"])
    load_weight(P, C, w1, T.w1_c.ap().rearrange("(k p) f -> p k f", p=128), 8, 768, "w1", "w1_c")
    P.dve(lambda e: e.scalar_tensor_tensor(out=junk[:, 0:64], in0=lamv[:, 0, :], scalar=1.0, in1=lamv[:, 1, :], op0=ALU.mult, op1=ALU.mult, accum_out=lm[:, 0:1]),
          r=["constB"], w=["junk", "lm0"])
    P.dve(lambda e: e.scalar_tensor_tensor(out=junk[:, 0:64], in0=lamv[:, 2, :], scalar=1.0, in1=lamv[:, 3, :], op0=ALU.mult, op1=ALU.mult, accum_out=lm[:, 1:2]),
          r=["constB", "junk", "lm0"], w=["junk", "lm0"])
    P.act(lambda e: e.activation(out=lm[:, 2:4], in_=lm[:, 0:2], func=AF.Exp), r=["lm0"], w=["lm1"])
    P.dve(lambda e: e.tensor_tensor(out=lm[:, 4:5], in0=lm[:, 2:3], in1=lm[:, 3:4], op=ALU.subtract), r=["lm1"], w=["lm2"])
    P.dve(lambda e: e.tensor_scalar(out=lm[:, 5:6], in0=lm[:, 4:5], scalar1=-1.0, scalar2=-LAMBDA_INIT, op0=ALU.mult, op1=ALU.add), r=["lm2"], w=["neglam"])
    P.dve(lambda e: e.tensor_scalar(out=gsub[:, :], in0=gsub[:, :], scalar1=1.0 - LAMBDA_INIT, scalar2=None, op0=ALU.mult), r=["constB"], w=["gsub2"])

    pj = [0]

    def pbank():
        b = 6 + (pj[0] % 2)
        pj[0] += 1
        return b

    sidx = [0]
    for t in range(16):
        bt = t % 2
        r, c0 = t // 4, (t % 4) * 512
        P.dma("sp", lambda e, t=t, bt=bt: e.dma_start(out=QTall[66:68, bt, :, :, :], in_=T.qaug[:, t, :, :, :]), r=[], w=[("Qaug", bt)], key="k_qa%d" % bt)
        P.dma("sp", lambda e, r=r, t=t, bt=bt: e.dma_start(out=xT[bt][:, :, :], in_=T.x1T_all[t % 4][r * 1024:(r + 1) * 1024, :].rearrange("(k p) c -> p k c", p=128)),
              r=["x1T_all"], w=[("xT", bt)], key="k_lat%d" % bt)
        for s in range(4):
            pb = pbank()
            for k in range(8):
                P.pe(lambda e, s=s, k=k, pb=pb, bt=bt: e.matmul(ps[pb][:, 0:256], lhsT=xT[bt][:, k, s * 128:(s + 1) * 128], rhs=w1[:, k, 512:768], start=(k == 0), stop=(k == 7)),
                     r=[("xT", bt), "w1"], w=[("ps", pb)])
            P.act(lambda e, s=s, pb=pb, t=t: e.copy(out=Vaug[:, :, 4 * t + s, 0:128], in_=ps[pb][:, 0:256].rearrange("p (h d) -> p h d", d=128)),
                  r=[("ps", pb)], w=[("V", 4 * t + s)])
        for hl in range(2):
            pb = pbank()
            for k in range(8):
                P.pe(lambda e, k=k, pb=pb, bt=bt, hl=hl: e.matmul(ps[pb][:, :], lhsT=w1[:, k, 256 + hl * 128:256 + (hl + 1) * 128], rhs=xT[bt][:, k, :], start=(k == 0), stop=(k == 7)),
                     r=[("xT", bt), "w1"], w=[("ps", pb)])
            P.act(lambda e, pb=pb, hl=hl, t=t: e.copy(out=KT[0][hl][0:64, t * 512:(t + 1) * 512], in_=ps[pb][0:64, :]), r=[("ps", pb)], w=[("K", 0, hl, t)])
            P.dve(lambda e, pb=pb, hl=hl, t=t: e.tensor_copy(out=KT[1][hl][0:64, t * 512:(t + 1) * 512], in_=ps[pb][64:128, :]), r=[("ps", pb)], w=[("K", 1, hl, t)])
            pb = pbank()
            for k in range(8):
                P.pe(lambda e, k=k, pb=pb, bt=bt, hl=hl: e.matmul(ps[pb][:, :], lhsT=w1[:, k, hl * 128:(hl + 1) * 128], rhs=xT[bt][:, k, :], start=(k == 0), stop=(k == 7)),
                     r=[("xT", bt), "w1"], w=[("ps", pb)])
            P.act(lambda e, pb=pb, hl=hl, bt=bt: e.copy(out=QT[0][bt][hl][0:64, :], in_=ps[pb][0:64, :]), r=[("ps", pb)], w=[("Q", 0, bt, hl)])
            P.dve(lambda e, pb=pb, hl=hl, bt=bt: e.tensor_copy(out=QT[1][bt][hl][0:64, :], in_=ps[pb][64:128, :]), r=[("ps", pb)], w=[("Q", 1, bt, hl)])
        for hl in range(2):
            nJ = 4 * (t + 1)
            tiles = [(J, mp) for J in range(nJ) for mp in range(2)]
            meta = {}

            def qk(tl, hl=hl, t=t, bt=bt):
                J, mp = tl
                n = sidx[0]
                sidx[0] += 1
                sbk, pbf = n % 3, n % 4
                m = max(0, J - 4 * t)
                q0 = 128 * m
                meta[tl] = (sbk, pbf, m, q0)
                P.pe(lambda e: e.matmul(ps[sbk][:, q0:512], lhsT=KT[mp][hl][0:68, J * 128:(J + 1) * 128], rhs=QT[mp][bt][hl][0:68, q0:512], start=True, stop=True),
                     r=[("K", mp, hl, J // 4), ("Q", mp, bt, hl), ("Qaug", bt), "Qones", "constB"], w=[("ps", sbk)])

            def ex(tl, hl=hl, t=t):
                J, mp = tl
                sbk, pbf, m, q0 = meta[tl]
                P.act(lambda e: e.activation(out=PT[pbf][:, q0:512], in_=ps[sbk][:, q0:512], func=AF.Exp, scale=SC1),
                      r=[("ps", sbk)], w=[("PT", pbf)])
                if J >= 4 * t:
                    P.pool(lambda e: e.tensor_tensor(out=PT[pbf][:, q0:q0 + 128], in0=PT[pbf][:, q0:q0 + 128], in1=F1[:, hl, :], op=ALU.mult),
                           r=[("PT", pbf), "constB"], w=[("PT", pbf)])

            def av(tl, hl=hl, nJ=nJ):
                J, mp = tl
                sbk, pbf, m, q0 = meta[tl]
                for qs in range(m, 4):
                    a = mp * 4 + qs
                    bank, off = 3 + a // 3, (a % 3) * 129
                    P.pe(lambda e, qs=qs, bank=bank, off=off, a=a: e.matmul(ps[bank][:, off:off + 129], lhsT=PT[pbf][:, qs * 128:(qs + 1) * 128], rhs=Vaug[:, hl, J, :],
                                                                      start=(J == 0 and a % 3 == 0), stop=(J == nJ - 1), skip_group_check=True),
                         r=[("PT", pbf), ("V", J), "Vones"], w=[("ps", bank)])

            qk(tiles[0])
            qk(tiles[1])
            for i, tl in enumerate(tiles):
                ex(tl)
                if i + 2 < len(tiles):
                    qk(tiles[i + 2])
                av(tl)
            eb = (2 * t + hl) % 2
            pbT = pbank()
            pT = ps[pbT][:, :].bitcast(BF16)
            for qs in range(4):
                a1, a2 = qs, 4 + qs
                b1_, f1_ = 3 + a1 // 3, (a1 % 3) * 129
                b2_, f2_ = 3 + a2 // 3, (a2 % 3) * 129
                q2 = qs % 2
                P.dve(lambda e, b1_=b1_, f1_=f1_, q2=q2: e.reciprocal(out=rl[q2][:, 0:1], in_=ps[b1_][:, f1_ + 128:f1_ + 129]), r=[("ps", b1_)], w=[("rl", q2)])
                P.dve(lambda e, b2_=b2_, f2_=f2_, q2=q2: e.reciprocal(out=rl[q2][:, 1:2], in_=ps[b2_][:, f2_ + 128:f2_ + 129]), r=[("ps", b2_), ("rl", q2)], w=[("rl", q2)])
                P.dve(lambda e, q2=q2: e.tensor_tensor(out=rl[q2][:, 2:3], in0=rl[q2][:, 1:2], in1=lm[:, 5:6], op=ALU.mult), r=[("rl", q2), "neglam"], w=[("rl2", q2)])
                P.dve(lambda e, b1_=b1_, f1_=f1_, q2=q2: e.tensor_scalar(out=o1[q2][:, :], in0=ps[b1_][:, f1_:f1_ + 128], scalar1=rl[q2][:, 0:1], scalar2=None, op0=ALU.mult),
                      r=[("ps", b1_), ("rl", q2)], w=[("o1", q2)])
                P.dve(lambda e, b2_=b2_, f2_=f2_, q2=q2: e.scalar_tensor_tensor(out=oo[q2][:, :], in0=ps[b2_][:, f2_:f2_ + 128], scalar=rl[q2][:, 2:3], in1=o1[q2][:, :], op0=ALU.mult, op1=ALU.add),
                      r=[("ps", b2_), ("rl2", q2), ("o1", q2)], w=[("oo", q2)])
                P.act(lambda e, q2=q2: e.activation(out=junk[:, :], in_=oo[q2][:, :], func=AF.Square, accum_out=rl[q2][:, 3:4]), r=[("oo", q2)], w=["junk", ("rl3", q2)])
                P.dve(lambda e, q2=q2: e.tensor_scalar(out=rl[q2][:, 4:5], in0=rl[q2][:, 3:4], scalar1=1.0 / 128, scalar2=RMS_EPS, op0=ALU.mult, op1=ALU.add), r=[("rl3", q2)], w=[("rl4", q2)])
                P.act(lambda e, q2=q2: e.activation(out=rl[q2][:, 5:6], in_=rl[q2][:, 4:5], func=AF.Sqrt), r=[("rl4", q2)], w=[("rl5", q2)])
                P.dve(lambda e, q2=q2: e.reciprocal(out=rl[q2][:, 6:7], in_=rl[q2][:, 5:6]), r=[("rl5", q2)], w=[("rl6", q2)])
                P.dve(lambda e, q2=q2: e.scalar_tensor_tensor(out=ob[q2][:, :], in0=oo[q2][:, :], scalar=rl[q2][:, 6:7], in1=gsub[:, :], op0=ALU.mult, op1=ALU.mult),
                      r=[("oo", q2), ("rl6", q2), "gsub2"], w=[("ob", q2)])
                P.pe(lambda e, qs=qs, q2=q2, pT=pT: e.transpose(pT[:, qs * 128:(qs + 1) * 128], ob[q2][:, :], ident[:, :]), r=[("ob", q2), "ident"], w=[("ps", pbT)])
            P.act(lambda e, eb=eb, pT=pT: e.copy(out=OTs[eb][:, :], in_=pT[:, 0:512]), r=[("ps", pbT)], w=[("OTs", eb)])
            P.dma("sp", lambda e, eb=eb, hl=hl, t=t: e.dma_start(out=T.OT_in[t // 4, hl * 128:(hl + 1) * 128, (t % 4) * 512:(t % 4 + 1) * 512], in_=OTs[eb][:, :]),
                  r=[("OTs", eb)], w=[("OT_in", t // 4, eb)], key="k_ot%d_%d" % (t // 4, eb))
            if t % 4 == 3 and hl == 1:
                P.add("pool", lambda e, c=t // 4: e.collective_compute("AllGather", ALU.bypass, replica_groups=[[0, 1, 2, 3], [4, 5, 6, 7]],
                                                               ins=[T.OT_in[c]], outs=[T.OT_all[c]]),
                      reads=[("OT_in", t // 4, 0), ("OT_in", t // 4, 1)], writes=["OT_all"], kind="cc", key="k_cc")
    P.barrier()


def _layernorm(P, C, src, dst, g, b, st, mv, tag, srctok, dsttok, eng2="pool", eng3=None):
    for hf in range(2):
        P.dve(lambda e, hf=hf: e.bn_stats(out=st[:, hf * 6:(hf + 1) * 6], in_=src[:, hf * 512:(hf + 1) * 512]),
              r=[srctok] + ([(tag, "st")] if hf else []), w=[(tag, "st")])
    P.dve(lambda e: e.bn_aggr(out=mv[:, 0:2], in_=st[:, 0:12]), r=[(tag, "st")], w=[(tag, "mv")])
    P.dve(lambda e: e.tensor_scalar(out=mv[:, 2:3], in0=mv[:, 1:2], scalar1=LN_EPS, scalar2=None, op0=ALU.add), r=[(tag, "mv")], w=[(tag, "mv2")])
    P.act(lambda e: e.activation(out=mv[:, 3:4], in_=mv[:, 2:3], func=AF.Sqrt), r=[(tag, "mv2")], w=[(tag, "mv3")])
    P.dve(lambda e: e.reciprocal(out=mv[:, 4:5], in_=mv[:, 3:4]), r=[(tag, "mv3")], w=[(tag, "mv4")])
    P.dve(lambda e: e.tensor_scalar(out=dst[:, :], in0=src[:, :], scalar1=mv[:, 0:1], scalar2=mv[:, 4:5], op0=ALU.subtract, op1=ALU.mult),
          r=[srctok, (tag, "mv"), (tag, "mv4")], w=[dsttok])
    P.add(eng2, lambda e: e.tensor_tensor(out=dst[:, :], in0=dst[:, :], in1=g[:, :], op=ALU.mult), reads=[dsttok, "lnp"], writes=[dsttok])
    P.add(eng3 or eng2, lambda e: e.tensor_tensor(out=dst[:, :], in0=dst[:, :], in1=b[:, :], op=ALU.add), reads=[dsttok, "lnp"], writes=[dsttok])


def phase_c(nc, P, C, T, L):
    ps = C.ps
    ident = C.ident
    KB = 1024
    base = SB_BASE

    def at(off_kb, shape, dtype, name):
        C.cn += 1
        return nc.alloc_sbuf_tensor_at("%s_c%d" % (name, C.cn), list(shape), dtype, offset=base + int(off_kb * KB))

    res_src = T.x if L == 0 else T.h1
    out_dst = T.h1 if L == 0 else T.out
    w_o = T.w_o0 if L == 0 else T.w_o1
    lng = [T.ln_mix_g[L], T.ln_mix_b[L], T.ln_ffn_g[L], T.ln_ffn_b[L]]
    w_gu, w_dn = T.w_gu[L], T.w_dn[L]

    XmT = at(0, [128, 8, NT], BF16, "XmT")
    HT = at(32, [128, 22, NT], BF16, "HT")
    wd_b = at(120, [128, 22, 1024], BF16, "wd_b")
    ln2g = at(164, [128, 1024], F32, "ln2g")
    ln2b = at(168, [128, 1024], F32, "ln2b")
    small = at(172, [128, 64], F32, "small")
    OTm = at(32, [128, 8, NT], BF16, "OTm")
    wo_b = at(64, [128, 8, 1024], BF16, "wo_b")
    ln1g = at(80, [128, 1024], F32, "ln1g")
    ln1b = at(84, [128, 1024], F32, "ln1b")
    xres = [at(88 + 4 * i, [128, 1024], F32, "xres") for i in range(2)]
    y = [at(96 + 4 * i, [128, 1024], F32, "y") for i in range(2)]
    hm = [at(104 + 4 * i, [128, 1024], F32, "hm") for i in range(2)]
    hb = [at(112 + 2 * i, [128, 1024], BF16, "hb") for i in range(2)]
    C.stg = [at(120 + 16 * i, [128, 4096], F32, "stg") for i in range(2)]
    C.stg_elems, C.stg_i = 4096, 0
    st = [small[:, 0:12], small[:, 16:28]]
    mv = [small[:, 32:40], small[:, 40:48]]

    def dyn_load(e, h):
        r = P.pid % 4
        return e.dma_start(out=OTm[:, h, :], in_=T.OT_all[bass.ds(r, 1), h * 128:(h + 1) * 128, :].rearrange("o p c -> (o p) c"))

    for h in range(8):
        P.dma("sp", lambda e, h=h: dyn_load(e, h), r=["OT_all"], w=["OTm"], key="k_otm")
    P.dma("sp", lambda e: e.dma_start(out=ln1g[:, :], in_=lng[0][:, :]), r=[], w=["lnp"], key="k_c0")
    P.dma("sp", lambda e: e.dma_start(out=ln1b[:, :], in_=lng[1][:, :]), r=[], w=["lnp"], key="k_c0")
    load_weight(P, C, wo_b, w_o.ap().rearrange("(k p) f -> p k f", p=128), 8, 1024, "wo_b", "w_o", q="act")

    def pre1(i):
        b = i % 2
        P.dma("sp", lambda e, i=i, b=b: e.dma_start(out=xres[b][:, :], in_=res_src[i * 128:(i + 1) * 128, :]), r=["res_src"], w=[("xres", b)], key="k_xr%d" % b)
        for hf in range(2):
            pb = 2 * b + hf
            for h in range(8):
                P.pe(lambda e, h=h, hf=hf, pb=pb, i=i: e.matmul(ps[pb][:, :], lhsT=OTm[:, h, i * 128:(i + 1) * 128], rhs=wo_b[:, h, hf * 512:(hf + 1) * 512], start=(h == 0), stop=(h == 7)),
                     r=["OTm", "wo_b"], w=[("ps", pb)])

    def y1(i):
        b = i % 2
        for hf in range(2):
            pb = 2 * b + hf
            P.dve(lambda e, hf=hf, pb=pb, b=b: e.scalar_tensor_tensor(out=y[b][:, hf * 512:(hf + 1) * 512], in0=xres[b][:, hf * 512:(hf + 1) * 512], scalar=ALPHA, in1=ps[pb][:, :], op0=ALU.mult, op1=ALU.add),
                  r=[("xres", b), ("ps", pb)], w=[("y", b)])

    def post1(i):
        b = i % 2
        _layernorm(P, C, y[b], hm[b], ln1g, ln1b, st[b], mv[b], ("ln", b), ("y", b), ("hm", b), eng2="pool", eng3="dve")
        P.act(lambda e, b=b: e.copy(out=hb[b][:, :], in_=hm[b][:, :]), r=[("hm", b)], w=[("hb", b)])
        pT = ps[4 + b][:, :].bitcast(BF16)
        for j in range(8):
            P.pe(lambda e, j=j, b=b, pT=pT: e.transpose(pT[:, j * 128:(j + 1) * 128], hb[b][:, j * 128:(j + 1) * 128], ident[:, :]),
                 r=[("hb", b), "ident"], w=[("ps", 4 + b)])
        P.act(lambda e, i=i, b=b, pT=pT: e.copy(out=XmT[:, :, i * 128:(i + 1) * 128], in_=pT.rearrange("p (k t) -> p k t", t=128)),
              r=[("ps", 4 + b)], w=["XmT"])
        P.dma("sp", lambda e, i=i, b=b: e.dma_start(out=T.hmid[i * 128:(i + 1) * 128, :], in_=hm[b][:, :]), r=[("hm", b)], w=["hmid"], key="k_hm%d" % b)

    for i in range(16):
        pre1(i)
        if i >= 1:
            post1(i - 1)
        y1(i)
    post1(15)
    P.barrier()

    wgu = [at(172.5 + 4 * i, [128, 8, 256], BF16, "wgu") for i in range(2)]
    stgA = [at(180.5 + 8 * i, [128, 2048], F32, "stgA") for i in range(2)]
    stgB = [at(196.5 + 4 * i, [128, 1024], F32, "stgB") for i in range(2)]
    C.sg = [at(164 + 2 * i, [128, 512], F32, "sg") for i in range(2)]
    gsrc = w_gu.ap().rearrange("(k p) f -> p k f", p=128)
    wd_src = w_dn.ap().rearrange("(j p) f -> p j f", p=128)
    for j in range(22):
        bj = j % 2
        s = j % 2
        stv = stgA[s][:, :].rearrange("p (k f) -> p k f", f=256)
        P.dma("sp", lambda e, j=j, stv=stv: e.dma_start(out=stv[:, :, 0:128], in_=gsrc[:, :, j * 128:(j + 1) * 128]), r=["w_gu"], w=[("stgA", s)], key="k_stg%d" % s)
        P.dma("sp", lambda e, j=j, stv=stv: e.dma_start(out=stv[:, :, 128:256], in_=gsrc[:, :, DFF + j * 128:DFF + (j + 1) * 128]), r=["w_gu"], w=[("stgA", s)], key="k_stg%d" % s)
        P.pool(lambda e, bj=bj, stv=stv: e.tensor_copy(out=wgu[bj][:, :, :], in_=stv), r=[("stgA", s)], w=[("wgu", bj)])
        stw = stgB[s][:, :]
        P.dma("sp", lambda e, j=j, stw=stw: e.dma_start(out=stw, in_=wd_src[:, j, :]), r=["w_dn"], w=[("stgB", s)], key="k_stgb%d" % s)
        P.pool(lambda e, j=j, stw=stw: e.tensor_copy(out=wd_b[:, j, :], in_=stw), r=[("stgB", s)], w=["wd_b"])
        for tt in range(4):
            n = (j * 4 + tt) % 2
            pg, pu = 2 * n, 2 * n + 1
            for k in range(8):
                P.pe(lambda e, k=k, bj=bj, tt=tt, pg=pg: e.matmul(ps[pg][:, :], lhsT=wgu[bj][:, k, 0:128], rhs=XmT[:, k, tt * 512:(tt + 1) * 512], start=(k == 0), stop=(k == 7)),
                     r=[("wgu", bj), "XmT"], w=[("ps", pg)])
            for k in range(8):
                P.pe(lambda e, k=k, bj=bj, tt=tt, pu=pu: e.matmul(ps[pu][:, :], lhsT=wgu[bj][:, k, 128:256], rhs=XmT[:, k, tt * 512:(tt + 1) * 512], start=(k == 0), stop=(k == 7)),
                     r=[("wgu", bj), "XmT"], w=[("ps", pu)])
            P.act(lambda e, n=n, pg=pg: e.activation(out=C.sg[n][:, :], in_=ps[pg][:, :], func=AF.Silu), r=[("ps", pg)], w=[("sg", n)])
            P.dve(lambda e, n=n, pu=pu, j=j, tt=tt: e.tensor_tensor(out=HT[:, j, tt * 512:(tt + 1) * 512], in0=C.sg[n][:, :], in1=ps[pu][:, :], op=ALU.mult),
                  r=[("sg", n), ("ps", pu)], w=["HT"])
    P.barrier()

    xres2 = [at(0 + 4 * i, [128, 1024], F32, "xres2") for i in range(2)]
    y2 = [at(8 + 4 * i, [128, 1024], F32, "y2") for i in range(2)]
    o2 = [at(16 + 4 * i, [128, 1024], F32, "o2") for i in range(2)]
    hb2 = [at(24 + 2 * i, [128, 1024], BF16, "hb2") for i in range(2)]
    if L == 0:
        X1T = at(172.5, [128, 8, NT], BF16, "X1T")
    P.dma("sp", lambda e: e.dma_start(out=ln2g[:, :], in_=lng[2][:, :]), r=[], w=["lnp"], key="k_c0")
    P.dma("sp", lambda e: e.dma_start(out=ln2b[:, :], in_=lng[3][:, :]), r=[], w=["lnp"], key="k_c0")
    def pre3(i):
        b = i % 2
        P.dma("sp", lambda e, i=i, b=b: e.dma_start(out=xres2[b][:, :], in_=T.hmid[i * 128:(i + 1) * 128, :]), r=["hmid"], w=[("xres2", b)], key="k_xr%d" % b)
        for hf in range(2):
            pb = 2 * b + hf
            for j in range(22):
                P.pe(lambda e, j=j, hf=hf, pb=pb, i=i: e.matmul(ps[pb][:, :], lhsT=HT[:, j, i * 128:(i + 1) * 128], rhs=wd_b[:, j, hf * 512:(hf + 1) * 512], start=(j == 0), stop=(j == 21)),
                     r=["HT", "wd_b"], w=[("ps", pb)])

    def y3(i):
        b = i % 2
        for hf in range(2):
            pb = 2 * b + hf
            P.dve(lambda e, hf=hf, pb=pb, b=b: e.scalar_tensor_tensor(out=y2[b][:, hf * 512:(hf + 1) * 512], in0=xres2[b][:, hf * 512:(hf + 1) * 512], scalar=ALPHA, in1=ps[pb][:, :], op0=ALU.mult, op1=ALU.add),
                  r=[("xres2", b), ("ps", pb)], w=[("y2", b)])

    def post3(i):
        b = i % 2
        _layernorm(P, C, y2[b], o2[b], ln2g, ln2b, st[b], mv[b], ("ln", b), ("y2", b), ("o2", b), eng2=("dve" if L == 0 else "pool"))
        P.dma("sp", lambda e, i=i, b=b: e.dma_start(out=out_dst[i * 128:(i + 1) * 128, :], in_=o2[b][:, :]), r=[("o2", b)], w=["out_dst"], key="k_o2%d" % b)
        if L == 0:
            P.act(lambda e, b=b: e.copy(out=hb2[b][:, :], in_=o2[b][:, :]), r=[("o2", b)], w=[("hb2", b)])
            pT = ps[4 + b][:, :].bitcast(BF16)
            for j in range(8):
                P.pe(lambda e, j=j, b=b, pT=pT: e.transpose(pT[:, j * 128:(j + 1) * 128], hb2[b][:, j * 128:(j + 1) * 128], ident[:, :]),
                     r=[("hb2", b), "ident"], w=[("ps", 4 + b)])
            P.act(lambda e, i=i, b=b, pT=pT: e.copy(out=X1T[:, :, i * 128:(i + 1) * 128], in_=pT.rearrange("p (k t) -> p k t", t=128)),
                  r=[("ps", 4 + b)], w=["X1T"])
    def ship3(c):
        for k in range(8):
            P.dma("sp", lambda e, k=k, c=c: e.dma_start(out=T.x1T_in[c][k * 128:(k + 1) * 128, :], in_=X1T[:, k, c * 512:(c + 1) * 512]), r=["X1T"], w=[("x1T_in", c)], key="k_x1st%d" % c)
        P.add("pool", lambda e, c=c: e.collective_compute("AllGather", ALU.bypass, replica_groups=[[0, 1, 2, 3], [4, 5, 6, 7]],
                                                           ins=[T.x1T_in[c].ap()], outs=[T.x1T_all[c].ap()]),
              reads=[("x1T_in", c)], writes=["x1T_all"], kind="cc", key="k_cc")

    for i in range(16):
        pre3(i)
        if i >= 1:
            post3(i - 1)
            if L == 0 and (i - 1) % 4 == 3:
                ship3((i - 1) // 4)
        y3(i)
    post3(15)
    if L == 0:
        ship3(3)
    P.barrier()


NLAYERS = 2


def build(nlayers=NLAYERS, phases=None):
    nc = bass.Bass("TRN2", target_bir_lowering=False)
    specs = dict(x=([NT, DM], F32), w_in0=([DM, 704], F32), g_q=([128, 384], F32), g_kv=([128, 256], F32),
                 cs_tok=([128, 16, 64], F32), cc_f=([64, SEQ], F32), ss_f=([64, SEQ], F32), w_uq_c=([384, 512], F32),
                 w_ukv_c=([256, 512], F32), w_o0=([DM, DM], F32), w_o1=([DM, DM], F32), ident_in=([128, 128], BF16), identf_in=([128, 128], F32),
                 w1_c=([DM, 768], F32), lamv=([128, 4, 64], F32), g_sub=([128, 128], F32), F1=([128, 2, 128], F32), qaug=([2, 16, 2, 2, 512], BF16), kaug=([2, 2, SEQ], BF16), ones2=([2, SEQ], BF16))
    for l in range(2):
        for nm in ("ln_mix_g", "ln_mix_b", "ln_ffn_g", "ln_ffn_b"):
            specs["%s%d" % (nm, l)] = ([128, DM], F32)
        specs["w_gu%d" % l] = ([DM, 2 * DFF], F32)
        specs["w_dn%d" % l] = ([DFF, DM], F32)

    class Lazy:
        def __init__(self):
            self.__dict__["names"] = []

        def __getattr__(self, name):
            if name in specs:
                t = nc.dram_tensor(name, list(specs[name][0]), specs[name][1], kind="ExternalInput")
                self.__dict__[name] = t
                self.names.append(name)
                return t
            if name in ("ln_mix_g", "ln_mix_b", "ln_ffn_g", "ln_ffn_b", "w_gu", "w_dn"):
                outer = self

                class Idx:
                    def __getitem__(self, l):
                        return getattr(outer, "%s%d" % (name, l))
                return Idx()
            raise AttributeError(name)

    T = Lazy()
    T.out = nc.dram_tensor("out", [NT, DM], F32, kind="ExternalOutput")
    T.latT_in = [nc.dram_tensor("latT_in%d" % k, [704, 512], BF16) for k in range(4)]
    T.latT_all = [nc.dram_tensor("latT_all%d" % k, [4 * 704, 512], BF16) for k in range(4)]
    T.OT_in = nc.dram_tensor("OT_in", [4, 256, NT], BF16)
    T.OT_all = nc.dram_tensor("OT_all", [4, 1024, NT], BF16)
    T.hmid = nc.dram_tensor("hmid", [NT, DM], F32)
    if nlayers == 1:
        T.h1 = T.out
    else:
        T.h1 = nc.dram_tensor("h1", [NT, DM], F32)
    T.x1T_in = [nc.dram_tensor("x1T_in%d" % k, [DM, 512], BF16) for k in range(4)]
    T.x1T_all = [nc.dram_tensor("x1T_all%d" % k, [4 * DM, 512], BF16) for k in range(4)]

    C = Ctx()
    C.cn = 0
    C.ps = [nc.alloc_psum_tensor("ps%d" % i, [128, 512], F32) for i in range(8)]
    C.ident = nc.alloc_sbuf_tensor_at("ident", [128, 128], BF16, offset=SB_TOP + 768)
    C.ident_f = nc.alloc_sbuf_tensor_at("ident_f", [128, 128], F32, offset=SB_TOP + 256)
    C.ones_f = nc.alloc_sbuf_tensor_at("ones_f", [128, 16], F32, offset=SB_TOP + 128)
    P = Prog(nc)
    _ = (T.ident_in, T.x, T.identf_in)
    P.dma("sp", lambda e: e.dma_start(out=C.ident[:, :], in_=T.ident_in[:, :]), r=[], w=["ident"], key="k_id")
    P.dma("sp", lambda e: e.dma_start(out=C.ident_f[:, :], in_=T.identf_in[:, :]), r=[], w=["ident"], key="k_id")
    P.pool(lambda e: e.memset(C.ones_f[:, :], 1.0), r=[], w=["ones_f"])
    if phases is None:
        phases = ["a0", "b0", "c0"] + (["b1", "c1"] if nlayers == 2 else [])
    if "a0" in phases:
        phase_a0(nc, P, C, T)
    if "b0" in phases:
        phase_b0(nc, P, C, T)
    if "c0" in phases:
        phase_c(nc, P, C, T, 0)
    if "b1" in phases:
        phase_b1(nc, P, C, T)
    if "c1" in phases:
        phase_c(nc, P, C, T, 1)
    if ("c1" if nlayers == 2 else "c0") not in phases:
        P.dma("sp", lambda e: e.dma_start(out=T.out[0:128, :], in_=T.x[0:128, :]), r=[], w=["out_dst"], key="k_c0")
    n = P.emit()
    nc.input_names = T.names
    return nc, n


def _host_inputs(inputs):
    f32 = np.float32
    x = np.ascontiguousarray(inputs["x"], dtype=f32).reshape(BATCH * SEQ, DM)
    rep = lambda v, n=128: np.ascontiguousarray(np.broadcast_to(np.asarray(v, dtype=f32).reshape(1, -1), (n, np.asarray(v).size)))
    inv_freq = (np.float32(10000.0) ** (-np.arange(0, 64, 2, dtype=f32) / np.float32(64))).astype(f32)
    ang = (np.arange(SEQ, dtype=f32)[:, None] * inv_freq[None, :]).astype(f32)
    cos, sin = np.cos(ang).astype(f32), np.sin(ang).astype(f32)
    cc_f = np.ascontiguousarray(np.concatenate([cos.T, cos.T], 0))
    ss_f = np.ascontiguousarray(np.concatenate([-sin.T, sin.T], 0))
    w_uq = np.asarray(inputs["mla_w_uq"][0], dtype=f32)
    w_ukv = np.asarray(inputs["mla_w_ukv"][0], dtype=f32)
    w_in1 = np.asarray(inputs["diff_w_in"][0], dtype=f32)
    ident = np.eye(128, dtype=f32).astype(ml_dtypes.bfloat16)
    lamv = np.stack([rep(inputs["diff_lam_q1"][0]), rep(inputs["diff_lam_k1"][0]), rep(inputs["diff_lam_q2"][0]), rep(inputs["diff_lam_k2"][0])], 1)
    common = dict(
        w_in0=np.ascontiguousarray(inputs["mla_w_in"][0], dtype=f32),
        g_q=rep(inputs["mla_g_q"][0]), g_kv=rep(inputs["mla_g_kv"][0]),
        cc_f=cc_f, ss_f=ss_f,
        w_o0=np.ascontiguousarray(inputs["mla_w_o"][0], dtype=f32),
        w_o1=np.ascontiguousarray(inputs["diff_w_o"][0], dtype=f32),
        ident_in=ident, identf_in=np.eye(128, dtype=f32), lamv=np.ascontiguousarray(lamv), g_sub=rep(inputs["diff_g_sub"][0]),
    )
    for l in range(2):
        common["ln_mix_g%d" % l] = rep(inputs["ln_mix_g"][l])
        common["ln_mix_b%d" % l] = rep(inputs["ln_mix_b"][l])
        common["ln_ffn_g%d" % l] = rep(inputs["ln_ffn_g"][l])
        common["ln_ffn_b%d" % l] = rep(inputs["ln_ffn_b"][l])
        common["w_gu%d" % l] = np.ascontiguousarray(inputs["ffn_w_gu"][l], dtype=f32)
        common["w_dn%d" % l] = np.ascontiguousarray(inputs["ffn_w_down"][l], dtype=f32)
    kk = np.arange(128, dtype=np.float64)
    maps = []
    for c in range(NCORES):
        g = c % 4
        m = dict(common)
        m["x"] = np.ascontiguousarray(x[c * NT:(c + 1) * NT])
        pos = g * NT + np.arange(NT)
        cs = np.concatenate([cos[pos], sin[pos]], 1)
        m["cs_tok"] = np.ascontiguousarray(cs.reshape(16, 128, 64).transpose(1, 0, 2))
        cols_q, cols_kv_k, cols_kv_v, cq1, ck1, cv1 = [], [], [], [], [], []
        for hl in range(2):
            h = 2 * g + hl
            b0 = h * 192
            cols_q += list(range(b0, b0 + 128)) + list(range(b0 + 128, b0 + 192)) + list(range(b0 + 160, b0 + 192)) + list(range(b0 + 128, b0 + 160))
            cols_kv_k += list(range(h * 256, h * 256 + 128))
            cols_kv_v += list(range(h * 256 + 128, h * 256 + 256))
            cq1 += list(range(h * 128, (h + 1) * 128))
            ck1 += list(range(1024 + h * 128, 1024 + (h + 1) * 128))
            cv1 += list(range(2048 + h * 128, 2048 + (h + 1) * 128))
        m["w_uq_c"] = np.ascontiguousarray(w_uq[:, cols_q])
        m["w_ukv_c"] = np.ascontiguousarray(w_ukv[:, cols_kv_k + cols_kv_v])
        m["w1_c"] = np.ascontiguousarray(w_in1[:, cq1 + ck1 + cv1])
        F1 = np.zeros((128, 2, 128), f32)
        qaug = np.zeros((2, 16, 2, 2, 512), f32)
        kaug = np.zeros((2, 2, SEQ), f32)
        qq = np.arange(512)
        kp = np.arange(SEQ)
        for hl in range(2):
            slope = 2.0 ** (-(2 * g + hl + 1))
            K_, Q_ = np.meshgrid(np.arange(128), np.arange(128), indexing="ij")
            same = (K_ // 64) == (Q_ // 64)
            Fm = np.where(K_ // 64 > Q_ // 64, 0.0, np.where(same & (K_ > Q_), np.exp(-2.0 * slope * (K_ - Q_)), 1.0))
            F1[:, hl, :] = Fm
            kaug[hl, 0] = 8.0 * slope * (kp % 128)
            kaug[hl, 1] = 8.0 * slope * 128.0 * (kp // 128)
            for t in range(16):
                qaug[0, t, :, hl, :] = -8.0 * slope * (qq % 256)
                qaug[1, t, :, hl, :] = -8.0 * slope * 256.0 * (qq // 256 + 2 * t)
        m["F1"], m["qaug"], m["kaug"] = F1, qaug.astype(ml_dtypes.bfloat16), kaug.astype(ml_dtypes.bfloat16)
        m["ones2"] = np.ones((2, SEQ), f32).astype(ml_dtypes.bfloat16)
        maps.append(m)
    return maps


_CACHE = {}


def kernel(**inputs):
    maps = _host_inputs(inputs)
    if "nc" not in _CACHE:
        _CACHE["nc"] = build(NLAYERS)[0]
    nc = _CACHE["nc"]
    res = run_bass_kernel_spmd(nc, maps, core_ids=list(range(NCORES)))
    out = np.concatenate([np.asarray(res.results[c]["out"], dtype=np.float32) for c in range(NCORES)], 0)
    return out.reshape(BATCH, SEQ, DM)
```
